# Optimizing a Trainium2 kernel written in Bass

```python
import math
import jax, jax.numpy as jnp
from jax import lax
import numpy as np


D_MODEL = 1024
BATCH = 2
SEQ = 8192
DEPTH = 4

N_MIXERS = 4
PLE_DIM = 256
HEAD_DIM = 64
N_HEADS = D_MODEL // HEAD_DIM
ROPE_THETA = 10000.0
NORM_EPS = 1e-6
Q_BLOCK = 128
LRU_WIDTH = D_MODEL
LRU_BLOCKS = N_HEADS
LRU_BLOCK_DIM = LRU_WIDTH // LRU_BLOCKS
CONV_WIDTH = 4
RGLRU_C = 8.0
DIL_PATTERNS = ((128, 1), (512, 4), (2048, 16))
DIFF_HEADS = D_MODEL // (2 * HEAD_DIM)
DIFF_SUBLN_EPS = 1e-5
FFN_HIDDEN = -(-(8 * D_MODEL) // (3 * 256)) * 256
N_A = (DEPTH + 3) // 4
N_B = (DEPTH + 2) // 4
N_C = (DEPTH + 1) // 4
N_D = DEPTH // 4

kernel_name = 'hybrid_interleaved_rglru_dilated_stickbreak_diffattn'


def rmsnorm(x, g, eps=NORM_EPS):
    xf = x.astype(jnp.float32)
    xf = xf * lax.rsqrt(jnp.mean(xf * xf, axis=-1, keepdims=True) + eps)
    return (xf * g.astype(jnp.float32)).astype(x.dtype)


def rope(t, positions):
    hd = t.shape[-1]
    inv = ROPE_THETA ** (-jnp.arange(0, hd, 2, dtype=jnp.float32) / hd)
    ang = positions.astype(jnp.float32)[..., None] * inv
    cos = jnp.cos(ang)[:, :, None, :]
    sin = jnp.sin(ang)[:, :, None, :]
    tf = t.astype(jnp.float32)
    t1, t2 = tf[..., :hd // 2], tf[..., hd // 2:]
    return jnp.concatenate([t1 * cos - t2 * sin, t2 * cos + t1 * sin], axis=-1).astype(t.dtype)


def to_blocks(t):
    B, S, H, e = t.shape
    return t.reshape(B, S // Q_BLOCK, Q_BLOCK, H, e).transpose(1, 0, 2, 3, 4)


def from_blocks(t):
    nb, B, blk, H, e = t.shape
    return t.transpose(1, 0, 2, 3, 4).reshape(B, nb * blk, H, e)


def _lin_rec_combine(c1, c2):
    a1, b1 = c1
    a2, b2 = c2
    return a1 * a2, a2 * b1 + b2


def rglru_block(h, w_in, conv_w, conv_b, gate_r_w, gate_r_b, gate_i_w, gate_i_b, a_param, w_out):
    B, S, _ = h.shape
    u = h @ w_in
    y = jax.nn.gelu(u[..., :LRU_WIDTH], approximate=True)
    xb = u[..., LRU_WIDTH:]
    xp = jnp.pad(xb, ((0, 0), (CONV_WIDTH - 1, 0), (0, 0)))
    xc = conv_b
    for tap in range(CONV_WIDTH):
        xc = xc + xp[:, tap:tap + S] * conv_w[tap]
    xh = xc.reshape(B, S, LRU_BLOCKS, LRU_BLOCK_DIM)
    r = jax.nn.sigmoid(jnp.einsum('bsnd,nde->bsne', xh, gate_r_w).reshape(B, S, LRU_WIDTH) + gate_r_b)
    ig = jax.nn.sigmoid(jnp.einsum('bsnd,nde->bsne', xh, gate_i_w).reshape(B, S, LRU_WIDTH) + gate_i_b)
    log_a = -RGLRU_C * r.astype(jnp.float32) * jax.nn.softplus(-a_param.astype(jnp.float32))
    a = jnp.exp(log_a)
    mult = jnp.sqrt(-jnp.expm1(2.0 * log_a))
    b = mult * (ig * xc).astype(jnp.float32)
    _, hs = lax.associative_scan(_lin_rec_combine, (a, b), axis=1)
    return (hs.astype(h.dtype) * y) @ w_out


def dilated_branch(q, k, v, window, dilation):
    B, S, H, hd = q.shape
    blk = window // dilation
    span = blk * dilation
    Sp = -(-S // span) * span
    pad = Sp - S
    nb = Sp // span

    def blocks(t):
        return jnp.pad(t, ((0, 0), (0, pad), (0, 0), (0, 0))).reshape(B, nb, blk, dilation, H, t.shape[-1])

    qb, kb, vb = blocks(q), blocks(k), blocks(v)
    kk = jnp.concatenate([jnp.concatenate([jnp.zeros_like(kb[:, :1]), kb[:, :-1]], axis=1), kb], axis=2)
    vv = jnp.concatenate([jnp.concatenate([jnp.zeros_like(vb[:, :1]), vb[:, :-1]], axis=1), vb], axis=2)
    i = np.arange(blk)[:, None]
    j = np.arange(2 * blk)[None, :]
    band = (j >= i) & (j <= i + blk)
    mask = band[None] & ((np.arange(nb)[:, None, None] > 0) | (j >= blk)[None])
    mask = jnp.asarray(mask)[None, :, :, None, None, :]
    s = jnp.einsum('bnqrhd,bnkrhd->bnqrhk', qb, kk).astype(jnp.float32)
    s = jnp.where(mask, s, -jnp.inf)
    m = jnp.max(s, axis=-1, keepdims=True)
    pe = jnp.exp(s - m)
    l = jnp.sum(pe, axis=-1, keepdims=True)
    o = jnp.einsum('bnqrhk,bnkrhd->bnqrhd', pe, vv.astype(jnp.float32))

    def unblock(t):
        return t.reshape(B, Sp, H, t.shape[-1])[:, :S]

    return unblock(o), unblock(m), unblock(l)


def dilated_attention(h, positions, w_qkv, w_out):
    B, S, _ = h.shape
    u = (h @ w_qkv).reshape(B, S, 3, N_HEADS, HEAD_DIM)
    q = rope(u[:, :, 0], positions) * (HEAD_DIM ** -0.5)
    k = rope(u[:, :, 1], positions)
    v = u[:, :, 2]
    outs = [dilated_branch(q, k, v, w, d) for (w, d) in DIL_PATTERNS]
    os_ = jnp.stack([o for (o, _, _) in outs])
    ms = jnp.stack([m for (_, m, _) in outs])
    ls = jnp.stack([l for (_, _, l) in outs])
    wgt = jnp.exp(ms - jnp.max(ms, axis=0, keepdims=True))
    out = jnp.sum(os_ * wgt, axis=0) / jnp.sum(ls * wgt, axis=0)
    return out.astype(h.dtype).reshape(B, S, D_MODEL) @ w_out


def stick_breaking_attention(h, w_qkv, w_out):
    B, S, _ = h.shape
    u = (h @ w_qkv).reshape(B, S, 3, N_HEADS, HEAD_DIM)
    q = u[:, :, 0] * (HEAD_DIM ** -0.5)
    k, v = u[:, :, 1], u[:, :, 2]
    kpos = jnp.arange(S)

    def one_block(args):
        qblk, start = args
        z = jnp.einsum('bqhd,bkhd->bhqk', qblk, k).astype(jnp.float32)
        qpos = start + jnp.arange(Q_BLOCK)
        mask = kpos[None, :] < qpos[:, None]
        lneg = jnp.where(mask, jax.nn.log_sigmoid(-z), 0.0)
        rest = lax.cumsum(lneg, axis=3, reverse=True) - lneg
        att = jnp.where(mask, jnp.exp(jax.nn.log_sigmoid(z) + rest), 0.0)
        return jnp.einsum('bhqk,bkhd->bqhd', att.astype(v.dtype), v)

    starts = jnp.arange(S // Q_BLOCK) * Q_BLOCK
    o = from_blocks(lax.map(one_block, (to_blocks(q), starts)))
    return o.reshape(B, S, D_MODEL) @ w_out


def differential_attention(h, positions, w_qkv, lq1, lk1, lq2, lk2, subln, w_out, layer_idx):
    B, S, _ = h.shape
    u = h @ w_qkv
    q = rope(u[..., :D_MODEL].reshape(B, S, 2 * DIFF_HEADS, HEAD_DIM), positions) * (HEAD_DIM ** -0.5)
    k = rope(u[..., D_MODEL:2 * D_MODEL].reshape(B, S, 2 * DIFF_HEADS, HEAD_DIM), positions)
    v = u[..., 2 * D_MODEL:].reshape(B, S, DIFF_HEADS, 2 * HEAD_DIM)
    lam_init = 0.8 - 0.6 * math.exp(-0.3 * layer_idx)
    lam = (jnp.exp(jnp.sum(lq1.astype(jnp.float32) * lk1.astype(jnp.float32)))
           - jnp.exp(jnp.sum(lq2.astype(jnp.float32) * lk2.astype(jnp.float32))) + lam_init)
    kpos = jnp.arange(S)

    def one_block(args):
        qblk, start = args
        s = jnp.einsum('bqhd,bkhd->bhqk', qblk, k).astype(jnp.float32)
        qpos = start + jnp.arange(Q_BLOCK)
        mask = kpos[None, :] <= qpos[:, None]
        a = jax.nn.softmax(jnp.where(mask, s, -jnp.inf), axis=-1)
        a = a.reshape(B, DIFF_HEADS, 2, Q_BLOCK, S)
        wmap = a[:, :, 0] - lam * a[:, :, 1]
        return jnp.einsum('bhqk,bkhe->bqhe', wmap.astype(v.dtype), v)

    starts = jnp.arange(S // Q_BLOCK) * Q_BLOCK
    o = from_blocks(lax.map(one_block, (to_blocks(q), starts)))
    o = rmsnorm(o, subln, eps=DIFF_SUBLN_EPS) * (1.0 - lam_init)
    return o.reshape(B, S, D_MODEL) @ w_out


def swiglu(h, w_in, w_out):
    u = h @ w_in
    return (jax.nn.silu(u[..., :FFN_HIDDEN]) * u[..., FFN_HIDDEN:]) @ w_out


def setup_inputs(seed: int = 0) -> dict:
    key = jax.random.key(seed)
    ks = iter(jax.random.split(key, 48))

    def dense(shape, fan_in):
        return jax.random.normal(next(ks), shape, jnp.float32) * (fan_in ** -0.5)

    def gain(shape):
        return 1.0 + 0.05 * jax.random.normal(next(ks), shape, jnp.float32)

    def bias(shape):
        return 0.01 * jax.random.normal(next(ks), shape, jnp.float32)

    x = jax.random.normal(next(ks), (BATCH, SEQ, D_MODEL), jnp.float32)
    p = jax.random.normal(next(ks), (DEPTH, BATCH, SEQ, PLE_DIM), jnp.float32)
    offsets = jax.random.randint(next(ks), (BATCH, 1), 0, 4096, dtype=jnp.int32)
    positions = (offsets + jnp.arange(SEQ, dtype=jnp.int32)[None, :]).astype(jnp.int32)

    ln_mix_pre = gain((DEPTH, D_MODEL))
    ln_mix_post = gain((DEPTH, D_MODEL))
    ln_ffn_pre = gain((DEPTH, D_MODEL))
    ln_ffn_post = gain((DEPTH, D_MODEL))
    ln_ple = gain((DEPTH, D_MODEL))
    w_ffn_in = dense((DEPTH, D_MODEL, 2 * FFN_HIDDEN), D_MODEL)
    w_ffn_out = dense((DEPTH, FFN_HIDDEN, D_MODEL), FFN_HIDDEN)
    w_ple_gate = dense((DEPTH, D_MODEL, D_MODEL), D_MODEL)
    b_ple_gate = bias((DEPTH, D_MODEL))
    w_ple_proj = dense((DEPTH, PLE_DIM, D_MODEL), PLE_DIM)

    a_w_in = dense((N_A, D_MODEL, 2 * LRU_WIDTH), D_MODEL)
    a_conv_w = dense((N_A, CONV_WIDTH, LRU_WIDTH), CONV_WIDTH)
    a_conv_b = bias((N_A, LRU_WIDTH))
    a_gate_r_w = dense((N_A, LRU_BLOCKS, LRU_BLOCK_DIM, LRU_BLOCK_DIM), LRU_BLOCK_DIM)
    a_gate_r_b = bias((N_A, LRU_WIDTH))
    a_gate_i_w = dense((N_A, LRU_BLOCKS, LRU_BLOCK_DIM, LRU_BLOCK_DIM), LRU_BLOCK_DIM)
    a_gate_i_b = bias((N_A, LRU_WIDTH))
    a_pow = jax.random.uniform(next(ks), (N_A, LRU_WIDTH), jnp.float32, minval=0.9, maxval=0.999)
    a_base = a_pow ** (1.0 / RGLRU_C)
    a_lambda = jnp.log(a_base) - jnp.log1p(-a_base)
    a_w_out = dense((N_A, LRU_WIDTH, D_MODEL), LRU_WIDTH)

    b_w_qkv = dense((N_B, D_MODEL, 3 * D_MODEL), D_MODEL)
    b_w_out = dense((N_B, D_MODEL, D_MODEL), D_MODEL)

    c_w_qkv = dense((N_C, D_MODEL, 3 * D_MODEL), D_MODEL)
    c_w_out = dense((N_C, D_MODEL, D_MODEL), D_MODEL)

    d_w_qkv = dense((N_D, D_MODEL, 3 * D_MODEL), D_MODEL)
    d_lambda_q1 = 0.1 * jax.random.normal(next(ks), (N_D, HEAD_DIM), jnp.float32)
    d_lambda_k1 = 0.1 * jax.random.normal(next(ks), (N_D, HEAD_DIM), jnp.float32)
    d_lambda_q2 = 0.1 * jax.random.normal(next(ks), (N_D, HEAD_DIM), jnp.float32)
    d_lambda_k2 = 0.1 * jax.random.normal(next(ks), (N_D, HEAD_DIM), jnp.float32)
    d_subln = gain((N_D, 2 * HEAD_DIM))
    d_w_out = dense((N_D, D_MODEL, D_MODEL), D_MODEL)

    return {'x': x, 'p': p, 'positions': positions,
            'ln_mix_pre': ln_mix_pre, 'ln_mix_post': ln_mix_post,
            'ln_ffn_pre': ln_ffn_pre, 'ln_ffn_post': ln_ffn_post, 'ln_ple': ln_ple,
            'w_ffn_in': w_ffn_in, 'w_ffn_out': w_ffn_out,
            'w_ple_gate': w_ple_gate, 'b_ple_gate': b_ple_gate, 'w_ple_proj': w_ple_proj,
            'a_w_in': a_w_in, 'a_conv_w': a_conv_w, 'a_conv_b': a_conv_b,
            'a_gate_r_w': a_gate_r_w, 'a_gate_r_b': a_gate_r_b,
            'a_gate_i_w': a_gate_i_w, 'a_gate_i_b': a_gate_i_b,
            'a_lambda': a_lambda, 'a_w_out': a_w_out,
            'b_w_qkv': b_w_qkv, 'b_w_out': b_w_out,
            'c_w_qkv': c_w_qkv, 'c_w_out': c_w_out,
            'd_w_qkv': d_w_qkv, 'd_lambda_q1': d_lambda_q1, 'd_lambda_k1': d_lambda_k1,
            'd_lambda_q2': d_lambda_q2, 'd_lambda_k2': d_lambda_k2,
            'd_subln': d_subln, 'd_w_out': d_w_out}


def reference(x, p, positions, ln_mix_pre, ln_mix_post, ln_ffn_pre, ln_ffn_post, ln_ple,
              w_ffn_in, w_ffn_out, w_ple_gate, b_ple_gate, w_ple_proj,
              a_w_in, a_conv_w, a_conv_b, a_gate_r_w, a_gate_r_b, a_gate_i_w, a_gate_i_b,
              a_lambda, a_w_out, b_w_qkv, b_w_out, c_w_qkv, c_w_out,
              d_w_qkv, d_lambda_q1, d_lambda_k1, d_lambda_q2, d_lambda_k2, d_subln, d_w_out):
    for i in range(DEPTH):
        kind = i % N_MIXERS
        j = i // N_MIXERS
        h = rmsnorm(x, ln_mix_pre[i])
        if kind == 0:
            y = rglru_block(h, a_w_in[j], a_conv_w[j], a_conv_b[j], a_gate_r_w[j], a_gate_r_b[j],
                            a_gate_i_w[j], a_gate_i_b[j], a_lambda[j], a_w_out[j])
        elif kind == 1:
            y = dilated_attention(h, positions, b_w_qkv[j], b_w_out[j])
        elif kind == 2:
            y = stick_breaking_attention(h, c_w_qkv[j], c_w_out[j])
        else:
            y = differential_attention(h, positions, d_w_qkv[j], d_lambda_q1[j], d_lambda_k1[j],
                                       d_lambda_q2[j], d_lambda_k2[j], d_subln[j], d_w_out[j], i)
        x = x + rmsnorm(y, ln_mix_post[i])
        h = rmsnorm(x, ln_ffn_pre[i])
        x = x + rmsnorm(swiglu(h, w_ffn_in[i], w_ffn_out[i]), ln_ffn_post[i])
        gate = jax.nn.sigmoid(x @ w_ple_gate[i] + b_ple_gate[i])
        x = x + rmsnorm(gate * (p[i] @ w_ple_proj[i]), ln_ple[i])
    return x
```

```python
import math
import numpy as np
import ml_dtypes
import concourse.bass as bass
import concourse.mybir as mybir
from concourse.bass_utils import run_bass_kernel_spmd

F32, BF16, I32 = mybir.dt.float32, mybir.dt.bfloat16, mybir.dt.int32
AF = mybir.ActivationFunctionType
ALU = mybir.AluOpType
NPBF = ml_dtypes.bfloat16

D = 1024; B = 2; S = 8192; DEPTH = 4; FF = 2816; PLE = 256; HD = 64; NH = 16
NCORE = 8; TOK = 2048
SG = 1024; TG = 512
EPS = 1e-6


class Buf:
    __slots__ = ("ap", "w", "r", "dsem", "dcnt", "excl")

    def __init__(self, ap, excl=False):
        self.ap = ap; self.w = None; self.r = {}; self.dsem = None; self.dcnt = 0; self.excl = excl

    def __getitem__(self, k):
        return self.ap[k]


class Ctx:
    NDSEM = 40

    def __init__(self):
        self.nc = bass.Bass("TRN2", target_bir_lowering=False)
        nc = self.nc
        self.E = {"pe": nc.tensor, "act": nc.scalar, "dve": nc.vector, "pool": nc.gpsimd, "sp": nc.sync}
        self.sem = {k: nc.semaphore("s_" + k).__enter__() for k in self.E}
        self.cnt = {k: 0 for k in self.E}
        self.waited = {}
        self.nbuf = 0
        self.out_tickets = []
        self.dpool = [[nc.semaphore(f"dq{i}").__enter__(), 0] for i in range(self.NDSEM)]
        self.dpool_i = 0
        self.cc_sem = nc.semaphore("cc").__enter__(); self.cc_cnt = 0
        self.live = []

    def sb(self, shape, dt, name=None):
        self.nbuf += 1
        cm = self.nc.sbuf_tensor(f"{name or 'sb'}_{self.nbuf}", list(shape), dt)
        self.live.append(cm)
        return Buf(cm.__enter__())

    def ps(self, shape=(128, 512), dt=F32, name=None):
        self.nbuf += 1
        cm = self.nc.psum_tensor(f"{name or 'ps'}_{self.nbuf}", list(shape), dt)
        self.live.append(cm)
        return Buf(cm.__enter__(), excl=True)

    def dram(self, name, shape, dt, kind):
        return self.nc.dram_tensor(name, list(shape), dt, kind=kind).ap()

    def scratch(self, name, shape, dt):
        return self.nc.dram_tensor(name, list(shape), dt).ap()

    def _wait(self, eng, tickets):
        for (sem, val, src) in tickets:
            if src == eng and eng == "pe":
                continue
            key = (eng, id(sem))
            if self.waited.get(key, 0) < val:
                self.E[eng].wait_ge(sem, val)
                self.waited[key] = val

    def _deps(self, reads, writes):
        tk = []
        for b in reads:
            if b.w is not None:
                tk.append(b.w)
            if b.excl:
                tk.extend(b.r.values())
        for b in writes:
            if b.w is not None:
                tk.append(b.w)
            tk.extend(b.r.values())
        return tk

    def _commit(self, t, reads, writes):
        for b in reads:
            b.r[(id(t[0]), t[2])] = t
        for b in writes:
            b.w = t; b.r = {}

    def op(self, eng, fn, reads=(), writes=()):
        self._wait(eng, self._deps(reads, writes))
        ins = fn()
        self.cnt[eng] += 1
        ins.then_inc(self.sem[eng], 1)
        t = (self.sem[eng], self.cnt[eng], eng)
        self._commit(t, reads, writes)
        return t

    def dma(self, q, out, in_, reads=(), writes=(), track=None):
        self._wait(q, self._deps(reads, writes))
        b = track
        if b.dsem is None:
            assert self.dpool_i < self.NDSEM, "out of dma semaphores in this phase"
            b.dsem = self.dpool[self.dpool_i]; self.dpool_i += 1
        ins = self.E[q].dma_start(out=out, in_=in_)
        b.dsem[1] += 16
        ins.then_inc(b.dsem[0], 16)
        t = (b.dsem[0], b.dsem[1], "dma")
        self._commit(t, reads, writes)
        return t

    def gather(self, out, src2d, idx_col, out_buf, read_bufs):
        self._wait("pool", self._deps(read_bufs, (out_buf,)))
        b = out_buf
        if b.dsem is None:
            assert self.dpool_i < self.NDSEM, "out of dma semaphores in this phase"
            b.dsem = self.dpool[self.dpool_i]; self.dpool_i += 1
        ins = self.nc.gpsimd.indirect_dma_start(out=out, out_offset=None, in_=src2d,
                                                in_offset=bass.IndirectOffsetOnAxis(ap=idx_col, axis=0))
        b.dsem[1] += 16
        ins.then_inc(b.dsem[0], 16)
        t = (b.dsem[0], b.dsem[1], "dma")
        self._commit(t, read_bufs, (out_buf,))
        return t

    def all_gather(self, in_ap, out_ap, out_buf):
        self._wait("pool", self._deps((), (out_buf,)))
        ins = self.nc.gpsimd.collective_compute("AllGather", ALU.bypass, replica_groups=[[0, 1, 2, 3], [4, 5, 6, 7]],
                                                ins=[in_ap.opt()], outs=[out_ap.opt()])
        self.cc_cnt += 1
        ins.then_inc(self.cc_sem)
        t = (self.cc_sem, self.cc_cnt, "cc")
        self._commit(t, (), (out_buf,))
        return t

    def barrier(self):
        tk = [(self.sem[e], self.cnt[e], e) for e in self.E if self.cnt[e] > 0]
        tk += [(d[0], d[1], "dma") for d in self.dpool if d[1] > 0]
        if self.cc_cnt:
            tk.append((self.cc_sem, self.cc_cnt, "cc"))
        for e in self.E:
            self._wait(e, tk)

    def end_phase(self):
        self.barrier()
        for cm in reversed(self.live):
            cm.__exit__(None, None, None)
        self.live = []
        self.dpool_i = 0

    def finish(self, tickets):
        self._wait("sp", tickets)


def run_pipeline(items, skews):
    n = len(items)
    for step in range(n + max(skews)):
        for k, sk in enumerate(skews):
            idx = step - sk
            if 0 <= idx < n:
                items[idx][k]()


class Rot:
    def __init__(self, bufs):
        self.bufs = bufs; self.i = 0

    def next(self):
        b = self.bufs[self.i % len(self.bufs)]; self.i += 1
        return b


def w_chunks(W, cols=None):
    if cols is not None:
        W = W[:, cols]
    Din, Dout = W.shape
    K, J = Din // 128, Dout // 128
    return np.ascontiguousarray(W.reshape(K, 128, J, 128).transpose(2, 1, 0, 3))


def col_vec(v):
    return np.ascontiguousarray(v.reshape(-1, 128).T)


def fm(x2d):
    return np.ascontiguousarray(x2d.T)


class Dense:
    def __init__(self, c, post, pre, io):
        self.c = c
        self.post = post; self.pre = pre
        self.io = io
        self.piece_tix = {}
        self.acc_pending = []
        self.ag_queue = []
        self.x_in = io["x_in"]; self.x_out = io["x_out"]; self.ones_d = io["ones_bf"]
        if post:
            for k in ("w_mo", "w_f1", "w_f2", "w_pg", "w_pp", "p_in", "gains_post"):
                setattr(self, k, io[k])
        if pre:
            self.gain_pre = io["gain_pre"]; self.w_pre = io["w_pre"]
            self.npre = {"rglru": 16, "rope": 40, "plain": 24}[pre]
            self.pre_dt = F32 if pre == "rglru" else BF16
            self.pos = io["pos"]; self.invf = io["invf"]
            self.pr_out = io["pr_loc"]

    def build(self):
        c = self.c; nc = c.nc
        self.xT = c.sb((128, 8, SG), F32, "xT_sb")
        self.aT = c.sb((128, 8, SG), BF16, "aT_sb")
        self.yT = c.sb((128, 8, SG), F32, "yT_sb")
        self.ones = c.sb((128, 128), BF16, "ones_sb")
        self.wrot = Rot([c.sb((128, 22 * 128), BF16, f"w{i}") for i in range(4)])
        self.prot = Rot([c.ps(name=f"pb{i}") for i in range(6)])
        self.ssb = [c.ps(name=f"ss{i}") for i in range(2)]
        self.sqrot = Rot([c.sb((128, TG), BF16, f"sq{i}") for i in range(6)])
        self.rstd = [c.sb((128, TG), F32, f"rstd{i}") for i in range(2)]
        self.tmprot = Rot([c.sb((128, TG), F32, f"tmp{i}") for i in range(4)])
        self.strot = Rot([c.sb((128, TG), self.pre_dt if self.pre else F32, f"st{i}") for i in range(4)])
        c.dma("sp", self.ones[:], self.ones_d[:], writes=[self.ones], track=self.ones)
        if self.post:
            self.oidx = c.sb((128, 16), mybir.dt.uint32, "oidx")
            c.dma("sp", self.oidx[:], self.io["idx_o"], writes=[self.oidx], track=self.oidx)
            self.oB = c.sb((128, 8, SG), BF16, "oB_sb")
            self.gT = c.sb((128, 22, SG), BF16, "gT_sb")
            self.pT = c.sb((128, 2, SG), BF16, "pT_sb")
            self.gp = c.sb((128, 5, 8), F32, "gp_sb")
            c.dma("sp", self.gp[:], self.gains_post[:], writes=[self.gp], track=self.gp)
        if self.pre:
            self.gpre = c.sb((128, 8), F32, "gpre_sb")
            c.dma("sp", self.gpre[:], self.gain_pre[:], writes=[self.gpre], track=self.gpre)
        if self.pre == "rope":
            rs = self.io["rope_scr"]
            self.cosT = c.sb((128, TOK), F32, "cosT"); self.sinT = c.sb((128, TOK), F32, "sinT")
            c.dma("sp", self.cosT[:], rs[0], writes=[self.cosT], track=self.cosT)
            c.dma("sp", self.sinT[:], rs[1], writes=[self.sinT], track=self.sinT)
        for sg in range(TOK // SG):
            self.run_sg(sg)
        if self.io.get("rope_build"):
            self.build_rope_tables()
            rs = self.io["rope_scr"]
            c.dma("sp", rs[0], self.cosT[:], reads=[self.cosT], track=self.cosT)
            c.dma("sp", rs[1], self.sinT[:], reads=[self.sinT], track=self.sinT)

    def build_rope_tables(self):
        c = self.c; nc = c.nc
        self.cosT = c.sb((128, TOK), F32, "cosT"); self.sinT = c.sb((128, TOK), F32, "sinT")
        CW = 512
        posi = c.sb((128, CW), I32, "posi"); ang = c.sb((128, CW), F32, "ang")
        kf = c.sb((128, CW), F32, "kf"); ki = c.sb((128, CW), I32, "ki"); m = c.sb((128, CW), F32, "rm")
        inv = c.sb((128, 2), F32, "invf_sb")
        c.dma("sp", inv[:], self.invf[:], writes=[inv], track=inv)
        V = nc.vector; G = nc.gpsimd
        TWO_PI = 2.0 * math.pi
        C1 = 6.28125; C2 = TWO_PI - C1

        def wrap(t):
            c.op("pool", lambda: G.tensor_scalar(out=m[:], in0=t[:], scalar1=math.pi, scalar2=-TWO_PI, op0=ALU.is_gt, op1=ALU.mult), [t], [m])
            c.op("dve", lambda: V.tensor_tensor(out=t[:], in0=t[:], in1=m[:], op=ALU.add), [t, m], [t])
            c.op("pool", lambda: G.tensor_scalar(out=m[:], in0=t[:], scalar1=-math.pi, scalar2=TWO_PI, op0=ALU.is_lt, op1=ALU.mult), [t], [m])
            c.op("dve", lambda: V.tensor_tensor(out=t[:], in0=t[:], in1=m[:], op=ALU.add), [t, m], [t])

        for ch in range(TOK // CW):
            sl = slice(ch * CW, (ch + 1) * CW)
            c.dma("sp", posi[:], self.pos[:, sl].partition_broadcast(128), writes=[posi], track=posi)
            c.op("dve", lambda: V.tensor_copy(out=ang[:], in_=posi[:]), [posi], [ang])
            c.op("pool", lambda: G.tensor_scalar(out=ang[:], in0=ang[:], scalar1=inv[:, 0:1], scalar2=None, op0=ALU.mult), [ang, inv], [ang])
            c.op("dve", lambda: V.tensor_scalar(out=kf[:], in0=ang[:], scalar1=1.0 / TWO_PI, scalar2=0.5, op0=ALU.mult, op1=ALU.add), [ang], [kf])
            c.op("pool", lambda: G.tensor_copy(out=ki[:], in_=kf[:]), [kf], [ki])
            c.op("dve", lambda: V.tensor_copy(out=kf[:], in_=ki[:]), [ki], [kf])
            c.op("dve", lambda: V.scalar_tensor_tensor(out=ang[:], in0=kf[:], scalar=-C1, in1=ang[:], op0=ALU.mult, op1=ALU.add), [kf, ang], [ang])
            c.op("dve", lambda: V.scalar_tensor_tensor(out=ang[:], in0=kf[:], scalar=-C2, in1=ang[:], op0=ALU.mult, op1=ALU.add), [kf, ang], [ang])
            wrap(ang); wrap(ang)
            c.op("act", lambda: nc.scalar.activation(out=self.sinT[:, sl], in_=ang[:], func=AF.Sin), [ang], [self.sinT])
            c.op("pool", lambda: G.tensor_scalar(out=self.sinT[:, sl], in0=self.sinT[:, sl], scalar1=inv[:, 1:2], scalar2=None, op0=ALU.mult), [self.sinT, inv], [self.sinT])
            c.op("dve", lambda: V.tensor_scalar(out=ang[:], in0=ang[:], scalar1=math.pi / 2, scalar2=None, op0=ALU.add), [ang], [ang])
            wrap(ang)
            c.op("act", lambda: nc.scalar.activation(out=self.cosT[:, sl], in_=ang[:], func=AF.Sin), [ang], [self.cosT])

    def proj(self, wd, J, K, act, evac):
        c = self.c; nc = c.nc
        for j in range(J):
            wt = self.wrot.next()
            wv = wt[:, 0:K * 128]
            c.dma("pool", wv, wd[j].rearrange("p k m -> p (k m)"), writes=[wt], track=wt)
            for tg in range(SG // TG):
                bank = self.prot.next()
                for k in range(K):
                    c.op("pe", lambda k=k: nc.tensor.matmul(bank[:], lhsT=wt[:, k * 128:(k + 1) * 128],
                                                             rhs=act[:, k, tg * TG:(tg + 1) * TG],
                                                             start=(k == 0), stop=(k == K - 1)),
                         [wt, act], [bank])
                evac(j, tg, bank)

    def stats(self, src):
        c = self.c; nc = c.nc
        for tg in range(SG // TG):
            for k in range(8):
                sq = self.sqrot.next()
                c.op("act", lambda: nc.scalar.activation(out=sq[:], in_=src[:, k, tg * TG:(tg + 1) * TG], func=AF.Square), [src], [sq])
                c.op("pe", lambda: nc.tensor.matmul(self.ssb[tg][:], lhsT=self.ones[:], rhs=sq[:], start=(k == 0), stop=(k == 7)),
                     [self.ones, sq], [self.ssb[tg]])
            r = self.rstd[tg]
            c.op("act", lambda: nc.scalar.activation(out=r[:], in_=self.ssb[tg][:], func=AF.Sqrt, scale=1.0 / D, bias=self.epsb[:, 0:1]), [self.ssb[tg], self.epsb], [r])
            c.op("dve", lambda: nc.vector.reciprocal(out=r[:], in_=r[:]), [r], [r])

    def acc_stats(self, src_ap, srcbuf, j, tg):
        c = self.c; nc = c.nc
        sq = self.sqrot.next()
        c.op("act", lambda: nc.scalar.activation(out=sq[:], in_=src_ap, func=AF.Square), [srcbuf], [sq])
        self.acc_pending.append((sq, j, tg))
        while len(self.acc_pending) > 3:
            self._acc_mm(*self.acc_pending.pop(0))

    def _acc_mm(self, sq, j, tg):
        c = self.c; nc = c.nc
        c.op("pe", lambda: nc.tensor.matmul(self.ssb[tg][:], lhsT=self.ones[:], rhs=sq[:], start=(j == 0), stop=(j == 7)),
             [self.ones, sq], [self.ssb[tg]])

    def finish_stats(self):
        c = self.c; nc = c.nc
        while self.acc_pending:
            self._acc_mm(*self.acc_pending.pop(0))
        for tg in range(SG // TG):
            r = self.rstd[tg]
            c.op("act", lambda: nc.scalar.activation(out=r[:], in_=self.ssb[tg][:], func=AF.Sqrt, scale=1.0 / D, bias=self.epsb[:, 0:1]), [self.ssb[tg], self.epsb], [r])
            c.op("dve", lambda: nc.vector.reciprocal(out=r[:], in_=r[:]), [r], [r])

    def norm_add(self, gi):
        c = self.c; nc = c.nc
        self.finish_stats()
        for tg in range(SG // TG):
            sl = slice(tg * TG, (tg + 1) * TG)
            for k in range(8):
                t = self.tmprot.next()
                c.op("dve", lambda: nc.vector.scalar_tensor_tensor(out=t[:], in0=self.yT[:, k, sl], scalar=self.gp[:, gi, k:k + 1],
                                                                   in1=self.rstd[tg][:], op0=ALU.mult, op1=ALU.mult),
                     [self.yT, self.gp, self.rstd[tg]], [t])
                c.op("dve", lambda: nc.vector.tensor_tensor(out=self.xT[:, k, sl], in0=self.xT[:, k, sl], in1=t[:], op=ALU.add),
                     [self.xT, t], [self.xT])
                self.acc_stats(self.xT[:, k, sl], self.xT, k, tg)

    def norm_to_a(self, gains, gi=None, have_stats=False):
        c = self.c; nc = c.nc
        if have_stats:
            self.finish_stats()
        else:
            self.stats(self.xT)
        for tg in range(SG // TG):
            sl = slice(tg * TG, (tg + 1) * TG)
            for k in range(8):
                g = gains[:, gi, k:k + 1] if gi is not None else gains[:, k:k + 1]
                c.op("dve", lambda: nc.vector.scalar_tensor_tensor(out=self.aT[:, k, sl], in0=self.xT[:, k, sl], scalar=g,
                                                                   in1=self.rstd[tg][:], op0=ALU.mult, op1=ALU.mult),
                     [self.xT, gains, self.rstd[tg]], [self.aT])

    def run_sg(self, sg):
        c = self.c; nc = c.nc
        t0 = sg * SG
        if sg == 0:
            self.epsb = c.sb((128, 1), F32, "epsb")
            c.op("pool", lambda: nc.gpsimd.memset(self.epsb[:], EPS), [], [self.epsb])
        c.dma("sp", self.xT[:], self.x_in[:, t0:t0 + SG].rearrange("(k p) t -> p k t", p=128), writes=[self.xT], track=self.xT)
        if self.post:
            if sg == 0:
                for k in range(8):
                    c.gather(self.aT[:, k, :], self.io["o_all2d"], self.oidx[:, k:k + 1], self.aT, [self.io["o_all_buf"], self.oidx])
            c.dma("pool", self.pT[:], self.p_in[:, t0:t0 + SG].rearrange("(k p) t -> p k t", p=128), writes=[self.pT], track=self.pT)

            def evac_y(j, tg, bank):
                c.op("act", lambda: nc.scalar.copy(out=self.yT[:, j, tg * TG:(tg + 1) * TG], in_=bank[:]), [bank], [self.yT])
                self.acc_stats(bank[:], bank, j, tg)

            self.proj(self.w_mo, 8, 8, (self.aT if sg == 0 else self.oB), evac_y)
            if sg == 0:
                for k in range(8):
                    c.gather(self.oB[:, k, :], self.io["o_all2d"], self.oidx[:, 8 + k:8 + k + 1], self.oB, [self.io["o_all_buf"], self.oidx])
            self.norm_add(0)
            self.norm_to_a(self.gp, 1, have_stats=True)
            pend = {}

            def evac_f1(j, tg, bank):
                cch, half = divmod(j, 2)
                if half == 0:
                    pend[tg] = bank
                    return
                b1 = pend.pop(tg)
                t = self.tmprot.next()
                c.op("act", lambda: nc.scalar.activation(out=t[:], in_=b1[:], func=AF.Silu), [b1], [t])
                c.op("dve", lambda: nc.vector.tensor_tensor(out=self.gT[:, cch, tg * TG:(tg + 1) * TG], in0=t[:], in1=bank[:], op=ALU.mult),
                     [t, bank], [self.gT])

            self.proj(self.w_f1, 44, 8, self.aT, evac_f1)
            self.proj(self.w_f2, 8, 22, self.gT, evac_y)
            self.norm_add(2)
            for k in range(8):
                c.op("dve", lambda: nc.vector.tensor_copy(out=self.aT[:, k, :], in_=self.xT[:, k, :]), [self.xT], [self.aT])

            def evac_gate(j, tg, bank):
                c.op("act", lambda: nc.scalar.activation(out=self.yT[:, j, tg * TG:(tg + 1) * TG], in_=bank[:], func=AF.Sigmoid,
                                                         bias=self.gp[:, 4, j:j + 1]), [bank, self.gp], [self.yT])

            self.proj(self.w_pg, 8, 8, self.aT, evac_gate)

            def evac_pp(j, tg, bank):
                sl = slice(tg * TG, (tg + 1) * TG)
                c.op("dve", lambda: nc.vector.tensor_tensor(out=self.yT[:, j, sl], in0=self.yT[:, j, sl], in1=bank[:], op=ALU.mult),
                     [self.yT, bank], [self.yT])
                self.acc_stats(self.yT[:, j, sl], self.yT, j, tg)

            self.proj(self.w_pp, 8, 2, self.pT, evac_pp)
            self.norm_add(3)
        t = c.dma("sp", self.x_out[:, t0:t0 + SG].rearrange("(k p) t -> p k t", p=128), self.xT[:], reads=[self.xT], track=self.xT)
        if self.io.get("final"):
            c.out_tickets.append(t)
        if not self.pre:
            return
        self.norm_to_a(self.gpre, have_stats=self.post)
        pr = self.pr_out

        def store(st, jo, tg):
            typ, hc = divmod(jo, 8)
            g, half = divmod(hc, 2)
            if self.pre == "rglru":
                dst = pr[g * 4 + typ * 2 + half, :, t0 + tg * TG:t0 + (tg + 1) * TG]
            else:
                dst = pr[g * 3 + typ, half * 128:(half + 1) * 128, t0 + tg * TG:t0 + (tg + 1) * TG]
            tk = c.dma("sp", dst, st[:], reads=[st], track=st)
            if sg == TOK // SG - 1:
                piece = (g * 4 + typ * 2 + half) if self.pre == "rglru" else (g * 3 + typ)
                lst = self.piece_tix.setdefault(piece, [])
                lst.append(tk)
                if len(lst) == (2 if self.pre == "rglru" else 4):
                    self.ag_queue.append((piece, lst))
                    while len(self.ag_queue) > 2:
                        pk, l = self.ag_queue.pop(0)
                        c._wait("pool", l); self.io["ag_pr"](pk)

        if self.pre == "rglru":
            def evac(j, tg, bank):
                st = self.strot.next()
                c.op("act", lambda: nc.scalar.copy(out=st[:], in_=bank[:]), [bank], [st])
                store(st, j, tg)
        elif self.pre == "plain":
            def evac(j, tg, bank):
                st = self.strot.next()
                c.op("act", lambda: nc.scalar.activation(out=st[:], in_=bank[:], func=AF.Copy, scale=(0.125 if j < 8 else 1.0)), [bank], [st])
                store(st, j, tg)
        else:
            pend = {}

            def evac(j, tg, bank):
                if j >= 32:
                    st = self.strot.next()
                    c.op("act", lambda: nc.scalar.copy(out=st[:], in_=bank[:]), [bank], [st])
                    store(st, j - 16, tg)
                    return
                jj, var = divmod(j, 2)
                if var == 0:
                    pend[tg] = bank
                    return
                b1 = pend.pop(tg)
                sc = 0.125 if jj < 8 else 1.0
                tsl = slice(t0 + tg * TG, t0 + (tg + 1) * TG)
                t1 = self.tmprot.next(); t2 = self.tmprot.next(); st = self.strot.next()
                c.op("dve", lambda: nc.vector.scalar_tensor_tensor(out=t1[:], in0=b1[:], scalar=sc, in1=self.cosT[:, tsl], op0=ALU.mult, op1=ALU.mult),
                     [b1, self.cosT], [t1])
                c.op("dve", lambda: nc.vector.scalar_tensor_tensor(out=t2[:], in0=bank[:], scalar=sc, in1=self.sinT[:, tsl], op0=ALU.mult, op1=ALU.mult),
                     [bank, self.sinT], [t2])
                c.op("dve", lambda: nc.vector.tensor_tensor(out=st[:], in0=t1[:], in1=t2[:], op=ALU.add), [t1, t2], [st])
                store(st, jj, tg)
        self.proj(self.w_pre, self.npre, 8, self.aT, evac)
        while self.ag_queue:
            pk, l = self.ag_queue.pop(0)
            c._wait("pool", l); self.io["ag_pr"](pk)


def emit_dense(c, post, pre, io):
    Dense(c, post, pre, io).build()
    c.end_phase()


def emit_rglru(c, io):
    nc = c.nc
    CH = 2048
    cw = io["cw"]; gw = io["gw"]; o_loc = io["o_loc"].rearrange("a p s -> (a p) s")
    V = nc.vector; G = nc.gpsimd; A = nc.scalar
    cws = c.sb((128, 2, 8), F32, "cws"); c.dma("sp", cws[:], cw, writes=[cws], track=cws)
    gws = c.sb((128, 4, 128), BF16, "gws")
    c.dma("pool", gws[:], gw.rearrange("a b p m -> p (a b) m"), writes=[gws], track=gws)
    ridx = c.sb((128, 16), mybir.dt.uint32, "ridx"); c.dma("sp", ridx[:], io["idx_r"], writes=[ridx], track=ridx)
    cs = c.sb((128, 2), F32, "cs")
    onec = c.sb((128, 1), F32, "onec")
    c.op("pool", lambda: G.memset(onec[:], 1.0), [], [onec])
    for ct in range(2):
        c.op("act", lambda: A.activation(out=cs[:, ct:ct + 1], in_=cws[:, ct, 7:8], func=AF.Exp, scale=-1.0), [cws], [cs])
        c.op("act", lambda: A.activation(out=cs[:, ct:ct + 1], in_=cs[:, ct:ct + 1], func=AF.Ln, bias=onec[:, 0:1]), [cs, onec], [cs])
    c.op("dve", lambda: V.tensor_scalar(out=cs[:], in0=cs[:], scalar1=-8.0, scalar2=None, op0=ALU.mult), [cs], [cs])
    xfull = c.sb((128, S + 3), F32, "xfull")
    yr = Rot([c.sb((128, CH), F32, f"y{i}") for i in range(2)])
    xc = c.sb((128, CH), F32, "xc"); xcb = c.sb((128, CH), BF16, "xcb")
    ra = c.sb((128, CH), F32, "ra"); mm = c.sb((128, CH), F32, "mm"); ib = c.sb((128, CH), F32, "ib")
    hh = c.sb((128, CH), F32, "hh"); tt = c.sb((128, CH), F32, "tt")
    orot = Rot([c.sb((128, CH), BF16, f"o{i}") for i in range(2)])
    carry = c.sb((128, 1), F32, "carry")
    prot = Rot([c.ps(name=f"pb{i}") for i in range(4)])
    src = io["pr_all2d"]; srcb = io["pr_all_buf"]
    for ct in range(2):
        rows = slice(ct * 128, (ct + 1) * 128)
        otix = []
        c.op("dve", lambda: V.memset(xfull[:, 0:3], 0.0), [], [xfull])
        for j in range(4):
            col = (2 + ct) * 4 + j
            c.gather(xfull[:, 3 + j * CH:3 + (j + 1) * CH], src, ridx[:, col:col + 1], xfull, [srcb, ridx])
        for tc in range(S // CH):
            t0 = tc * CH
            y = yr.next()
            col = ct * 4 + tc
            c.gather(y[:], src, ridx[:, col:col + 1], y, [srcb, ridx])
            c.op("dve", lambda: V.tensor_scalar(out=xc[:], in0=xfull[:, t0:t0 + CH], scalar1=cws[:, ct, 0:1], scalar2=cws[:, ct, 4:5], op0=ALU.mult, op1=ALU.add), [xfull, cws], [xc])
            for tap in range(1, 4):
                c.op("dve", lambda: V.scalar_tensor_tensor(out=xc[:], in0=xfull[:, t0 + tap:t0 + tap + CH], scalar=cws[:, ct, tap:tap + 1], in1=xc[:], op0=ALU.mult, op1=ALU.add), [xfull, cws, xc], [xc])
            c.op("act", lambda: A.copy(out=xcb[:], in_=xc[:]), [xc], [xcb])
            for gi, dst in ((0, ra), (1, ib)):
                for sb_ in range(CH // 512):
                    bank = prot.next(); sl = slice(sb_ * 512, (sb_ + 1) * 512)
                    c.op("pe", lambda: nc.tensor.matmul(bank[:], lhsT=gws[:, ct * 2 + gi, :], rhs=xcb[:, sl], start=True, stop=True), [gws, xcb], [bank])
                    c.op("act", lambda: A.activation(out=dst[:, sl], in_=bank[:], func=AF.Sigmoid, bias=cws[:, ct, 5 + gi:6 + gi]), [bank, cws], [dst])
            c.op("act", lambda: A.activation(out=ra[:], in_=ra[:], func=AF.Exp, scale=cs[:, ct:ct + 1]), [ra, cs], [ra])
            c.op("pool", lambda: G.tensor_tensor(out=mm[:], in0=ra[:], in1=ra[:], op=ALU.mult), [ra], [mm])
            c.op("dve", lambda: V.tensor_scalar(out=mm[:], in0=mm[:], scalar1=-1.0, scalar2=1.0, op0=ALU.mult, op1=ALU.add), [mm], [mm])
            c.op("act", lambda: A.activation(out=mm[:], in_=mm[:], func=AF.Sqrt), [mm], [mm])
            c.op("dve", lambda: V.tensor_tensor(out=ib[:], in0=ib[:], in1=xc[:], op=ALU.mult), [ib, xc], [ib])
            c.op("dve", lambda: V.tensor_tensor(out=ib[:], in0=ib[:], in1=mm[:], op=ALU.mult), [ib, mm], [ib])
            if tc > 0:
                c.op("pool", lambda: G.tensor_tensor(out=carry[:], in0=ra[:, 0:1], in1=hh[:, CH - 1:CH], op=ALU.mult), [ra, hh], [carry])
                c.op("pool", lambda: G.tensor_tensor(out=ib[:, 0:1], in0=ib[:, 0:1], in1=carry[:], op=ALU.add), [ib, carry], [ib])
            c.op("dve", lambda: V.tensor_tensor_scan(out=hh[:], data0=ra[:], data1=ib[:], initial=0.0, op0=ALU.mult, op1=ALU.add),
                 [ra, ib], [hh])
            c.op("pool", lambda: G.tensor_tensor(out=tt[:], in0=y[:], in1=y[:], op=ALU.mult), [y], [tt])
            c.op("dve", lambda: V.tensor_scalar(out=tt[:], in0=tt[:], scalar1=0.044715, scalar2=1.0, op0=ALU.mult, op1=ALU.add), [tt], [tt])
            c.op("dve", lambda: V.tensor_tensor(out=tt[:], in0=tt[:], in1=y[:], op=ALU.mult), [tt, y], [tt])
            c.op("act", lambda: A.activation(out=tt[:], in_=tt[:], func=AF.Sigmoid, scale=2.0 * math.sqrt(2.0 / math.pi)), [tt], [tt])
            c.op("pool", lambda: G.tensor_tensor(out=tt[:], in0=tt[:], in1=y[:], op=ALU.mult), [tt, y], [tt])
            ob = orot.next()
            c.op("dve", lambda: V.tensor_tensor(out=ob[:], in0=hh[:], in1=tt[:], op=ALU.mult), [hh, tt], [ob])
            otix.append(c.dma("sp", o_loc[rows, t0:t0 + CH], ob[:], reads=[ob], track=ob))
        c._wait("pool", otix)
        io["ag_o"](2 * ct); io["ag_o"](2 * ct + 1)
    c.end_phase()


def attn_common(c, io, vcols, dil=False):
    nc = c.nc
    src = io["pr_all2d"]; srcb = io["pr_all_buf"]
    aidx = c.sb((128, 24), mybir.dt.uint32, "aidx"); c.dma("sp", aidx[:], io["idx_a"], writes=[aidx], track=aidx)
    ident = c.sb((128, 128), BF16, "ident"); c.dma("sp", ident[:], io["ident_bf"], writes=[ident], track=ident)
    qs = c.sb((128, 2, S), BF16, "qs"); ks = c.sb((128, 4, S), BF16, "kz"); vs = c.sb((128, S // 128, vcols + (0 if dil else 64)), BF16, "vs")
    vtr = Rot([c.sb((128, 2048), BF16, f"vT{i}") for i in range(2)])
    for h in range(4):
        c.op("pool" if h % 2 else "dve", lambda: (nc.gpsimd if h % 2 else nc.vector).memset(ks[:, h, :], 0.0), [], [ks])
    for s_ in range(2):
        for j in range(4):
            c.gather(qs[:, s_, j * 2048:(j + 1) * 2048], src, aidx[:, s_ * 4 + j:s_ * 4 + j + 1], qs, [srcb, aidx])
            kst = vtr.next()
            c.gather(kst[:], src, aidx[:, 8 + s_ * 4 + j:8 + s_ * 4 + j + 1], kst, [srcb, aidx])
            c.op("dve", lambda: nc.vector.tensor_copy(out=ks[0:64, s_, j * 2048:(j + 1) * 2048], in_=kst[0:64, :]), [kst], [ks])
            c.op("pool", lambda: nc.gpsimd.tensor_copy(out=ks[64:128, 2 + s_, j * 2048:(j + 1) * 2048], in_=kst[64:128, :]), [kst], [ks])
    if dil:
        c.op("dve", lambda: nc.vector.memset(vs[:], 1.0), [], [vs])
    tpr = Rot([c.ps((128, 1024), BF16, name="tp")])
    for vc in range(2):
        for j in range(4):
            vT = vtr.next()
            c.gather(vT[:], src, aidx[:, 16 + vc * 4 + j:16 + vc * 4 + j + 1], vT, [srcb, aidx])
            for q4 in range(4):
                tp = tpr.next()
                for b4 in range(4):
                    blk = q4 * 4 + b4
                    c.op("pe", lambda: nc.tensor.transpose(out=tp[:, b4 * 128:(b4 + 1) * 128], in_=vT[:, blk * 128:(blk + 1) * 128], identity=ident[:]),
                         [vT, ident], [tp])
                b0 = j * 16 + q4 * 4
                if dil:
                    dst = vs[:, b0:b0 + 4, vc * 130:(vc + 1) * 130].rearrange("p b (h c) -> p b h c", c=65)[:, :, :, 0:64]
                    srcp = tp[:, 0:512].rearrange("p (b h c) -> p b h c", b=4, h=2)
                else:
                    dst = vs[:, b0:b0 + 4, vc * 128:(vc + 1) * 128]
                    srcp = tp[:, 0:512].rearrange("p (b c) -> p b c", b=4)
                c.op("act", lambda: nc.scalar.copy(out=dst, in_=srcp), [tp], [vs])
    return qs, ks, vs


def head_ap(t, h):
    if t.ap.shape[1] == 4:
        return lambda sl: t[:, h, sl]
    return lambda sl: t[:, h % 2, sl]


def emit_sb(c, io):
    nc = c.nc
    V = nc.vector; G = nc.gpsimd; A = nc.scalar; PE = nc.tensor
    qs, ks, vs = attn_common(c, io, 256)
    oT = io["o_loc"]
    cs_ = c.sb((128, 18, 128), BF16, "cst_sb"); c.dma("sp", cs_[:], io["sb_cst"], writes=[cs_], track=cs_)
    tri = cs_[:, 0, :]; ones = cs_[:, 1, :]
    onec = c.sb((128, 1), F32, "onec")
    c.op("pool", lambda: G.memset(onec[:], 1.0), [], [onec])
    zr = Rot([c.ps(name=f"z{i}") for i in range(2)])
    br = Rot([c.ps(name=f"b{i}") for i in range(2)])
    cr = Rot([c.ps(name=f"c{i}") for i in range(2)])
    ob = c.ps(name="o0")
    er = Rot([c.sb((128, 512), F32, f"e{i}") for i in range(2)])
    spr = Rot([c.sb((128, 512), BF16, f"sp{i}") for i in range(3)])
    t1r = Rot([c.sb((128, 512), F32, f"t1{i}") for i in range(2)])
    atr = Rot([c.sb((128, 512), BF16, f"at{i}") for i in range(3)])
    R = c.sb((128, 512), F32, "R")
    ost = Rot([c.sb((64, 512), BF16, f"ost{i}") for i in range(2)])
    items = []
    otix = {}
    for h in range(4):
        qh = head_ap(qs, h); kh = head_ap(ks, h)
        for qg in range(S // 512):
            qsl = slice(qg * 512, (qg + 1) * 512)
            nkb = 4 * qg + 4
            for i, kb in enumerate(range(nkb - 1, -1, -1)):
                def mk_item(h=h, qh=qh, kh=kh, qg=qg, qsl=qsl, nkb=nkb, i=i, kb=kb):
                    diag = kb >= 4 * qg
                    ksl = slice(kb * 128, (kb + 1) * 128)
                    zb = zr.next(); e = er.next(); sp = spr.next(); bb = br.next(); cb = cr.next(); t1 = t1r.next(); at = atr.next()
                    mk = cs_[:, 2 + 4 * (kb - 4 * qg):6 + 4 * (kb - 4 * qg), :].rearrange("p a b -> p (a b)") if diag else None
                    last = i == nkb - 1

                    def s1():
                        c.op("pe", lambda: PE.matmul(zb[:], lhsT=kh(ksl), rhs=qh(qsl), start=True, stop=True), [ks, qs], [zb])
                        c.op("act", lambda: A.activation(out=e[:], in_=zb[:], func=AF.Exp), [zb], [e])
                        c.op("act", lambda: A.activation(out=sp[:], in_=e[:], func=AF.Ln, bias=onec[:, 0:1]), [e, onec], [sp])
                        if diag:
                            c.op("pool", lambda: G.tensor_tensor(out=sp[:], in0=sp[:], in1=mk, op=ALU.mult), [sp, cs_], [sp])

                    def s2():
                        c.op("pe", lambda: PE.matmul(bb[:], lhsT=kh(ksl), rhs=qh(qsl), start=True, stop=False), [ks, qs], [bb])
                        c.op("pe", lambda: PE.matmul(bb[:], lhsT=tri, rhs=sp[:], start=False, stop=True), [cs_, sp], [bb])
                        if not last:
                            c.op("pe", lambda: PE.matmul(cb[:], lhsT=ones, rhs=sp[:], start=True, stop=True), [cs_, sp], [cb])

                    def s3a():
                        if i == 0:
                            c.op("pool", lambda: G.memset(R[:], 0.0), [], [R])
                        c.op("dve", lambda: V.tensor_tensor(out=t1[:], in0=bb[:], in1=R[:], op=ALU.subtract), [bb, R], [t1])
                        if not last:
                            c.op("dve", lambda: V.tensor_tensor(out=R[:], in0=R[:], in1=cb[:], op=ALU.add), [R, cb], [R])
                        c.op("act", lambda: A.activation(out=at[:], in_=t1[:], func=AF.Exp), [t1], [at])
                        if diag:
                            c.op("pool", lambda: G.tensor_tensor(out=at[:], in0=at[:], in1=mk, op=ALU.mult), [at, cs_], [at])

                    def s3b():
                        c.op("pe", lambda: PE.matmul(ob[:], lhsT=vs[:, kb, h * 64:h * 64 + 128], rhs=at[:], start=(i == 0), stop=last),
                             [vs, at], [ob])
                        if last:
                            st = ost.next()
                            c.op("act", lambda: A.copy(out=st[:], in_=ob[0:64, :]), [ob], [st])
                            otix.setdefault(h, []).append(c.dma("sp", oT[h, :, qsl], st[:], reads=[st], track=st))
                            if qg == S // 512 - 1:
                                c._wait("pool", otix[h]); io["ag_o"](h)
                    return (s3a, s1, s2, s3b)
                items.append(mk_item())
    run_pipeline(items, (2, 0, 1, 2))
    c.end_phase()


def emit_softmax_attn(c, io, kind, lam_init=0.0):
    nc = c.nc
    V = nc.vector; G = nc.gpsimd; A = nc.scalar; PE = nc.tensor
    dil = kind == "dil"
    vcols = 260 if dil else 256
    qs, ks, vs = attn_common(c, io, vcols, dil)
    NM = 20 if dil else 4
    mk = c.sb((128, NM, 512), BF16, "mk"); c.dma("sp", mk[:], io["dil_masks" if dil else "diff_masks"], writes=[mk], track=mk)
    onesf = c.sb((128, 128), F32, "onesf"); c.dma("sp", onesf[:], io["ones_f"], writes=[onesf], track=onesf)
    zr = Rot([c.ps(name=f"z{i}") for i in range(2)])
    atr = Rot([c.sb((128, 512), BF16, f"at{i}") for i in range(4)])
    if dil:
        oT = io["o_loc"]
        orr = Rot([c.ps(name=f"o{i}") for i in range(2)])
        bcr = Rot([c.ps(name=f"bc{i}") for i in range(2)])
        rl = c.sb((128, 512), F32, "rl"); bcs = c.sb((64, 512), F32, "bcs")
        ost = Rot([c.sb((64, 512), BF16, f"ost{i}") for i in range(2)])
        items = []
        otix = {}
        for h in range(4):
            qh = head_ap(qs, h); kh = head_ap(ks, h)
            for qg in range(S // 512):
                qsl = slice(qg * 512, (qg + 1) * 512)
                ob = orr.next()
                kbs = list(range(max(0, 4 * qg - 16), 4 * qg + 4))
                for i, kb in enumerate(kbs):
                    def mk_item(h=h, qh=qh, kh=kh, qg=qg, qsl=qsl, ob=ob, kbs=kbs, i=i, kb=kb):
                        zb = zr.next(); at = atr.next()
                        mi = (512 * qg - 128 * kb + 384) // 128
                        last = i == len(kbs) - 1

                        def s1():
                            c.op("pe", lambda: PE.matmul(zb[:], lhsT=kh(slice(kb * 128, (kb + 1) * 128)), rhs=qh(qsl), start=True, stop=True), [ks, qs], [zb])
                            c.op("act", lambda: A.activation(out=at[:], in_=zb[:], func=AF.Exp), [zb], [at])
                            c.op("dve", lambda: V.tensor_tensor(out=at[:], in0=at[:], in1=mk[:, mi, :], op=ALU.mult), [at, mk], [at])

                        def s2():
                            c.op("pe", lambda: PE.matmul(ob[0:65, :], lhsT=vs[:, kb, h * 65:(h + 1) * 65], rhs=at[:], start=(i == 0), stop=last),
                                 [vs, at], [ob])
                            if last:
                                bc = bcr.next(); st = ost.next()
                                c.op("dve", lambda: V.reciprocal(out=rl[64:65, :], in_=ob[64:65, :]), [ob], [rl])
                                c.op("pe", lambda: PE.matmul(bc[0:64, :], lhsT=onesf[64:65, 0:64], rhs=rl[64:65, :], start=True, stop=True), [onesf, rl], [bc])
                                c.op("act", lambda: A.copy(out=bcs[:], in_=bc[0:64, :]), [bc], [bcs])
                                c.op("dve", lambda: V.tensor_tensor(out=st[:], in0=ob[0:64, :], in1=bcs[:], op=ALU.mult), [ob, bcs], [st])
                                otix.setdefault(h, []).append(c.dma("sp", oT[h, :, qsl], st[:], reads=[st], track=st))
                                if qg == S // 512 - 1:
                                    c._wait("pool", otix[h]); io["ag_o"](h)
                        return (s1, s2)
                    items.append(mk_item())
        run_pipeline(items, (0, 2))
        c.end_phase()
        return
    oT = io["o_loc"].rearrange("(d a) p s -> d (a p) s", a=2)
    onesb = c.sb((128, 128), BF16, "onesb"); c.dma("sp", onesb[:], io["ones_bf"], writes=[onesb], track=onesb)
    lam = c.sb((1, 4, 64), F32, "lam_sb"); c.dma("sp", lam[:], io["lam"], writes=[lam], track=lam)
    sub = c.sb((128, 1), F32, "sub_sb"); c.dma("sp", sub[:], io["subln"], writes=[sub], track=sub)
    prod = c.sb((1, 2, 64), F32, "prod"); dots = c.sb((1, 2), F32, "dots"); ee = c.sb((1, 2), F32, "ee")
    dl = c.sb((1, 2), F32, "dl"); nl = c.sb((1, 2), F32, "nl"); nlam = c.sb((128, 2), F32, "nlam")
    epsb = c.sb((128, 1), F32, "epsb")
    c.op("pool", lambda: G.memset(epsb[:], 1e-5), [], [epsb])
    for m in range(2):
        c.op("dve", lambda: V.tensor_tensor(out=prod[:, m, :], in0=lam[:, 2 * m, :], in1=lam[:, 2 * m + 1, :], op=ALU.mult), [lam], [prod])
    c.op("pool", lambda: G.memset(dots[:], 0.0), [], [dots])
    c.op("dve", lambda: V.tensor_reduce(out=dots[:], in_=prod[:], op=ALU.add, axis=mybir.AxisListType.X), [prod, dots], [dots])
    c.op("act", lambda: A.activation(out=ee[:], in_=dots[:], func=AF.Exp), [dots], [ee])
    for j in range(2):
        c.op("pool", lambda: G.tensor_tensor(out=dl[:, j:j + 1], in0=ee[:, 1:2], in1=ee[:, 0:1], op=ALU.subtract), [ee], [dl])
    c.op("dve", lambda: V.tensor_scalar(out=nl[:], in0=dl[:], scalar1=-lam_init, scalar2=None, op0=ALU.add), [dl], [nl])
    zb = zr.next()
    c.op("pe", lambda: PE.matmul(zb[:, 0:2], lhsT=onesf[0:1, :], rhs=nl[0:1, :], start=True, stop=True), [onesf, nl], [zb])
    c.op("act", lambda: A.copy(out=nlam[:], in_=zb[:, 0:2]), [zb], [nlam])
    c.op("pool", lambda: G.tensor_scalar(out=sub[:], in0=sub[:], scalar1=(1.0 - lam_init), scalar2=None, op0=ALU.mult), [sub], [sub])
    ob = [c.ps(name=f"o{i}") for i in range(2)]
    lb = [c.ps(name=f"l{i}") for i in range(2)]
    bc = lb[0]; ssb = c.ps(name="ss")
    rl = c.sb((1, 2, 512), F32, "rl"); rbs = [c.sb((128, 512), F32, f"rbs{i}") for i in range(2)]
    t1 = c.sb((128, 512), F32, "t1"); t2 = c.sb((128, 512), F32, "t2"); sq = c.sb((128, 512), BF16, "sq")
    rstd = c.sb((128, 512), F32, "rstd")
    ost = Rot([c.sb((128, 512), BF16, f"ost{i}") for i in range(2)])
    items = []
    for dh in range(2):
        for qg in range(S // 512):
            qsl = slice(qg * 512, (qg + 1) * 512)
            nkb = 4 * qg + 4
            for kb in range(nkb):
                for m in range(2):
                    def mk_item(dh=dh, qg=qg, qsl=qsl, nkb=nkb, kb=kb, m=m):
                        h = 2 * dh + m
                        qh = head_ap(qs, h); kh = head_ap(ks, h)
                        zb = zr.next(); at = atr.next()

                        def s1():
                            c.op("pe", lambda: PE.matmul(zb[:], lhsT=kh(slice(kb * 128, (kb + 1) * 128)), rhs=qh(qsl), start=True, stop=True), [ks, qs], [zb])
                            c.op("act", lambda: A.activation(out=at[:], in_=zb[:], func=AF.Exp), [zb], [at])
                            if kb >= 4 * qg:
                                c.op("dve", lambda: V.tensor_tensor(out=at[:], in0=at[:], in1=mk[:, kb - 4 * qg, :], op=ALU.mult), [at, mk], [at])

                        def s2():
                            c.op("pe", lambda: PE.matmul(ob[m][:], lhsT=vs[:, kb, dh * 128:(dh + 1) * 128], rhs=at[:], start=(kb == 0), stop=(kb == nkb - 1)),
                                 [vs, at], [ob[m]])
                            c.op("pe", lambda: PE.matmul(lb[m][:], lhsT=onesb[:], rhs=at[:], start=(kb == 0), stop=(kb == nkb - 1)),
                                 [onesb, at], [lb[m]])
                            if kb == nkb - 1 and m == 1:
                                epilogue(dh, qsl)
                        return (s1, s2)
                    items.append(mk_item())

    def epilogue(dh, qsl):
        for m in range(2):
            c.op("dve", lambda: V.reciprocal(out=rbs[m][:], in_=lb[m][:]), [lb[m]], [rbs[m]])
        c.op("dve", lambda: V.tensor_tensor(out=t1[:], in0=ob[0][:], in1=rbs[0][:], op=ALU.mult), [ob[0], rbs[0]], [t1])
        c.op("dve", lambda: V.tensor_tensor(out=t2[:], in0=ob[1][:], in1=rbs[1][:], op=ALU.mult), [ob[1], rbs[1]], [t2])
        c.op("pool", lambda: G.tensor_scalar(out=t2[:], in0=t2[:], scalar1=nlam[:, 0:1], scalar2=None, op0=ALU.mult), [t2, nlam], [t2])
        c.op("dve", lambda: V.tensor_tensor(out=t1[:], in0=t1[:], in1=t2[:], op=ALU.add), [t1, t2], [t1])
        c.op("act", lambda: A.activation(out=sq[:], in_=t1[:], func=AF.Square), [t1], [sq])
        c.op("pe", lambda: PE.matmul(ssb[:], lhsT=onesb[:], rhs=sq[:], start=True, stop=True), [onesb, sq], [ssb])
        c.op("act", lambda: A.activation(out=rstd[:], in_=ssb[:], func=AF.Sqrt, scale=1.0 / 128, bias=epsb[:, 0:1]), [ssb, epsb], [rstd])
        c.op("pool", lambda: G.tensor_copy(out=rstd2[:], in_=rstd[:]), [rstd], [rstd2])
        c.op("dve", lambda: V.reciprocal(out=rstd2[:], in_=rstd2[:]), [rstd2], [rstd2])
        st = ost.next()
        c.op("dve", lambda: V.scalar_tensor_tensor(out=st[:], in0=t1[:], scalar=sub[:, 0:1], in1=rstd2[:], op0=ALU.mult, op1=ALU.mult), [t1, sub, rstd2], [st])
        otix.setdefault(dh, []).append(c.dma("sp", oT[dh, :, qsl], st[:], reads=[st], track=st))
        if len(otix[dh]) == S // 512:
            c._wait("pool", otix[dh]); io["ag_o"](2 * dh); io["ag_o"](2 * dh + 1)

    otix = {}
    rstd2 = c.sb((128, 512), F32, "rstd2")
    bcb = [lb[0], lb[1]]
    run_pipeline(items, (0, 2))
    c.end_phase()


PRE_KINDS = ["rglru", "rope", "plain", "rope"]
LAM_INIT3 = 0.8 - 0.6 * math.exp(-0.3 * 3)
U32 = mybir.dt.uint32


def build_fused(stop=999):
    c = Ctx()
    step = [0]

    def done():
        step[0] += 1
        return step[0] >= stop

    EI = lambda n, sh, dt: c.dram(n, sh, dt, "ExternalInput")
    g = {}
    g["xT"] = EI("xT", (D, TOK), F32)
    g["xT_out"] = c.dram("xT_out", (D, TOK), F32, "ExternalOutput")
    g["ones_bf"] = EI("ones_bf", (128, 128), BF16); g["ones_f"] = EI("ones_f", (128, 128), F32)
    g["ident_bf"] = EI("ident_bf", (128, 128), BF16)
    g["pos"] = EI("pos", (1, TOK), I32); g["invf"] = EI("invf", (128, 2), F32)
    g["idx_o"] = EI("idx_o", (128, 16), U32); g["idx_a"] = EI("idx_a", (128, 24), U32); g["idx_r"] = EI("idx_r", (128, 16), U32)
    g["cw"] = EI("cw", (128, 2, 8), F32); g["gw"] = EI("gw", (2, 2, 128, 128), F32)
    g["dil_masks"] = EI("dil_masks", (128, 20, 512), BF16); g["diff_masks"] = EI("diff_masks", (128, 4, 512), BF16)
    g["sb_cst"] = EI("sb_cst", (128, 18, 128), BF16)
    g["lam"] = EI("lam", (1, 4, 64), F32); g["subln"] = EI("subln", (128, 1), F32)
    L = []
    for l in range(DEPTH):
        npre = {"rglru": 16, "rope": 40, "plain": 24}[PRE_KINDS[l]]
        L.append(dict(w_mo=EI(f"w_mo{l}", (8, 128, 8, 128), F32), w_f1=EI(f"w_f1{l}", (44, 128, 8, 128), F32),
                      w_f2=EI(f"w_f2{l}", (8, 128, 22, 128), F32), w_pg=EI(f"w_pg{l}", (8, 128, 8, 128), F32),
                      w_pp=EI(f"w_pp{l}", (8, 128, 2, 128), F32), p_in=EI(f"pT{l}", (PLE, TOK), F32),
                      gains_post=EI(f"gains_post{l}", (128, 5, 8), F32), gain_pre=EI(f"gain_pre{l}", (128, 8), F32),
                      w_pre=EI(f"w_pre{l}", (npre, 128, 8, 128), F32)))
    x_scr = c.scratch("x_scr", (D, TOK), F32)
    pr_loc_r = c.scratch("pr_loc_r", (16, 128, TOK), F32); pr_all_r = c.scratch("pr_all_r", (16, 4 * 128, TOK), F32)
    pr_loc_a = c.scratch("pr_loc_a", (12, 256, TOK), BF16); pr_all_a = c.scratch("pr_all_a", (12, 4 * 256, TOK), BF16)
    o_loc = c.scratch("o_loc", (4, 64, S), BF16); o_all = c.scratch("o_all", (4, 4 * 64, S), BF16)
    rope_scr = c.scratch("rope_scr", (2, 128, TOK), F32)
    pr_all_r_buf = Buf(pr_all_r); pr_all_a_buf = Buf(pr_all_a); o_all_buf = Buf(o_all)
    o_all2d = o_all.rearrange("a r (c t) -> (a r c) t", t=SG)
    for i in range(DEPTH + 1):
        post = i > 0; pre = PRE_KINDS[i] if i < DEPTH else None
        io = dict(x_in=(g["xT"] if i == 0 else x_scr), x_out=(g["xT_out"] if i == DEPTH else x_scr), ones_bf=g["ones_bf"],
                  final=(i == DEPTH), idx_o=g["idx_o"], o_all2d=o_all2d, o_all_buf=o_all_buf,
                  rope_scr=rope_scr, rope_build=(i == 0))
        if post:
            io.update({k: L[i - 1][k] for k in ("w_mo", "w_f1", "w_f2", "w_pg", "w_pp", "p_in", "gains_post")})
        if pre:
            io.update(gain_pre=L[i]["gain_pre"], w_pre=L[i]["w_pre"], pos=g["pos"], invf=g["invf"],
                      pr_loc=(pr_loc_r if pre == "rglru" else pr_loc_a))
            if pre == "rglru":
                io["ag_pr"] = lambda k: c.all_gather(pr_loc_r[k], pr_all_r[k], pr_all_r_buf)
            else:
                io["ag_pr"] = lambda k: c.all_gather(pr_loc_a[k], pr_all_a[k], pr_all_a_buf)
        emit_dense(c, post, pre, io)
        if not pre or done():
            break
        if pre == "rglru":
            mio = dict(pr_all2d=pr_all_r.rearrange("a r t -> (a r) t"), pr_all_buf=pr_all_r_buf)
        else:
            mio = dict(pr_all2d=pr_all_a.rearrange("a r t -> (a r) t"), pr_all_buf=pr_all_a_buf)
        mio["ag_o"] = lambda k: c.all_gather(o_loc[k], o_all[k], o_all_buf)
        if done():
            break
        mio.update(g); mio["o_loc"] = o_loc
        if i == 0:
            emit_rglru(c, mio)
        elif i == 1:
            emit_softmax_attn(c, mio, "dil")
        elif i == 2:
            emit_sb(c, mio)
        else:
            emit_softmax_attn(c, mio, "diff", LAM_INIT3)
        if done():
            break
        if done():
            break
    c.barrier()
    c.finish(c.out_tickets)
    return c.nc


def _tok(cc):
    b, j = divmod(cc, 4)
    return b, slice(j * TOK, (j + 1) * TOK)


def _rope_cols():
    cols = []
    for jj in range(16):
        base = jj * 128
        e = np.arange(128)
        cols.append(np.arange(base, base + 128))
        cols.append(base + (e // 64) * 64 + ((e % 64) + 32) % 64)
    cols.append(np.arange(2048, 3072))
    return np.concatenate(cols)


def _mult(dist):
    m = np.zeros(dist.shape, np.float32)
    for dd in (1, 4, 16):
        m += ((dist >= 0) & (dist % dd == 0) & (dist <= 128 * dd)).astype(np.float32)
    return m


def kernel(**inp):
    inp = {k: np.asarray(v) for k, v in inp.items()}
    x = inp["x"]; p = inp["p"]; pos = inp["positions"].astype(np.int32)
    pp = np.arange(128)[:, None]; ff = np.arange(512)[None, :]
    sh = {"ones_bf": np.ones((128, 128), NPBF), "ones_f": np.ones((128, 128), np.float32),
          "ident_bf": np.eye(128, dtype=np.float32).astype(NPBF)}
    invf = np.zeros((128, 2), np.float32)
    e = np.arange(128) % 64
    invf[:, 0] = (np.float32(10000.0) ** (-(np.arange(0, 64, 2, dtype=np.float32)) / np.float32(64)))[e % 32]
    invf[:, 1] = np.where(e < 32, -1.0, 1.0)
    sh["invf"] = invf
    f1cols = np.concatenate([np.concatenate([np.arange(cc * 128, (cc + 1) * 128), FF + np.arange(cc * 128, (cc + 1) * 128)]) for cc in range(22)])
    rcols = _rope_cols()
    mix_out_w = [inp["a_w_out"][0], inp["b_w_out"][0], inp["c_w_out"][0], inp["d_w_out"][0]]
    pre_w = [w_chunks(inp["a_w_in"][0]), w_chunks(inp["b_w_qkv"][0], rcols), w_chunks(inp["c_w_qkv"][0]), w_chunks(inp["d_w_qkv"][0], rcols)]
    for l in range(DEPTH):
        sh[f"w_mo{l}"] = w_chunks(mix_out_w[l]); sh[f"w_f1{l}"] = w_chunks(inp["w_ffn_in"][l], f1cols)
        sh[f"w_f2{l}"] = w_chunks(inp["w_ffn_out"][l]); sh[f"w_pg{l}"] = w_chunks(inp["w_ple_gate"][l])
        sh[f"w_pp{l}"] = w_chunks(inp["w_ple_proj"][l])
        sh[f"gains_post{l}"] = np.ascontiguousarray(np.stack([col_vec(inp[n][l]) for n in
                                                             ("ln_mix_post", "ln_ffn_pre", "ln_ffn_post", "ln_ple", "b_ple_gate")], axis=1))
        sh[f"gain_pre{l}"] = col_vec(inp["ln_mix_pre"][l]); sh[f"w_pre{l}"] = pre_w[l]
    sh["dil_masks"] = np.ascontiguousarray(np.stack([_mult((128 * mi - 384) + ff - pp) for mi in range(20)], axis=1)).astype(NPBF)
    sh["diff_masks"] = np.ascontiguousarray(np.stack([((ff - pp - 128 * j) >= 0).astype(np.float32) for j in range(4)], axis=1)).astype(NPBF)
    cst = np.zeros((128, 18, 128), np.float32)
    cst[:, 0, :] = -1.0 * (pp >= np.arange(128)[None, :]); cst[:, 1, :] = 1.0
    for o in range(4):
        cst[:, 2 + 4 * o:6 + 4 * o, :] = ((128 * o + pp) < ff).astype(np.float32).reshape(128, 4, 128)
    sh["sb_cst"] = cst.astype(NPBF)
    sh["lam"] = np.ascontiguousarray(np.stack([inp[n][0] for n in ("d_lambda_q1", "d_lambda_k1", "d_lambda_q2", "d_lambda_k2")])[None])
    sh["subln"] = np.ascontiguousarray(inp["d_subln"][0].reshape(128, 1))
    pa = np.arange(128)
    maps = []
    for cc in range(NCORE):
        b, ts = _tok(cc)
        r = cc % 4
        m = dict(sh)
        m["xT"] = fm(x[b, ts]); m["pos"] = np.ascontiguousarray(pos[b:b + 1, ts])
        for l in range(DEPTH):
            m[f"pT{l}"] = fm(p[l, b, ts])
        cw = np.zeros((128, 2, 8), np.float32); gw = np.zeros((2, 2, 128, 128), np.float32)
        for ct in range(2):
            cs = slice(r * 256 + ct * 128, r * 256 + (ct + 1) * 128)
            cw[:, ct, 0:4] = inp["a_conv_w"][0][:, cs].T
            cw[:, ct, 4] = inp["a_conv_b"][0][cs]; cw[:, ct, 5] = inp["a_gate_r_b"][0][cs]
            cw[:, ct, 6] = inp["a_gate_i_b"][0][cs]; cw[:, ct, 7] = inp["a_lambda"][0][cs]
            for gi, nm in enumerate(("a_gate_r_w", "a_gate_i_w")):
                for hh in range(2):
                    n = r * 4 + ct * 2 + hh
                    gw[ct, gi, hh * 64:(hh + 1) * 64, hh * 64:(hh + 1) * 64] = inp[nm][0][n]
        m["cw"] = cw; m["gw"] = gw
        io_ = np.zeros((128, 16), np.uint32)
        for sg in range(2):
            for k in range(8):
                rl = (k % 2) * 128 + pa
                io_[:, sg * 8 + k] = ((((rl // 64) * 4 + k // 2) * 64 + rl % 64) * 4 + r) * 2 + sg
        m["idx_o"] = io_
        ia = np.zeros((128, 24), np.uint32)
        for s_ in range(2):
            for j in range(4):
                hl = 2 * (pa // 64) + s_
                ia[:, s_ * 4 + j] = ((r * 3 + 0) * 4 + j) * 256 + hl * 64 + pa % 64
                ia[:, 8 + s_ * 4 + j] = ((r * 3 + 1) * 4 + j) * 256 + hl * 64 + pa % 64
        for vc in range(2):
            for j in range(4):
                ia[:, 16 + vc * 4 + j] = ((r * 3 + 2) * 4 + j) * 256 + vc * 128 + pa
        m["idx_a"] = ia
        ir = np.zeros((128, 16), np.uint32)
        for tc_ in range(4):
            for j in range(4):
                ir[:, tc_ * 4 + j] = ((r * 4 + tc_) * 4 + j) * 128 + pa
        m["idx_r"] = ir
        maps.append(m)
    nc = build_fused(STOP)
    res = run_bass_kernel_spmd(nc, maps, core_ids=list(range(NCORE))).results
    out = np.zeros((B, S, D), np.float32)
    for cc in range(NCORE):
        b, ts = _tok(cc)
        out[b, ts] = res[cc]["xT_out"].T
    return out


STOP = 999
```

```python
import math
import numpy as np
import ml_dtypes
import concourse.bass as bass
import concourse.mybir as mybir
from concourse.bass_utils import run_bass_kernel_spmd

F32, BF16, I32 = mybir.dt.float32, mybir.dt.bfloat16, mybir.dt.int32
AF = mybir.ActivationFunctionType
ALU = mybir.AluOpType
NPBF = ml_dtypes.bfloat16

D = 1024; B = 2; S = 8192; DEPTH = 4; FF = 2816; PLE = 256; HD = 64; NH = 16
NCORE = 8; TOK = 2048
SG = 1024; TG = 512
EPS = 1e-6


class Buf:
    __slots__ = ("ap", "w", "r", "dsem", "dcnt", "excl")

    def __init__(self, ap, excl=False):
        self.ap = ap; self.w = None; self.r = {}; self.dsem = None; self.dcnt = 0; self.excl = excl

    def __getitem__(self, k):
        return self.ap[k]


class Ctx:
    NDSEM = 40

    def __init__(self):
        self.nc = bass.Bass("TRN2", target_bir_lowering=False)
        nc = self.nc
        self.E = {"pe": nc.tensor, "act": nc.scalar, "dve": nc.vector, "pool": nc.gpsimd, "sp": nc.sync}
        self.sem = {k: nc.semaphore("s_" + k).__enter__() for k in self.E}
        self.cnt = {k: 0 for k in self.E}
        self.waited = {}
        self.nbuf = 0
        self.out_tickets = []
        self.dpool = [[nc.semaphore(f"dq{i}").__enter__(), 0] for i in range(self.NDSEM)]
        self.dpool_i = 0
        self.cc_sem = nc.semaphore("cc").__enter__(); self.cc_cnt = 0
        self.live = []

    def sb(self, shape, dt, name=None):
        self.nbuf += 1
        cm = self.nc.sbuf_tensor(f"{name or 'sb'}_{self.nbuf}", list(shape), dt)
        self.live.append(cm)
        return Buf(cm.__enter__())

    def ps(self, shape=(128, 512), dt=F32, name=None):
        self.nbuf += 1
        cm = self.nc.psum_tensor(f"{name or 'ps'}_{self.nbuf}", list(shape), dt)
        self.live.append(cm)
        return Buf(cm.__enter__(), excl=True)

    def dram(self, name, shape, dt, kind):
        return self.nc.dram_tensor(name, list(shape), dt, kind=kind).ap()

    def scratch(self, name, shape, dt):
        return self.nc.dram_tensor(name, list(shape), dt).ap()

    def _wait(self, eng, tickets):
        for (sem, val, src) in tickets:
            if src == eng:
                continue
            key = (eng, id(sem))
            if self.waited.get(key, 0) < val:
                self.E[eng].wait_ge(sem, val)
                self.waited[key] = val

    def _deps(self, reads, writes):
        tk = []
        for b in reads:
            if b.w is not None:
                tk.append(b.w)
            if b.excl:
                tk.extend(b.r.values())
        for b in writes:
            if b.w is not None:
                tk.append(b.w)
            tk.extend(b.r.values())
        return tk

    def _commit(self, t, reads, writes):
        for b in reads:
            b.r[(id(t[0]), t[2])] = t
        for b in writes:
            b.w = t; b.r = {}

    def op(self, eng, fn, reads=(), writes=()):
        self._wait(eng, self._deps(reads, writes))
        ins = fn()
        self.cnt[eng] += 1
        ins.then_inc(self.sem[eng], 1)
        t = (self.sem[eng], self.cnt[eng], eng)
        self._commit(t, reads, writes)
        return t

    def dma(self, q, out, in_, reads=(), writes=(), track=None):
        self._wait(q, self._deps(reads, writes))
        b = track
        if b.dsem is None:
            assert self.dpool_i < self.NDSEM, "out of dma semaphores in this phase"
            b.dsem = self.dpool[self.dpool_i]; self.dpool_i += 1
        ins = self.E[q].dma_start(out=out, in_=in_)
        b.dsem[1] += 16
        ins.then_inc(b.dsem[0], 16)
        t = (b.dsem[0], b.dsem[1], "dma")
        self._commit(t, reads, writes)
        return t

    def gather(self, out, src2d, idx_col, out_buf, read_bufs):
        self._wait("pool", self._deps(read_bufs, (out_buf,)))
        b = out_buf
        if b.dsem is None:
            assert self.dpool_i < self.NDSEM, "out of dma semaphores in this phase"
            b.dsem = self.dpool[self.dpool_i]; self.dpool_i += 1
        ins = self.nc.gpsimd.indirect_dma_start(out=out, out_offset=None, in_=src2d,
                                                in_offset=bass.IndirectOffsetOnAxis(ap=idx_col, axis=0))
        b.dsem[1] += 16
        ins.then_inc(b.dsem[0], 16)
        t = (b.dsem[0], b.dsem[1], "dma")
        self._commit(t, read_bufs, (out_buf,))
        return t

    def all_gather(self, in_ap, out_ap, out_buf):
        self._wait("pool", self._deps((), (out_buf,)))
        ins = self.nc.gpsimd.collective_compute("AllGather", ALU.bypass, replica_groups=[[0, 1, 2, 3], [4, 5, 6, 7]],
                                                ins=[in_ap.opt()], outs=[out_ap.opt()])
        self.cc_cnt += 1
        ins.then_inc(self.cc_sem)
        t = (self.cc_sem, self.cc_cnt, "cc")
        self._commit(t, (), (out_buf,))
        return t

    def barrier(self):
        tk = [(self.sem[e], self.cnt[e], e) for e in self.E if self.cnt[e] > 0]
        tk += [(d[0], d[1], "dma") for d in self.dpool if d[1] > 0]
        if self.cc_cnt:
            tk.append((self.cc_sem, self.cc_cnt, "cc"))
        for e in self.E:
            self._wait(e, tk)

    def end_phase(self):
        self.barrier()
        for cm in reversed(self.live):
            cm.__exit__(None, None, None)
        self.live = []
        self.dpool_i = 0

    def finish(self, tickets):
        self._wait("sp", tickets)


def run_pipeline(items, skews):
    n = len(items)
    for step in range(n + max(skews)):
        for k, sk in enumerate(skews):
            idx = step - sk
            if 0 <= idx < n:
                items[idx][k]()


class Rot:
    def __init__(self, bufs):
        self.bufs = bufs; self.i = 0

    def next(self):
        b = self.bufs[self.i % len(self.bufs)]; self.i += 1
        return b


def w_chunks(W, cols=None):
    if cols is not None:
        W = W[:, cols]
    Din, Dout = W.shape
    K, J = Din // 128, Dout // 128
    return np.ascontiguousarray(W.reshape(K, 128, J, 128).transpose(2, 1, 0, 3))


def col_vec(v):
    return np.ascontiguousarray(v.reshape(-1, 128).T)


def fm(x2d):
    return np.ascontiguousarray(x2d.T)


class Dense:
    def __init__(self, c, post, pre, io):
        self.c = c
        self.post = post; self.pre = pre
        self.io = io
        self.piece_tix = {}
        self.acc_pending = []
        self.ag_queue = []
        self.x_in = io["x_in"]; self.x_out = io["x_out"]; self.ones_d = io["ones_bf"]
        if post:
            for k in ("w_mo", "w_f1", "w_f2", "w_pg", "w_pp", "p_in", "gains_post"):
                setattr(self, k, io[k])
        if pre:
            self.gain_pre = io["gain_pre"]; self.w_pre = io["w_pre"]
            self.npre = {"rglru": 16, "rope": 40, "plain": 24, "hnorm": 0}[pre]
            self.pre_dt = F32 if pre == "rglru" else BF16
            self.pos = io["pos"]; self.invf = io["invf"]
            self.pr_out = io["pr_loc"]

    def build(self):
        c = self.c; nc = c.nc
        self.xT = c.sb((128, 8, SG), F32, "xT_sb")
        self.aT = c.sb((128, 8, SG), BF16, "aT_sb")
        self.yT = c.sb((128, 8, SG), F32, "yT_sb")
        self.ones = c.sb((128, 128), BF16, "ones_sb")
        self.wrot = Rot([c.sb((128, 22 * 128), BF16, f"w{i}") for i in range(4)])
        self.prot = Rot([c.ps(name=f"pb{i}") for i in range(6)])
        self.ssb = [c.ps(name=f"ss{i}") for i in range(2)]
        self.sqrot = Rot([c.sb((128, TG), BF16, f"sq{i}") for i in range(6)])
        self.rstd = [c.sb((128, TG), F32, f"rstd{i}") for i in range(2)]
        self.tmprot = Rot([c.sb((128, TG), F32, f"tmp{i}") for i in range(4)])
        self.strot = Rot([c.sb((128, TG), self.pre_dt if self.pre else F32, f"st{i}") for i in range(4)])
        c.dma("sp", self.ones[:], self.ones_d[:], writes=[self.ones], track=self.ones)
        if self.post:
            self.oidx = c.sb((128, 16), mybir.dt.uint32, "oidx")
            c.dma("sp", self.oidx[:], self.io["idx_o"], writes=[self.oidx], track=self.oidx)
            self.oB = c.sb((128, 8, SG), BF16, "oB_sb")
            self.gT = c.sb((128, 22, SG), BF16, "gT_sb")
            self.pT = c.sb((128, 2, SG), BF16, "pT_sb")
            self.gp = c.sb((128, 5, 8), F32, "gp_sb")
            c.dma("sp", self.gp[:], self.gains_post[:], writes=[self.gp], track=self.gp)
        if self.pre:
            self.gpre = c.sb((128, 8), F32, "gpre_sb")
            c.dma("sp", self.gpre[:], self.gain_pre[:], writes=[self.gpre], track=self.gpre)
        if self.pre == "rope":
            rs = self.io["rope_scr"]
            self.cosT = c.sb((128, TOK), F32, "cosT"); self.sinT = c.sb((128, TOK), F32, "sinT")
            c.dma("sp", self.cosT[:], rs[0], writes=[self.cosT], track=self.cosT)
            c.dma("sp", self.sinT[:], rs[1], writes=[self.sinT], track=self.sinT)
        for sg in range(TOK // SG):
            self.run_sg(sg)
        if self.io.get("rope_build"):
            self.build_rope_tables()
            rs = self.io["rope_scr"]
            c.dma("sp", rs[0], self.cosT[:], reads=[self.cosT], track=self.cosT)
            c.dma("sp", rs[1], self.sinT[:], reads=[self.sinT], track=self.sinT)

    def build_rope_tables(self):
        c = self.c; nc = c.nc
        self.cosT = c.sb((128, TOK), F32, "cosT"); self.sinT = c.sb((128, TOK), F32, "sinT")
        CW = 512
        posi = c.sb((128, CW), I32, "posi"); ang = c.sb((128, CW), F32, "ang")
        kf = c.sb((128, CW), F32, "kf"); ki = c.sb((128, CW), I32, "ki"); m = c.sb((128, CW), F32, "rm")
        inv = c.sb((128, 2), F32, "invf_sb")
        c.dma("sp", inv[:], self.invf[:], writes=[inv], track=inv)
        V = nc.vector; G = nc.gpsimd
        TWO_PI = 2.0 * math.pi
        C1 = 6.28125; C2 = TWO_PI - C1

        def wrap(t):
            c.op("pool", lambda: G.tensor_scalar(out=m[:], in0=t[:], scalar1=math.pi, scalar2=-TWO_PI, op0=ALU.is_gt, op1=ALU.mult), [t], [m])
            c.op("dve", lambda: V.tensor_tensor(out=t[:], in0=t[:], in1=m[:], op=ALU.add), [t, m], [t])
            c.op("pool", lambda: G.tensor_scalar(out=m[:], in0=t[:], scalar1=-math.pi, scalar2=TWO_PI, op0=ALU.is_lt, op1=ALU.mult), [t], [m])
            c.op("dve", lambda: V.tensor_tensor(out=t[:], in0=t[:], in1=m[:], op=ALU.add), [t, m], [t])

        for ch in range(TOK // CW):
            sl = slice(ch * CW, (ch + 1) * CW)
            c.dma("sp", posi[:], self.pos[:, sl].partition_broadcast(128), writes=[posi], track=posi)
            c.op("dve", lambda: V.tensor_copy(out=ang[:], in_=posi[:]), [posi], [ang])
            c.op("pool", lambda: G.tensor_scalar(out=ang[:], in0=ang[:], scalar1=inv[:, 0:1], scalar2=None, op0=ALU.mult), [ang, inv], [ang])
            c.op("dve", lambda: V.tensor_scalar(out=kf[:], in0=ang[:], scalar1=1.0 / TWO_PI, scalar2=0.5, op0=ALU.mult, op1=ALU.add), [ang], [kf])
            c.op("pool", lambda: G.tensor_copy(out=ki[:], in_=kf[:]), [kf], [ki])
            c.op("dve", lambda: V.tensor_copy(out=kf[:], in_=ki[:]), [ki], [kf])
            c.op("dve", lambda: V.scalar_tensor_tensor(out=ang[:], in0=kf[:], scalar=-C1, in1=ang[:], op0=ALU.mult, op1=ALU.add), [kf, ang], [ang])
            c.op("dve", lambda: V.scalar_tensor_tensor(out=ang[:], in0=kf[:], scalar=-C2, in1=ang[:], op0=ALU.mult, op1=ALU.add), [kf, ang], [ang])
            wrap(ang); wrap(ang)
            c.op("act", lambda: nc.scalar.activation(out=self.sinT[:, sl], in_=ang[:], func=AF.Sin), [ang], [self.sinT])
            c.op("pool", lambda: G.tensor_scalar(out=self.sinT[:, sl], in0=self.sinT[:, sl], scalar1=inv[:, 1:2], scalar2=None, op0=ALU.mult), [self.sinT, inv], [self.sinT])
            c.op("dve", lambda: V.tensor_scalar(out=ang[:], in0=ang[:], scalar1=math.pi / 2, scalar2=None, op0=ALU.add), [ang], [ang])
            wrap(ang)
            c.op("act", lambda: nc.scalar.activation(out=self.cosT[:, sl], in_=ang[:], func=AF.Sin), [ang], [self.cosT])

    def proj(self, wd, J, K, act, evac):
        c = self.c; nc = c.nc
        for j in range(J):
            wt = self.wrot.next()
            wv = wt[:, 0:K * 128]
            c.dma("pool", wv, wd[j].rearrange("p k m -> p (k m)"), writes=[wt], track=wt)
            for tg in range(SG // TG):
                bank = self.prot.next()
                for k in range(K):
                    c.op("pe", lambda k=k: nc.tensor.matmul(bank[:], lhsT=wt[:, k * 128:(k + 1) * 128],
                                                             rhs=act[:, k, tg * TG:(tg + 1) * TG],
                                                             start=(k == 0), stop=(k == K - 1)),
                         [wt, act], [bank])
                evac(j, tg, bank)

    def stats(self, src):
        c = self.c; nc = c.nc
        for tg in range(SG // TG):
            for k in range(8):
                sq = self.sqrot.next()
                c.op("act", lambda: nc.scalar.activation(out=sq[:], in_=src[:, k, tg * TG:(tg + 1) * TG], func=AF.Square), [src], [sq])
                c.op("pe", lambda: nc.tensor.matmul(self.ssb[tg][:], lhsT=self.ones[:], rhs=sq[:], start=(k == 0), stop=(k == 7)),
                     [self.ones, sq], [self.ssb[tg]])
            r = self.rstd[tg]
            c.op("act", lambda: nc.scalar.activation(out=r[:], in_=self.ssb[tg][:], func=AF.Sqrt, scale=1.0 / D, bias=self.epsb[:, 0:1]), [self.ssb[tg], self.epsb], [r])
            c.op("dve", lambda: nc.vector.reciprocal(out=r[:], in_=r[:]), [r], [r])

    def acc_stats(self, src_ap, srcbuf, j, tg):
        c = self.c; nc = c.nc
        sq = self.sqrot.next()
        c.op("act", lambda: nc.scalar.activation(out=sq[:], in_=src_ap, func=AF.Square), [srcbuf], [sq])
        self.acc_pending.append((sq, j, tg))
        while len(self.acc_pending) > 3:
            self._acc_mm(*self.acc_pending.pop(0))

    def _acc_mm(self, sq, j, tg):
        c = self.c; nc = c.nc
        c.op("pe", lambda: nc.tensor.matmul(self.ssb[tg][:], lhsT=self.ones[:], rhs=sq[:], start=(j == 0), stop=(j == 7)),
             [self.ones, sq], [self.ssb[tg]])

    def finish_stats(self):
        c = self.c; nc = c.nc
        while self.acc_pending:
            self._acc_mm(*self.acc_pending.pop(0))
        for tg in range(SG // TG):
            r = self.rstd[tg]
            c.op("act", lambda: nc.scalar.activation(out=r[:], in_=self.ssb[tg][:], func=AF.Sqrt, scale=1.0 / D, bias=self.epsb[:, 0:1]), [self.ssb[tg], self.epsb], [r])
            c.op("dve", lambda: nc.vector.reciprocal(out=r[:], in_=r[:]), [r], [r])

    def norm_add(self, gi):
        c = self.c; nc = c.nc
        self.finish_stats()
        for tg in range(SG // TG):
            sl = slice(tg * TG, (tg + 1) * TG)
            for k in range(8):
                t = self.tmprot.next()
                c.op("dve", lambda: nc.vector.scalar_tensor_tensor(out=t[:], in0=self.yT[:, k, sl], scalar=self.gp[:, gi, k:k + 1],
                                                                   in1=self.rstd[tg][:], op0=ALU.mult, op1=ALU.mult),
                     [self.yT, self.gp, self.rstd[tg]], [t])
                c.op("dve", lambda: nc.vector.tensor_tensor(out=self.xT[:, k, sl], in0=self.xT[:, k, sl], in1=t[:], op=ALU.add),
                     [self.xT, t], [self.xT])
                self.acc_stats(self.xT[:, k, sl], self.xT, k, tg)

    def norm_to_a(self, gains, gi=None, have_stats=False):
        c = self.c; nc = c.nc
        if have_stats:
            self.finish_stats()
        else:
            self.stats(self.xT)
        for tg in range(SG // TG):
            sl = slice(tg * TG, (tg + 1) * TG)
            for k in range(8):
                g = gains[:, gi, k:k + 1] if gi is not None else gains[:, k:k + 1]
                c.op("dve", lambda: nc.vector.scalar_tensor_tensor(out=self.aT[:, k, sl], in0=self.xT[:, k, sl], scalar=g,
                                                                   in1=self.rstd[tg][:], op0=ALU.mult, op1=ALU.mult),
                     [self.xT, gains, self.rstd[tg]], [self.aT])

    def run_sg(self, sg):
        c = self.c; nc = c.nc
        t0 = sg * SG
        if sg == 0:
            self.epsb = c.sb((128, 1), F32, "epsb")
            c.op("pool", lambda: nc.gpsimd.memset(self.epsb[:], EPS), [], [self.epsb])
        c.dma("sp", self.xT[:], self.x_in[:, t0:t0 + SG].rearrange("(k p) t -> p k t", p=128), writes=[self.xT], track=self.xT)
        if self.post:
            if sg == 0:
                for k in range(8):
                    c.gather(self.aT[:, k, :], self.io["o_all2d"], self.oidx[:, k:k + 1], self.aT, [self.io["o_all_buf"], self.oidx])
            c.dma("pool", self.pT[:], self.p_in[:, t0:t0 + SG].rearrange("(k p) t -> p k t", p=128), writes=[self.pT], track=self.pT)

            def evac_y(j, tg, bank):
                c.op("act", lambda: nc.scalar.copy(out=self.yT[:, j, tg * TG:(tg + 1) * TG], in_=bank[:]), [bank], [self.yT])
                self.acc_stats(bank[:], bank, j, tg)

            self.proj(self.w_mo, 8, 8, (self.aT if sg == 0 else self.oB), evac_y)
            if sg == 0:
                for k in range(8):
                    c.gather(self.oB[:, k, :], self.io["o_all2d"], self.oidx[:, 8 + k:8 + k + 1], self.oB, [self.io["o_all_buf"], self.oidx])
            self.norm_add(0)
            self.norm_to_a(self.gp, 1, have_stats=True)
            pend = {}

            def evac_f1(j, tg, bank):
                cch, half = divmod(j, 2)
                if half == 0:
                    pend[tg] = bank
                    return
                b1 = pend.pop(tg)
                t = self.tmprot.next()
                c.op("act", lambda: nc.scalar.activation(out=t[:], in_=b1[:], func=AF.Silu), [b1], [t])
                c.op("dve", lambda: nc.vector.tensor_tensor(out=self.gT[:, cch, tg * TG:(tg + 1) * TG], in0=t[:], in1=bank[:], op=ALU.mult),
                     [t, bank], [self.gT])

            self.proj(self.w_f1, 44, 8, self.aT, evac_f1)
            self.proj(self.w_f2, 8, 22, self.gT, evac_y)
            self.norm_add(2)
            for k in range(8):
                c.op("dve", lambda: nc.vector.tensor_copy(out=self.aT[:, k, :], in_=self.xT[:, k, :]), [self.xT], [self.aT])

            def evac_gate(j, tg, bank):
                c.op("act", lambda: nc.scalar.activation(out=self.yT[:, j, tg * TG:(tg + 1) * TG], in_=bank[:], func=AF.Sigmoid,
                                                         bias=self.gp[:, 4, j:j + 1]), [bank, self.gp], [self.yT])

            self.proj(self.w_pg, 8, 8, self.aT, evac_gate)

            def evac_pp(j, tg, bank):
                sl = slice(tg * TG, (tg + 1) * TG)
                c.op("dve", lambda: nc.vector.tensor_tensor(out=self.yT[:, j, sl], in0=self.yT[:, j, sl], in1=bank[:], op=ALU.mult),
                     [self.yT, bank], [self.yT])
                self.acc_stats(self.yT[:, j, sl], self.yT, j, tg)

            self.proj(self.w_pp, 8, 2, self.pT, evac_pp)
            self.norm_add(3)
        t = c.dma("sp", self.x_out[:, t0:t0 + SG].rearrange("(k p) t -> p k t", p=128), self.xT[:], reads=[self.xT], track=self.xT)
        if self.io.get("final"):
            c.out_tickets.append(t)
        if not self.pre:
            return
        self.norm_to_a(self.gpre, have_stats=self.post)
        pr = self.pr_out
        if self.pre == "hnorm":
            for a in range(4):
                tk = c.dma("sp", pr[a, :, t0:t0 + SG].rearrange("(h p) t -> p h t", p=128), self.aT[:, 2 * a:2 * a + 2, :], reads=[self.aT], track=self.aT)
            if sg == TOK // SG - 1:
                c._wait("pool", [tk])
                for k in range(4):
                    self.io["ag_pr"](k)
            return

        def store(st, jo, tg):
            typ, hc = divmod(jo, 8)
            g, half = divmod(hc, 2)
            if self.pre == "rglru":
                dst = pr[g * 4 + typ * 2 + half, :, t0 + tg * TG:t0 + (tg + 1) * TG]
            else:
                dst = pr[g * 3 + typ, half * 128:(half + 1) * 128, t0 + tg * TG:t0 + (tg + 1) * TG]
            tk = c.dma("sp", dst, st[:], reads=[st], track=st)
            if sg == TOK // SG - 1:
                piece = (g * 4 + typ * 2 + half) if self.pre == "rglru" else (g * 3 + typ)
                lst = self.piece_tix.setdefault(piece, [])
                lst.append(tk)
                if len(lst) == (2 if self.pre == "rglru" else 4):
                    self.ag_queue.append((piece, lst))
                    while len(self.ag_queue) > 2:
                        pk, l = self.ag_queue.pop(0)
                        c._wait("pool", l); self.io["ag_pr"](pk)

        if self.pre == "rglru":
            def evac(j, tg, bank):
                st = self.strot.next()
                c.op("act", lambda: nc.scalar.copy(out=st[:], in_=bank[:]), [bank], [st])
                store(st, j, tg)
        elif self.pre == "plain":
            def evac(j, tg, bank):
                st = self.strot.next()
                c.op("act", lambda: nc.scalar.activation(out=st[:], in_=bank[:], func=AF.Copy, scale=(0.125 if j < 8 else 1.0)), [bank], [st])
                store(st, j, tg)
        else:
            pend = {}

            def evac(j, tg, bank):
                if j >= 32:
                    st = self.strot.next()
                    c.op("act", lambda: nc.scalar.copy(out=st[:], in_=bank[:]), [bank], [st])
                    store(st, j - 16, tg)
                    return
                jj, var = divmod(j, 2)
                if var == 0:
                    pend[tg] = bank
                    return
                b1 = pend.pop(tg)
                sc = 0.125 if jj < 8 else 1.0
                tsl = slice(t0 + tg * TG, t0 + (tg + 1) * TG)
                t1 = self.tmprot.next(); t2 = self.tmprot.next(); st = self.strot.next()
                c.op("dve", lambda: nc.vector.scalar_tensor_tensor(out=t1[:], in0=b1[:], scalar=sc, in1=self.cosT[:, tsl], op0=ALU.mult, op1=ALU.mult),
                     [b1, self.cosT], [t1])
                c.op("dve", lambda: nc.vector.scalar_tensor_tensor(out=t2[:], in0=bank[:], scalar=sc, in1=self.sinT[:, tsl], op0=ALU.mult, op1=ALU.mult),
                     [bank, self.sinT], [t2])
                c.op("dve", lambda: nc.vector.tensor_tensor(out=st[:], in0=t1[:], in1=t2[:], op=ALU.add), [t1, t2], [st])
                store(st, jj, tg)
        self.proj(self.w_pre, self.npre, 8, self.aT, evac)
        while self.ag_queue:
            pk, l = self.ag_queue.pop(0)
            c._wait("pool", l); self.io["ag_pr"](pk)


def emit_dense(c, post, pre, io):
    Dense(c, post, pre, io).build()
    c.end_phase()


def emit_rglru(c, io):
    nc = c.nc
    CH = 2048
    cw = io["cw"]; gw = io["gw"]; o_loc = io["o_loc"].rearrange("a p s -> (a p) s")
    V = nc.vector; G = nc.gpsimd; A = nc.scalar
    cws = c.sb((128, 2, 8), F32, "cws"); c.dma("sp", cws[:], cw, writes=[cws], track=cws)
    gws = c.sb((128, 4, 128), BF16, "gws")
    c.dma("pool", gws[:], gw.rearrange("a b p m -> p (a b) m"), writes=[gws], track=gws)
    ridx = c.sb((128, 16), mybir.dt.uint32, "ridx"); c.dma("sp", ridx[:], io["idx_r"], writes=[ridx], track=ridx)
    wg = c.sb((128, 4, 8 * 128), BF16, "wg")
    c.dma("pool", wg[:], io["w_rg"].rearrange("c p k m -> p c (k m)"), writes=[wg], track=wg)
    hall = io["h_all"]; hallb = io["h_all_buf"]
    hcr = Rot([c.sb((128, 8, CH), BF16, f"hc{i}") for i in range(2)])
    cs = c.sb((128, 2), F32, "cs")
    onec = c.sb((128, 1), F32, "onec")
    c.op("pool", lambda: G.memset(onec[:], 1.0), [], [onec])
    for ct in range(2):
        c.op("act", lambda: A.activation(out=cs[:, ct:ct + 1], in_=cws[:, ct, 7:8], func=AF.Exp, scale=-1.0), [cws], [cs])
        c.op("act", lambda: A.activation(out=cs[:, ct:ct + 1], in_=cs[:, ct:ct + 1], func=AF.Ln, bias=onec[:, 0:1]), [cs, onec], [cs])
    c.op("dve", lambda: V.tensor_scalar(out=cs[:], in0=cs[:], scalar1=-8.0, scalar2=None, op0=ALU.mult), [cs], [cs])
    xfull = c.sb((128, S + 3), F32, "xfull")
    yr = Rot([c.sb((128, CH), F32, f"y{i}") for i in range(2)])
    xc = c.sb((128, CH), F32, "xc"); xcb = c.sb((128, CH), BF16, "xcb")
    ra = c.sb((128, CH), F32, "ra"); mm = c.sb((128, CH), F32, "mm"); ib = c.sb((128, CH), F32, "ib")
    hh = c.sb((128, CH), F32, "hh"); tt = c.sb((128, CH), F32, "tt")
    orot = Rot([c.sb((128, CH), BF16, f"o{i}") for i in range(2)])
    carry = c.sb((128, 1), F32, "carry")
    prot = Rot([c.ps(name=f"pb{i}") for i in range(4)])
    for ct in range(2):
        rows = slice(ct * 128, (ct + 1) * 128)
        otix = []
        c.op("dve", lambda: V.memset(xfull[:, 0:3], 0.0), [], [xfull])
        for tc in range(S // CH):
            t0 = tc * CH
            y = yr.next()
            hc = hcr.next()
            for a in range(4):
                c.dma("sp", hc[:, 2 * a:2 * a + 2, :], hall[a, tc * 256:(tc + 1) * 256, :].rearrange("(h p) t -> p h t", p=128),
                      reads=[hallb], writes=[hc], track=hc)
            for typ in range(2):
                for sb_ in range(CH // 512):
                    bank = prot.next(); sl = slice(sb_ * 512, (sb_ + 1) * 512)
                    for k in range(8):
                        c.op("pe", lambda: nc.tensor.matmul(bank[:], lhsT=wg[:, typ * 2 + ct, k * 128:(k + 1) * 128], rhs=hc[:, k, sl],
                                                            start=(k == 0), stop=(k == 7)), [wg, hc], [bank])
                    if typ == 0:
                        c.op("act", lambda: A.copy(out=y[:, sl], in_=bank[:]), [bank], [y])
                    else:
                        c.op("act", lambda: A.copy(out=xfull[:, 3 + t0 + sb_ * 512:3 + t0 + (sb_ + 1) * 512], in_=bank[:]), [bank], [xfull])
            c.op("dve", lambda: V.tensor_scalar(out=xc[:], in0=xfull[:, t0:t0 + CH], scalar1=cws[:, ct, 0:1], scalar2=cws[:, ct, 4:5], op0=ALU.mult, op1=ALU.add), [xfull, cws], [xc])
            for tap in range(1, 4):
                c.op("dve", lambda: V.scalar_tensor_tensor(out=xc[:], in0=xfull[:, t0 + tap:t0 + tap + CH], scalar=cws[:, ct, tap:tap + 1], in1=xc[:], op0=ALU.mult, op1=ALU.add), [xfull, cws, xc], [xc])
            c.op("act", lambda: A.copy(out=xcb[:], in_=xc[:]), [xc], [xcb])
            for gi, dst in ((0, ra), (1, ib)):
                for sb_ in range(CH // 512):
                    bank = prot.next(); sl = slice(sb_ * 512, (sb_ + 1) * 512)
                    c.op("pe", lambda: nc.tensor.matmul(bank[:], lhsT=gws[:, ct * 2 + gi, :], rhs=xcb[:, sl], start=True, stop=True), [gws, xcb], [bank])
                    c.op("act", lambda: A.activation(out=dst[:, sl], in_=bank[:], func=AF.Sigmoid, bias=cws[:, ct, 5 + gi:6 + gi]), [bank, cws], [dst])
            c.op("act", lambda: A.activation(out=ra[:], in_=ra[:], func=AF.Exp, scale=cs[:, ct:ct + 1]), [ra, cs], [ra])
            c.op("pool", lambda: G.tensor_tensor(out=mm[:], in0=ra[:], in1=ra[:], op=ALU.mult), [ra], [mm])
            c.op("dve", lambda: V.tensor_scalar(out=mm[:], in0=mm[:], scalar1=-1.0, scalar2=1.0, op0=ALU.mult, op1=ALU.add), [mm], [mm])
            c.op("act", lambda: A.activation(out=mm[:], in_=mm[:], func=AF.Sqrt), [mm], [mm])
            c.op("dve", lambda: V.tensor_tensor(out=ib[:], in0=ib[:], in1=xc[:], op=ALU.mult), [ib, xc], [ib])
            c.op("dve", lambda: V.tensor_tensor(out=ib[:], in0=ib[:], in1=mm[:], op=ALU.mult), [ib, mm], [ib])
            if tc > 0:
                c.op("pool", lambda: G.tensor_tensor(out=carry[:], in0=ra[:, 0:1], in1=hh[:, CH - 1:CH], op=ALU.mult), [ra, hh], [carry])
                c.op("pool", lambda: G.tensor_tensor(out=ib[:, 0:1], in0=ib[:, 0:1], in1=carry[:], op=ALU.add), [ib, carry], [ib])
            c.op("dve", lambda: V.tensor_tensor_scan(out=hh[:], data0=ra[:], data1=ib[:], initial=0.0, op0=ALU.mult, op1=ALU.add),
                 [ra, ib], [hh])
            c.op("pool", lambda: G.tensor_tensor(out=tt[:], in0=y[:], in1=y[:], op=ALU.mult), [y], [tt])
            c.op("dve", lambda: V.tensor_scalar(out=tt[:], in0=tt[:], scalar1=0.044715, scalar2=1.0, op0=ALU.mult, op1=ALU.add), [tt], [tt])
            c.op("dve", lambda: V.tensor_tensor(out=tt[:], in0=tt[:], in1=y[:], op=ALU.mult), [tt, y], [tt])
            c.op("act", lambda: A.activation(out=tt[:], in_=tt[:], func=AF.Sigmoid, scale=2.0 * math.sqrt(2.0 / math.pi)), [tt], [tt])
            c.op("pool", lambda: G.tensor_tensor(out=tt[:], in0=tt[:], in1=y[:], op=ALU.mult), [tt, y], [tt])
            ob = orot.next()
            c.op("dve", lambda: V.tensor_tensor(out=ob[:], in0=hh[:], in1=tt[:], op=ALU.mult), [hh, tt], [ob])
            otix.append(c.dma("sp", o_loc[rows, t0:t0 + CH], ob[:], reads=[ob], track=ob))
        c._wait("pool", otix)
        io["ag_o"](2 * ct); io["ag_o"](2 * ct + 1)
    c.end_phase()


def attn_common(c, io, vcols, dil=False):
    nc = c.nc
    src = io["pr_all2d"]; srcb = io["pr_all_buf"]
    aidx = c.sb((128, 24), mybir.dt.uint32, "aidx"); c.dma("sp", aidx[:], io["idx_a"], writes=[aidx], track=aidx)
    ident = c.sb((128, 128), BF16, "ident"); c.dma("sp", ident[:], io["ident_bf"], writes=[ident], track=ident)
    qs = c.sb((128, 2, S), BF16, "qs"); ks = c.sb((128, 4, S), BF16, "kz"); vs = c.sb((128, S // 128, vcols + (0 if dil else 64)), BF16, "vs")
    vtr = Rot([c.sb((128, 2048), BF16, f"vT{i}") for i in range(2)])
    for h in range(4):
        c.op("pool" if h % 2 else "dve", lambda: (nc.gpsimd if h % 2 else nc.vector).memset(ks[:, h, :], 0.0), [], [ks])
    for s_ in range(2):
        for j in range(4):
            c.gather(qs[:, s_, j * 2048:(j + 1) * 2048], src, aidx[:, s_ * 4 + j:s_ * 4 + j + 1], qs, [srcb, aidx])
            kst = vtr.next()
            c.gather(kst[:], src, aidx[:, 8 + s_ * 4 + j:8 + s_ * 4 + j + 1], kst, [srcb, aidx])
            c.op("dve", lambda: nc.vector.tensor_copy(out=ks[0:64, s_, j * 2048:(j + 1) * 2048], in_=kst[0:64, :]), [kst], [ks])
            c.op("pool", lambda: nc.gpsimd.tensor_copy(out=ks[64:128, 2 + s_, j * 2048:(j + 1) * 2048], in_=kst[64:128, :]), [kst], [ks])
    if dil:
        c.op("dve", lambda: nc.vector.memset(vs[:], 1.0), [], [vs])
    tpr = Rot([c.ps((128, 1024), BF16, name="tp")])
    for vc in range(2):
        for j in range(4):
            vT = vtr.next()
            c.gather(vT[:], src, aidx[:, 16 + vc * 4 + j:16 + vc * 4 + j + 1], vT, [srcb, aidx])
            for q4 in range(4):
                tp = tpr.next()
                for b4 in range(4):
                    blk = q4 * 4 + b4
                    c.op("pe", lambda: nc.tensor.transpose(out=tp[:, b4 * 128:(b4 + 1) * 128], in_=vT[:, blk * 128:(blk + 1) * 128], identity=ident[:]),
                         [vT, ident], [tp])
                b0 = j * 16 + q4 * 4
                if dil:
                    dst = vs[:, b0:b0 + 4, vc * 130:(vc + 1) * 130].rearrange("p b (h c) -> p b h c", c=65)[:, :, :, 0:64]
                    srcp = tp[:, 0:512].rearrange("p (b h c) -> p b h c", b=4, h=2)
                else:
                    dst = vs[:, b0:b0 + 4, vc * 128:(vc + 1) * 128]
                    srcp = tp[:, 0:512].rearrange("p (b c) -> p b c", b=4)
                c.op("act", lambda: nc.scalar.copy(out=dst, in_=srcp), [tp], [vs])
    return qs, ks, vs


def head_ap(t, h):
    if t.ap.shape[1] == 4:
        return lambda sl: t[:, h, sl]
    return lambda sl: t[:, h % 2, sl]


def emit_sb(c, io):
    nc = c.nc
    V = nc.vector; G = nc.gpsimd; A = nc.scalar; PE = nc.tensor
    qs, ks, vs = attn_common(c, io, 256)
    oT = io["o_loc"]
    cs_ = c.sb((128, 18, 128), BF16, "cst_sb"); c.dma("sp", cs_[:], io["sb_cst"], writes=[cs_], track=cs_)
    tri = cs_[:, 0, :]; ones = cs_[:, 1, :]
    onec = c.sb((128, 1), F32, "onec")
    c.op("pool", lambda: G.memset(onec[:], 1.0), [], [onec])
    zr = Rot([c.ps(name=f"z{i}") for i in range(2)])
    br = Rot([c.ps(name=f"b{i}") for i in range(2)])
    cr = Rot([c.ps(name=f"c{i}") for i in range(2)])
    ob = c.ps(name="o0")
    er = Rot([c.sb((128, 512), F32, f"e{i}") for i in range(2)])
    spr = Rot([c.sb((128, 512), BF16, f"sp{i}") for i in range(3)])
    t1r = Rot([c.sb((128, 512), F32, f"t1{i}") for i in range(2)])
    atr = Rot([c.sb((128, 512), BF16, f"at{i}") for i in range(3)])
    R = c.sb((128, 512), F32, "R")
    ost = Rot([c.sb((64, 512), BF16, f"ost{i}") for i in range(2)])
    items = []
    otix = {}
    for h in range(4):
        qh = head_ap(qs, h); kh = head_ap(ks, h)
        for qg in range(S // 512):
            qsl = slice(qg * 512, (qg + 1) * 512)
            nkb = 4 * qg + 4
            for i, kb in enumerate(range(nkb - 1, -1, -1)):
                def mk_item(h=h, qh=qh, kh=kh, qg=qg, qsl=qsl, nkb=nkb, i=i, kb=kb):
                    diag = kb >= 4 * qg
                    ksl = slice(kb * 128, (kb + 1) * 128)
                    zb = zr.next(); e = er.next(); sp = spr.next(); bb = br.next(); cb = cr.next(); t1 = t1r.next(); at = atr.next()
                    mk = cs_[:, 2 + 4 * (kb - 4 * qg):6 + 4 * (kb - 4 * qg), :].rearrange("p a b -> p (a b)") if diag else None
                    last = i == nkb - 1

                    def s1():
                        c.op("pe", lambda: PE.matmul(zb[:], lhsT=kh(ksl), rhs=qh(qsl), start=True, stop=True), [ks, qs], [zb])
                        c.op("act", lambda: A.activation(out=e[:], in_=zb[:], func=AF.Exp), [zb], [e])
                        c.op("act", lambda: A.activation(out=sp[:], in_=e[:], func=AF.Ln, bias=onec[:, 0:1]), [e, onec], [sp])
                        if diag:
                            c.op("pool", lambda: G.tensor_tensor(out=sp[:], in0=sp[:], in1=mk, op=ALU.mult), [sp, cs_], [sp])

                    def s2():
                        c.op("pe", lambda: PE.matmul(bb[:], lhsT=kh(ksl), rhs=qh(qsl), start=True, stop=False), [ks, qs], [bb])
                        c.op("pe", lambda: PE.matmul(bb[:], lhsT=tri, rhs=sp[:], start=False, stop=True), [cs_, sp], [bb])
                        if not last:
                            c.op("pe", lambda: PE.matmul(cb[:], lhsT=ones, rhs=sp[:], start=True, stop=True), [cs_, sp], [cb])

                    def s3a():
                        if i == 0:
                            c.op("pool", lambda: G.memset(R[:], 0.0), [], [R])
                        c.op("dve", lambda: V.tensor_tensor(out=t1[:], in0=bb[:], in1=R[:], op=ALU.subtract), [bb, R], [t1])
                        if not last:
                            c.op("dve", lambda: V.tensor_tensor(out=R[:], in0=R[:], in1=cb[:], op=ALU.add), [R, cb], [R])
                        c.op("act", lambda: A.activation(out=at[:], in_=t1[:], func=AF.Exp), [t1], [at])
                        if diag:
                            c.op("pool", lambda: G.tensor_tensor(out=at[:], in0=at[:], in1=mk, op=ALU.mult), [at, cs_], [at])

                    def s3b():
                        c.op("pe", lambda: PE.matmul(ob[:], lhsT=vs[:, kb, h * 64:h * 64 + 128], rhs=at[:], start=(i == 0), stop=last),
                             [vs, at], [ob])
                        if last:
                            st = ost.next()
                            c.op("act", lambda: A.copy(out=st[:], in_=ob[0:64, :]), [ob], [st])
                            otix.setdefault(h, []).append(c.dma("sp", oT[h, :, qsl], st[:], reads=[st], track=st))
                            if qg == S // 512 - 1:
                                c._wait("pool", otix[h]); io["ag_o"](h)
                    return (s3a, s1, s2, s3b)
                items.append(mk_item())
    run_pipeline(items, (2, 0, 1, 2))
    c.end_phase()


def emit_softmax_attn(c, io, kind, lam_init=0.0):
    nc = c.nc
    V = nc.vector; G = nc.gpsimd; A = nc.scalar; PE = nc.tensor
    dil = kind == "dil"
    vcols = 260 if dil else 256
    qs, ks, vs = attn_common(c, io, vcols, dil)
    NM = 20 if dil else 4
    mk = c.sb((128, NM, 512), BF16, "mk"); c.dma("sp", mk[:], io["dil_masks" if dil else "diff_masks"], writes=[mk], track=mk)
    onesf = c.sb((128, 128), F32, "onesf"); c.dma("sp", onesf[:], io["ones_f"], writes=[onesf], track=onesf)
    zr = Rot([c.ps(name=f"z{i}") for i in range(2)])
    atr = Rot([c.sb((128, 512), BF16, f"at{i}") for i in range(4)])
    if dil:
        oT = io["o_loc"]
        orr = Rot([c.ps(name=f"o{i}") for i in range(2)])
        bcr = Rot([c.ps(name=f"bc{i}") for i in range(2)])
        rl = c.sb((128, 512), F32, "rl"); bcs = c.sb((64, 512), F32, "bcs")
        ost = Rot([c.sb((64, 512), BF16, f"ost{i}") for i in range(2)])
        items = []
        otix = {}
        for h in range(4):
            qh = head_ap(qs, h); kh = head_ap(ks, h)
            for qg in range(S // 512):
                qsl = slice(qg * 512, (qg + 1) * 512)
                ob = orr.next()
                kbs = list(range(max(0, 4 * qg - 16), 4 * qg + 4))
                for i, kb in enumerate(kbs):
                    def mk_item(h=h, qh=qh, kh=kh, qg=qg, qsl=qsl, ob=ob, kbs=kbs, i=i, kb=kb):
                        zb = zr.next(); at = atr.next()
                        mi = (512 * qg - 128 * kb + 384) // 128
                        last = i == len(kbs) - 1

                        def s1():
                            c.op("pe", lambda: PE.matmul(zb[:], lhsT=kh(slice(kb * 128, (kb + 1) * 128)), rhs=qh(qsl), start=True, stop=True), [ks, qs], [zb])
                            c.op("act", lambda: A.activation(out=at[:], in_=zb[:], func=AF.Exp), [zb], [at])
                            c.op("dve", lambda: V.tensor_tensor(out=at[:], in0=at[:], in1=mk[:, mi, :], op=ALU.mult), [at, mk], [at])

                        def s2():
                            c.op("pe", lambda: PE.matmul(ob[0:65, :], lhsT=vs[:, kb, h * 65:(h + 1) * 65], rhs=at[:], start=(i == 0), stop=last),
                                 [vs, at], [ob])
                            if last:
                                bc = bcr.next(); st = ost.next()
                                c.op("dve", lambda: V.reciprocal(out=rl[64:65, :], in_=ob[64:65, :]), [ob], [rl])
                                c.op("pe", lambda: PE.matmul(bc[0:64, :], lhsT=onesf[64:65, 0:64], rhs=rl[64:65, :], start=True, stop=True), [onesf, rl], [bc])
                                c.op("act", lambda: A.copy(out=bcs[:], in_=bc[0:64, :]), [bc], [bcs])
                                c.op("dve", lambda: V.tensor_tensor(out=st[:], in0=ob[0:64, :], in1=bcs[:], op=ALU.mult), [ob, bcs], [st])
                                otix.setdefault(h, []).append(c.dma("sp", oT[h, :, qsl], st[:], reads=[st], track=st))
                                if qg == S // 512 - 1:
                                    c._wait("pool", otix[h]); io["ag_o"](h)
                        return (s1, s2)
                    items.append(mk_item())
        run_pipeline(items, (0, 2))
        c.end_phase()
        return
    oT = io["o_loc"].rearrange("(d a) p s -> d (a p) s", a=2)
    onesb = c.sb((128, 128), BF16, "onesb"); c.dma("sp", onesb[:], io["ones_bf"], writes=[onesb], track=onesb)
    lam = c.sb((1, 4, 64), F32, "lam_sb"); c.dma("sp", lam[:], io["lam"], writes=[lam], track=lam)
    sub = c.sb((128, 1), F32, "sub_sb"); c.dma("sp", sub[:], io["subln"], writes=[sub], track=sub)
    prod = c.sb((1, 2, 64), F32, "prod"); dots = c.sb((1, 2), F32, "dots"); ee = c.sb((1, 2), F32, "ee")
    dl = c.sb((1, 2), F32, "dl"); nl = c.sb((1, 2), F32, "nl"); nlam = c.sb((128, 2), F32, "nlam")
    epsb = c.sb((128, 1), F32, "epsb")
    c.op("pool", lambda: G.memset(epsb[:], 1e-5), [], [epsb])
    for m in range(2):
        c.op("dve", lambda: V.tensor_tensor(out=prod[:, m, :], in0=lam[:, 2 * m, :], in1=lam[:, 2 * m + 1, :], op=ALU.mult), [lam], [prod])
    c.op("pool", lambda: G.memset(dots[:], 0.0), [], [dots])
    c.op("dve", lambda: V.tensor_reduce(out=dots[:], in_=prod[:], op=ALU.add, axis=mybir.AxisListType.X), [prod, dots], [dots])
    c.op("act", lambda: A.activation(out=ee[:], in_=dots[:], func=AF.Exp), [dots], [ee])
    for j in range(2):
        c.op("pool", lambda: G.tensor_tensor(out=dl[:, j:j + 1], in0=ee[:, 1:2], in1=ee[:, 0:1], op=ALU.subtract), [ee], [dl])
    c.op("dve", lambda: V.tensor_scalar(out=nl[:], in0=dl[:], scalar1=-lam_init, scalar2=None, op0=ALU.add), [dl], [nl])
    zb = zr.next()
    c.op("pe", lambda: PE.matmul(zb[:, 0:2], lhsT=onesf[0:1, :], rhs=nl[0:1, :], start=True, stop=True), [onesf, nl], [zb])
    c.op("act", lambda: A.copy(out=nlam[:], in_=zb[:, 0:2]), [zb], [nlam])
    c.op("pool", lambda: G.tensor_scalar(out=sub[:], in0=sub[:], scalar1=(1.0 - lam_init), scalar2=None, op0=ALU.mult), [sub], [sub])
    ob = [c.ps(name=f"o{i}") for i in range(2)]
    lb = [c.ps(name=f"l{i}") for i in range(2)]
    bc = lb[0]; ssb = c.ps(name="ss")
    rl = c.sb((1, 2, 512), F32, "rl"); rbs = [c.sb((128, 512), F32, f"rbs{i}") for i in range(2)]
    t1 = c.sb((128, 512), F32, "t1"); t2 = c.sb((128, 512), F32, "t2"); sq = c.sb((128, 512), BF16, "sq")
    rstd = c.sb((128, 512), F32, "rstd")
    ost = Rot([c.sb((128, 512), BF16, f"ost{i}") for i in range(2)])
    items = []
    for dh in range(2):
        for qg in range(S // 512):
            qsl = slice(qg * 512, (qg + 1) * 512)
            nkb = 4 * qg + 4
            for kb in range(nkb):
                for m in range(2):
                    def mk_item(dh=dh, qg=qg, qsl=qsl, nkb=nkb, kb=kb, m=m):
                        h = 2 * dh + m
                        qh = head_ap(qs, h); kh = head_ap(ks, h)
                        zb = zr.next(); at = atr.next()

                        def s1():
                            c.op("pe", lambda: PE.matmul(zb[:], lhsT=kh(slice(kb * 128, (kb + 1) * 128)), rhs=qh(qsl), start=True, stop=True), [ks, qs], [zb])
                            c.op("act", lambda: A.activation(out=at[:], in_=zb[:], func=AF.Exp), [zb], [at])
                            if kb >= 4 * qg:
                                c.op("dve", lambda: V.tensor_tensor(out=at[:], in0=at[:], in1=mk[:, kb - 4 * qg, :], op=ALU.mult), [at, mk], [at])

                        def s2():
                            c.op("pe", lambda: PE.matmul(ob[m][:], lhsT=vs[:, kb, dh * 128:(dh + 1) * 128], rhs=at[:], start=(kb == 0), stop=(kb == nkb - 1)),
                                 [vs, at], [ob[m]])
                            c.op("pe", lambda: PE.matmul(lb[m][:], lhsT=onesb[:], rhs=at[:], start=(kb == 0), stop=(kb == nkb - 1)),
                                 [onesb, at], [lb[m]])
                            if kb == nkb - 1 and m == 1:
                                epilogue(dh, qsl)
                        return (s1, s2)
                    items.append(mk_item())

    def epilogue(dh, qsl):
        for m in range(2):
            c.op("dve", lambda: V.reciprocal(out=rbs[m][:], in_=lb[m][:]), [lb[m]], [rbs[m]])
        c.op("dve", lambda: V.tensor_tensor(out=t1[:], in0=ob[0][:], in1=rbs[0][:], op=ALU.mult), [ob[0], rbs[0]], [t1])
        c.op("dve", lambda: V.tensor_tensor(out=t2[:], in0=ob[1][:], in1=rbs[1][:], op=ALU.mult), [ob[1], rbs[1]], [t2])
        c.op("pool", lambda: G.tensor_scalar(out=t2[:], in0=t2[:], scalar1=nlam[:, 0:1], scalar2=None, op0=ALU.mult), [t2, nlam], [t2])
        c.op("dve", lambda: V.tensor_tensor(out=t1[:], in0=t1[:], in1=t2[:], op=ALU.add), [t1, t2], [t1])
        c.op("act", lambda: A.activation(out=sq[:], in_=t1[:], func=AF.Square), [t1], [sq])
        c.op("pe", lambda: PE.matmul(ssb[:], lhsT=onesb[:], rhs=sq[:], start=True, stop=True), [onesb, sq], [ssb])
        c.op("act", lambda: A.activation(out=rstd[:], in_=ssb[:], func=AF.Sqrt, scale=1.0 / 128, bias=epsb[:, 0:1]), [ssb, epsb], [rstd])
        c.op("pool", lambda: G.tensor_copy(out=rstd2[:], in_=rstd[:]), [rstd], [rstd2])
        c.op("dve", lambda: V.reciprocal(out=rstd2[:], in_=rstd2[:]), [rstd2], [rstd2])
        st = ost.next()
        c.op("dve", lambda: V.scalar_tensor_tensor(out=st[:], in0=t1[:], scalar=sub[:, 0:1], in1=rstd2[:], op0=ALU.mult, op1=ALU.mult), [t1, sub, rstd2], [st])
        otix.setdefault(dh, []).append(c.dma("sp", oT[dh, :, qsl], st[:], reads=[st], track=st))
        if len(otix[dh]) == S // 512:
            c._wait("pool", otix[dh]); io["ag_o"](2 * dh); io["ag_o"](2 * dh + 1)

    otix = {}
    rstd2 = c.sb((128, 512), F32, "rstd2")
    bcb = [lb[0], lb[1]]
    run_pipeline(items, (0, 2))
    c.end_phase()


PRE_KINDS = ["rglru", "rope", "plain", "rope"]
LAM_INIT3 = 0.8 - 0.6 * math.exp(-0.3 * 3)
U32 = mybir.dt.uint32


def build_fused(stop=999):
    c = Ctx()
    step = [0]

    def done():
        step[0] += 1
        return step[0] >= stop

    EI = lambda n, sh, dt: c.dram(n, sh, dt, "ExternalInput")
    g = {}
    g["xT"] = EI("xT", (D, TOK), F32)
    g["xT_out"] = c.dram("xT_out", (D, TOK), F32, "ExternalOutput")
    g["ones_bf"] = EI("ones_bf", (128, 128), BF16); g["ones_f"] = EI("ones_f", (128, 128), F32)
    g["ident_bf"] = EI("ident_bf", (128, 128), BF16)
    g["pos"] = EI("pos", (1, TOK), I32); g["invf"] = EI("invf", (128, 2), F32)
    g["idx_o"] = EI("idx_o", (128, 16), U32); g["idx_a"] = EI("idx_a", (128, 24), U32); g["idx_r"] = EI("idx_r", (128, 16), U32)
    g["cw"] = EI("cw", (128, 2, 8), F32); g["gw"] = EI("gw", (2, 2, 128, 128), F32)
    g["dil_masks"] = EI("dil_masks", (128, 20, 512), BF16); g["diff_masks"] = EI("diff_masks", (128, 4, 512), BF16)
    g["sb_cst"] = EI("sb_cst", (128, 18, 128), BF16)
    g["lam"] = EI("lam", (1, 4, 64), F32); g["subln"] = EI("subln", (128, 1), F32)
    L = []
    for l in range(DEPTH):
        npre = {"rglru": 16, "rope": 40, "plain": 24}[PRE_KINDS[l]]
        L.append(dict(w_mo=EI(f"w_mo{l}", (8, 128, 8, 128), F32), w_f1=EI(f"w_f1{l}", (44, 128, 8, 128), F32),
                      w_f2=EI(f"w_f2{l}", (8, 128, 22, 128), F32), w_pg=EI(f"w_pg{l}", (8, 128, 8, 128), F32),
                      w_pp=EI(f"w_pp{l}", (8, 128, 2, 128), F32), p_in=EI(f"pT{l}", (PLE, TOK), F32),
                      gains_post=EI(f"gains_post{l}", (128, 5, 8), F32), gain_pre=EI(f"gain_pre{l}", (128, 8), F32),
                      w_pre=EI(f"w_pre{l}", (npre, 128, 8, 128), F32)))
    x_scr = c.scratch("x_scr", (D, TOK), F32)
    pr_loc_r = c.scratch("pr_loc_r", (16, 128, TOK), F32); pr_all_r = c.scratch("pr_all_r", (16, 4 * 128, TOK), F32)
    pr_loc_a = c.scratch("pr_loc_a", (12, 256, TOK), BF16); pr_all_a = c.scratch("pr_all_a", (12, 4 * 256, TOK), BF16)
    o_loc = c.scratch("o_loc", (4, 64, S), BF16); o_all = c.scratch("o_all", (4, 4 * 64, S), BF16)
    rope_scr = c.scratch("rope_scr", (2, 128, TOK), F32)
    pr_loc_h = c.scratch("pr_loc_h", (4, 256, TOK), BF16); h_all = c.scratch("h_all", (4, 4 * 256, TOK), BF16)
    h_all_buf = Buf(h_all)
    g["w_rg"] = EI("w_rg", (4, 128, 8, 128), F32)
    pr_all_r_buf = Buf(pr_all_r); pr_all_a_buf = Buf(pr_all_a); o_all_buf = Buf(o_all)
    o_all2d = o_all.rearrange("a r (c t) -> (a r c) t", t=SG)
    for i in range(DEPTH + 1):
        post = i > 0; pre = PRE_KINDS[i] if i < DEPTH else None
        if pre == "rglru":
            pre = "hnorm"
        io = dict(x_in=(g["xT"] if i == 0 else x_scr), x_out=(g["xT_out"] if i == DEPTH else x_scr), ones_bf=g["ones_bf"],
                  final=(i == DEPTH), idx_o=g["idx_o"], o_all2d=o_all2d, o_all_buf=o_all_buf,
                  rope_scr=rope_scr, rope_build=(i == 0))
        if post:
            io.update({k: L[i - 1][k] for k in ("w_mo", "w_f1", "w_f2", "w_pg", "w_pp", "p_in", "gains_post")})
        if pre:
            io.update(gain_pre=L[i]["gain_pre"], w_pre=L[i]["w_pre"], pos=g["pos"], invf=g["invf"],
                      pr_loc=(pr_loc_r if pre == "rglru" else pr_loc_a))
            if pre == "hnorm":
                io["pr_loc"] = pr_loc_h
                io["ag_pr"] = lambda k: c.all_gather(pr_loc_h[k], h_all[k], h_all_buf)
            else:
                io["ag_pr"] = lambda k: c.all_gather(pr_loc_a[k], pr_all_a[k], pr_all_a_buf)
        emit_dense(c, post, pre, io)
        if not pre or done():
            break
        if pre == "hnorm":
            mio = dict(h_all=h_all, h_all_buf=h_all_buf)
        else:
            mio = dict(pr_all2d=pr_all_a.rearrange("a r t -> (a r) t"), pr_all_buf=pr_all_a_buf)
        mio["ag_o"] = lambda k: c.all_gather(o_loc[k], o_all[k], o_all_buf)
        if done():
            break
        mio.update(g); mio["o_loc"] = o_loc
        if i == 0:
            emit_rglru(c, mio)
        elif i == 1:
            emit_softmax_attn(c, mio, "dil")
        elif i == 2:
            emit_sb(c, mio)
        else:
            emit_softmax_attn(c, mio, "diff", LAM_INIT3)
        if done():
            break
        if done():
            break
    c.barrier()
    c.finish(c.out_tickets)
    return c.nc


def _tok(cc):
    b, j = divmod(cc, 4)
    return b, slice(j * TOK, (j + 1) * TOK)


def _rope_cols():
    cols = []
    for jj in range(16):
        base = jj * 128
        e = np.arange(128)
        cols.append(np.arange(base, base + 128))
        cols.append(base + (e // 64) * 64 + ((e % 64) + 32) % 64)
    cols.append(np.arange(2048, 3072))
    return np.concatenate(cols)


def _mult(dist):
    m = np.zeros(dist.shape, np.float32)
    for dd in (1, 4, 16):
        m += ((dist >= 0) & (dist % dd == 0) & (dist <= 128 * dd)).astype(np.float32)
    return m


def kernel(**inp):
    inp = {k: np.asarray(v) for k, v in inp.items()}
    x = inp["x"]; p = inp["p"]; pos = inp["positions"].astype(np.int32)
    pp = np.arange(128)[:, None]; ff = np.arange(512)[None, :]
    sh = {"ones_bf": np.ones((128, 128), NPBF), "ones_f": np.ones((128, 128), np.float32),
          "ident_bf": np.eye(128, dtype=np.float32).astype(NPBF)}
    invf = np.zeros((128, 2), np.float32)
    e = np.arange(128) % 64
    invf[:, 0] = (np.float32(10000.0) ** (-(np.arange(0, 64, 2, dtype=np.float32)) / np.float32(64)))[e % 32]
    invf[:, 1] = np.where(e < 32, -1.0, 1.0)
    sh["invf"] = invf
    f1cols = np.concatenate([np.concatenate([np.arange(cc * 128, (cc + 1) * 128), FF + np.arange(cc * 128, (cc + 1) * 128)]) for cc in range(22)])
    rcols = _rope_cols()
    mix_out_w = [inp["a_w_out"][0], inp["b_w_out"][0], inp["c_w_out"][0], inp["d_w_out"][0]]
    pre_w = [w_chunks(inp["a_w_in"][0]), w_chunks(inp["b_w_qkv"][0], rcols), w_chunks(inp["c_w_qkv"][0]), w_chunks(inp["d_w_qkv"][0], rcols)]
    for l in range(DEPTH):
        sh[f"w_mo{l}"] = w_chunks(mix_out_w[l]); sh[f"w_f1{l}"] = w_chunks(inp["w_ffn_in"][l], f1cols)
        sh[f"w_f2{l}"] = w_chunks(inp["w_ffn_out"][l]); sh[f"w_pg{l}"] = w_chunks(inp["w_ple_gate"][l])
        sh[f"w_pp{l}"] = w_chunks(inp["w_ple_proj"][l])
        sh[f"gains_post{l}"] = np.ascontiguousarray(np.stack([col_vec(inp[n][l]) for n in
                                                             ("ln_mix_post", "ln_ffn_pre", "ln_ffn_post", "ln_ple", "b_ple_gate")], axis=1))
        sh[f"gain_pre{l}"] = col_vec(inp["ln_mix_pre"][l]); sh[f"w_pre{l}"] = pre_w[l]
    sh["dil_masks"] = np.ascontiguousarray(np.stack([_mult((128 * mi - 384) + ff - pp) for mi in range(20)], axis=1)).astype(NPBF)
    sh["diff_masks"] = np.ascontiguousarray(np.stack([((ff - pp - 128 * j) >= 0).astype(np.float32) for j in range(4)], axis=1)).astype(NPBF)
    cst = np.zeros((128, 18, 128), np.float32)
    cst[:, 0, :] = -1.0 * (pp >= np.arange(128)[None, :]); cst[:, 1, :] = 1.0
    for o in range(4):
        cst[:, 2 + 4 * o:6 + 4 * o, :] = ((128 * o + pp) < ff).astype(np.float32).reshape(128, 4, 128)
    sh["sb_cst"] = cst.astype(NPBF)
    sh["lam"] = np.ascontiguousarray(np.stack([inp[n][0] for n in ("d_lambda_q1", "d_lambda_k1", "d_lambda_q2", "d_lambda_k2")])[None])
    sh["subln"] = np.ascontiguousarray(inp["d_subln"][0].reshape(128, 1))
    pa = np.arange(128)
    maps = []
    for cc in range(NCORE):
        b, ts = _tok(cc)
        r = cc % 4
        m = dict(sh)
        m["xT"] = fm(x[b, ts]); m["pos"] = np.ascontiguousarray(pos[b:b + 1, ts])
        for l in range(DEPTH):
            m[f"pT{l}"] = fm(p[l, b, ts])
        cw = np.zeros((128, 2, 8), np.float32); gw = np.zeros((2, 2, 128, 128), np.float32)
        for ct in range(2):
            cs = slice(r * 256 + ct * 128, r * 256 + (ct + 1) * 128)
            cw[:, ct, 0:4] = inp["a_conv_w"][0][:, cs].T
            cw[:, ct, 4] = inp["a_conv_b"][0][cs]; cw[:, ct, 5] = inp["a_gate_r_b"][0][cs]
            cw[:, ct, 6] = inp["a_gate_i_b"][0][cs]; cw[:, ct, 7] = inp["a_lambda"][0][cs]
            for gi, nm in enumerate(("a_gate_r_w", "a_gate_i_w")):
                for hh in range(2):
                    n = r * 4 + ct * 2 + hh
                    gw[ct, gi, hh * 64:(hh + 1) * 64, hh * 64:(hh + 1) * 64] = inp[nm][0][n]
        m["cw"] = cw; m["gw"] = gw
        m["w_rg"] = np.ascontiguousarray(pre_w[0][[2 * r, 2 * r + 1, 8 + 2 * r, 8 + 2 * r + 1]])
        io_ = np.zeros((128, 16), np.uint32)
        for sg in range(2):
            for k in range(8):
                rl = (k % 2) * 128 + pa
                io_[:, sg * 8 + k] = ((((rl // 64) * 4 + k // 2) * 64 + rl % 64) * 4 + r) * 2 + sg
        m["idx_o"] = io_
        ia = np.zeros((128, 24), np.uint32)
        for s_ in range(2):
            for j in range(4):
                hl = 2 * (pa // 64) + s_
                ia[:, s_ * 4 + j] = ((r * 3 + 0) * 4 + j) * 256 + hl * 64 + pa % 64
                ia[:, 8 + s_ * 4 + j] = ((r * 3 + 1) * 4 + j) * 256 + hl * 64 + pa % 64
        for vc in range(2):
            for j in range(4):
                ia[:, 16 + vc * 4 + j] = ((r * 3 + 2) * 4 + j) * 256 + vc * 128 + pa
        m["idx_a"] = ia
        ir = np.zeros((128, 16), np.uint32)
        for tc_ in range(4):
            for j in range(4):
                ir[:, tc_ * 4 + j] = ((r * 4 + tc_) * 4 + j) * 128 + pa
        m["idx_r"] = ir
        maps.append(m)
    nc = build_fused(STOP)
    res = run_bass_kernel_spmd(nc, maps, core_ids=list(range(NCORE))).results
    out = np.zeros((B, S, D), np.float32)
    for cc in range(NCORE):
        b, ts = _tok(cc)
        out[b, ts] = res[cc]["xT_out"].T
    return out


STOP = 999
```

```python
import math
import numpy as np
import ml_dtypes
import concourse.bass as bass
import concourse.mybir as mybir
from concourse.bass_utils import run_bass_kernel_spmd

F32, BF16, I32 = mybir.dt.float32, mybir.dt.bfloat16, mybir.dt.int32
AF = mybir.ActivationFunctionType
ALU = mybir.AluOpType
NPBF = ml_dtypes.bfloat16

D = 1024; B = 2; S = 8192; DEPTH = 4; FF = 2816; PLE = 256; HD = 64; NH = 16
NCORE = 8; TOK = 2048
SG = 1024; TG = 512
EPS = 1e-6


class Buf:
    __slots__ = ("ap", "w", "r", "dsem", "dcnt", "excl")

    def __init__(self, ap, excl=False):
        self.ap = ap; self.w = None; self.r = {}; self.dsem = None; self.dcnt = 0; self.excl = excl

    def __getitem__(self, k):
        return self.ap[k]


class Ctx:
    NDSEM = 40

    def __init__(self):
        self.nc = bass.Bass("TRN2", target_bir_lowering=False)
        nc = self.nc
        self.E = {"pe": nc.tensor, "act": nc.scalar, "dve": nc.vector, "pool": nc.gpsimd, "sp": nc.sync}
        self.sem = {k: nc.semaphore("s_" + k).__enter__() for k in self.E}
        self.cnt = {k: 0 for k in self.E}
        self.waited = {}
        self.nbuf = 0
        self.out_tickets = []
        self.dpool = [[nc.semaphore(f"dq{i}").__enter__(), 0] for i in range(self.NDSEM)]
        self.dpool_i = 0
        self.cc_sem = nc.semaphore("cc").__enter__(); self.cc_cnt = 0
        self.live = []

    def sb(self, shape, dt, name=None):
        self.nbuf += 1
        cm = self.nc.sbuf_tensor(f"{name or 'sb'}_{self.nbuf}", list(shape), dt)
        self.live.append(cm)
        return Buf(cm.__enter__())

    def ps(self, shape=(128, 512), dt=F32, name=None):
        self.nbuf += 1
        cm = self.nc.psum_tensor(f"{name or 'ps'}_{self.nbuf}", list(shape), dt)
        self.live.append(cm)
        return Buf(cm.__enter__(), excl=True)

    def dram(self, name, shape, dt, kind):
        return self.nc.dram_tensor(name, list(shape), dt, kind=kind).ap()

    def scratch(self, name, shape, dt):
        return self.nc.dram_tensor(name, list(shape), dt).ap()

    def _wait(self, eng, tickets):
        for (sem, val, src) in tickets:
            if src == eng:
                continue
            key = (eng, id(sem))
            if self.waited.get(key, 0) < val:
                self.E[eng].wait_ge(sem, val)
                self.waited[key] = val

    def _deps(self, reads, writes):
        tk = []
        for b in reads:
            if b.w is not None:
                tk.append(b.w)
            if b.excl:
                tk.extend(b.r.values())
        for b in writes:
            if b.w is not None:
                tk.append(b.w)
            tk.extend(b.r.values())
        return tk

    def _commit(self, t, reads, writes):
        for b in reads:
            b.r[(id(t[0]), t[2])] = t
        for b in writes:
            b.w = t; b.r = {}

    def op(self, eng, fn, reads=(), writes=()):
        self._wait(eng, self._deps(reads, writes))
        ins = fn()
        self.cnt[eng] += 1
        ins.then_inc(self.sem[eng], 1)
        t = (self.sem[eng], self.cnt[eng], eng)
        self._commit(t, reads, writes)
        return t

    def dma(self, q, out, in_, reads=(), writes=(), track=None):
        self._wait(q, self._deps(reads, writes))
        b = track
        if b.dsem is None:
            assert self.dpool_i < self.NDSEM, "out of dma semaphores in this phase"
            b.dsem = self.dpool[self.dpool_i]; self.dpool_i += 1
        ins = self.E[q].dma_start(out=out, in_=in_)
        b.dsem[1] += 16
        ins.then_inc(b.dsem[0], 16)
        t = (b.dsem[0], b.dsem[1], "dma")
        self._commit(t, reads, writes)
        return t

    def gather(self, out, src2d, idx_col, out_buf, read_bufs):
        self._wait("pool", self._deps(read_bufs, (out_buf,)))
        b = out_buf
        if b.dsem is None:
            assert self.dpool_i < self.NDSEM, "out of dma semaphores in this phase"
            b.dsem = self.dpool[self.dpool_i]; self.dpool_i += 1
        ins = self.nc.gpsimd.indirect_dma_start(out=out, out_offset=None, in_=src2d,
                                                in_offset=bass.IndirectOffsetOnAxis(ap=idx_col, axis=0))
        b.dsem[1] += 16
        ins.then_inc(b.dsem[0], 16)
        t = (b.dsem[0], b.dsem[1], "dma")
        self._commit(t, read_bufs, (out_buf,))
        return t

    def all_gather(self, in_ap, out_ap, out_buf):
        self._wait("pool", self._deps((), (out_buf,)))
        ins = self.nc.gpsimd.collective_compute("AllGather", ALU.bypass, replica_groups=[[0, 1, 2, 3], [4, 5, 6, 7]],
                                                ins=[in_ap.opt()], outs=[out_ap.opt()])
        self.cc_cnt += 1
        ins.then_inc(self.cc_sem)
        t = (self.cc_sem, self.cc_cnt, "cc")
        self._commit(t, (), (out_buf,))
        return t

    def barrier(self):
        tk = [(self.sem[e], self.cnt[e], e) for e in self.E if self.cnt[e] > 0]
        tk += [(d[0], d[1], "dma") for d in self.dpool if d[1] > 0]
        if self.cc_cnt:
            tk.append((self.cc_sem, self.cc_cnt, "cc"))
        for e in self.E:
            self._wait(e, tk)

    def end_phase(self):
        self.barrier()
        for cm in reversed(self.live):
            cm.__exit__(None, None, None)
        self.live = []
        self.dpool_i = 0

    def finish(self, tickets):
        self._wait("sp", tickets)


def run_pipeline(items, skews):
    n = len(items)
    for step in range(n + max(skews)):
        for k, sk in enumerate(skews):
            idx = step - sk
            if 0 <= idx < n:
                items[idx][k]()


class Rot:
    def __init__(self, bufs):
        self.bufs = bufs; self.i = 0

    def next(self):
        b = self.bufs[self.i % len(self.bufs)]; self.i += 1
        return b


def w_chunks(W, cols=None):
    if cols is not None:
        W = W[:, cols]
    Din, Dout = W.shape
    K, J = Din // 128, Dout // 128
    return np.ascontiguousarray(W.reshape(K, 128, J, 128).transpose(2, 1, 0, 3))


def col_vec(v):
    return np.ascontiguousarray(v.reshape(-1, 128).T)


def fm(x2d):
    return np.ascontiguousarray(x2d.T)


class Dense:
    def __init__(self, c, post, pre, io):
        self.c = c
        self.post = post; self.pre = pre
        self.io = io
        self.piece_tix = {}
        self.acc_pending = []
        self.ag_queue = []
        self.x_in = io["x_in"]; self.x_out = io["x_out"]; self.ones_d = io["ones_bf"]
        if post:
            for k in ("w_mo", "w_f1", "w_f2", "w_pg", "w_pp", "p_in", "gains_post"):
                setattr(self, k, io[k])
        if pre:
            self.gain_pre = io["gain_pre"]; self.w_pre = io["w_pre"]
            self.npre = {"rglru": 16, "rope": 40, "plain": 24, "hnorm": 0}[pre]
            self.pre_dt = F32 if pre == "rglru" else BF16
            self.pos = io["pos"]; self.invf = io["invf"]
            self.pr_out = io["pr_loc"]

    def build(self):
        c = self.c; nc = c.nc
        self.xT = c.sb((128, 8, SG), F32, "xT_sb")
        self.aT = c.sb((128, 8, SG), BF16, "aT_sb")
        self.yT = c.sb((128, 8, SG), F32, "yT_sb")
        self.ones = c.sb((128, 128), BF16, "ones_sb")
        self.wrot = Rot([c.sb((128, 22 * 128), BF16, f"w{i}") for i in range(4)])
        self.prot = Rot([c.ps(name=f"pb{i}") for i in range(6)])
        self.ssb = [c.ps(name=f"ss{i}") for i in range(2)]
        self.sqrot = Rot([c.sb((128, TG), BF16, f"sq{i}") for i in range(6)])
        self.rstd = [c.sb((128, TG), F32, f"rstd{i}") for i in range(2)]
        self.tmprot = Rot([c.sb((128, TG), F32, f"tmp{i}") for i in range(4)])
        self.strot = Rot([c.sb((128, TG), self.pre_dt if self.pre else F32, f"st{i}") for i in range(4)])
        c.dma("sp", self.ones[:], self.ones_d[:], writes=[self.ones], track=self.ones)
        if self.post:
            self.oidx = c.sb((128, 16), mybir.dt.uint32, "oidx")
            c.dma("sp", self.oidx[:], self.io["idx_o"], writes=[self.oidx], track=self.oidx)
            self.oB = c.sb((128, 8, SG), BF16, "oB_sb")
            self.gT = c.sb((128, 22, SG), BF16, "gT_sb")
            self.pT = c.sb((128, 2, SG), BF16, "pT_sb")
            self.gp = c.sb((128, 5, 8), F32, "gp_sb")
            c.dma("sp", self.gp[:], self.gains_post[:], writes=[self.gp], track=self.gp)
        if self.pre:
            self.gpre = c.sb((128, 8), F32, "gpre_sb")
            c.dma("sp", self.gpre[:], self.gain_pre[:], writes=[self.gpre], track=self.gpre)
        if self.pre == "rope":
            rs = self.io["rope_scr"]
            self.cosT = c.sb((128, TOK), F32, "cosT"); self.sinT = c.sb((128, TOK), F32, "sinT")
            c.dma("sp", self.cosT[:], rs[0], writes=[self.cosT], track=self.cosT)
            c.dma("sp", self.sinT[:], rs[1], writes=[self.sinT], track=self.sinT)
        for sg in range(TOK // SG):
            self.run_sg(sg)
        if self.io.get("rope_build"):
            self.build_rope_tables()
            rs = self.io["rope_scr"]
            c.dma("sp", rs[0], self.cosT[:], reads=[self.cosT], track=self.cosT)
            c.dma("sp", rs[1], self.sinT[:], reads=[self.sinT], track=self.sinT)

    def build_rope_tables(self):
        c = self.c; nc = c.nc
        self.cosT = c.sb((128, TOK), F32, "cosT"); self.sinT = c.sb((128, TOK), F32, "sinT")
        CW = 512
        posi = c.sb((128, CW), I32, "posi"); ang = c.sb((128, CW), F32, "ang")
        kf = c.sb((128, CW), F32, "kf"); ki = c.sb((128, CW), I32, "ki"); m = c.sb((128, CW), F32, "rm")
        inv = c.sb((128, 2), F32, "invf_sb")
        c.dma("sp", inv[:], self.invf[:], writes=[inv], track=inv)
        V = nc.vector; G = nc.gpsimd
        TWO_PI = 2.0 * math.pi
        C1 = 6.28125; C2 = TWO_PI - C1

        def wrap(t):
            c.op("pool", lambda: G.tensor_scalar(out=m[:], in0=t[:], scalar1=math.pi, scalar2=-TWO_PI, op0=ALU.is_gt, op1=ALU.mult), [t], [m])
            c.op("dve", lambda: V.tensor_tensor(out=t[:], in0=t[:], in1=m[:], op=ALU.add), [t, m], [t])
            c.op("pool", lambda: G.tensor_scalar(out=m[:], in0=t[:], scalar1=-math.pi, scalar2=TWO_PI, op0=ALU.is_lt, op1=ALU.mult), [t], [m])
            c.op("dve", lambda: V.tensor_tensor(out=t[:], in0=t[:], in1=m[:], op=ALU.add), [t, m], [t])

        for ch in range(TOK // CW):
            sl = slice(ch * CW, (ch + 1) * CW)
            c.dma("sp", posi[:], self.pos[:, sl].partition_broadcast(128), writes=[posi], track=posi)
            c.op("dve", lambda: V.tensor_copy(out=ang[:], in_=posi[:]), [posi], [ang])
            c.op("pool", lambda: G.tensor_scalar(out=ang[:], in0=ang[:], scalar1=inv[:, 0:1], scalar2=None, op0=ALU.mult), [ang, inv], [ang])
            c.op("dve", lambda: V.tensor_scalar(out=kf[:], in0=ang[:], scalar1=1.0 / TWO_PI, scalar2=0.5, op0=ALU.mult, op1=ALU.add), [ang], [kf])
            c.op("pool", lambda: G.tensor_copy(out=ki[:], in_=kf[:]), [kf], [ki])
            c.op("dve", lambda: V.tensor_copy(out=kf[:], in_=ki[:]), [ki], [kf])
            c.op("dve", lambda: V.scalar_tensor_tensor(out=ang[:], in0=kf[:], scalar=-C1, in1=ang[:], op0=ALU.mult, op1=ALU.add), [kf, ang], [ang])
            c.op("dve", lambda: V.scalar_tensor_tensor(out=ang[:], in0=kf[:], scalar=-C2, in1=ang[:], op0=ALU.mult, op1=ALU.add), [kf, ang], [ang])
            wrap(ang); wrap(ang)
            c.op("act", lambda: nc.scalar.activation(out=self.sinT[:, sl], in_=ang[:], func=AF.Sin), [ang], [self.sinT])
            c.op("pool", lambda: G.tensor_scalar(out=self.sinT[:, sl], in0=self.sinT[:, sl], scalar1=inv[:, 1:2], scalar2=None, op0=ALU.mult), [self.sinT, inv], [self.sinT])
            c.op("dve", lambda: V.tensor_scalar(out=ang[:], in0=ang[:], scalar1=math.pi / 2, scalar2=None, op0=ALU.add), [ang], [ang])
            wrap(ang)
            c.op("act", lambda: nc.scalar.activation(out=self.cosT[:, sl], in_=ang[:], func=AF.Sin), [ang], [self.cosT])

    def proj(self, wd, J, K, act, evac):
        c = self.c; nc = c.nc
        for j in range(J):
            wt = self.wrot.next()
            wv = wt[:, 0:K * 128]
            c.dma("pool", wv, wd[j].rearrange("p k m -> p (k m)"), writes=[wt], track=wt)
            for tg in range(SG // TG):
                bank = self.prot.next()
                for k in range(K):
                    c.op("pe", lambda k=k: nc.tensor.matmul(bank[:], lhsT=wt[:, k * 128:(k + 1) * 128],
                                                             rhs=act[:, k, tg * TG:(tg + 1) * TG],
                                                             start=(k == 0), stop=(k == K - 1)),
                         [wt, act], [bank])
                evac(j, tg, bank)

    def stats(self, src):
        c = self.c; nc = c.nc
        for tg in range(SG // TG):
            for k in range(8):
                sq = self.sqrot.next()
                c.op("act", lambda: nc.scalar.activation(out=sq[:], in_=src[:, k, tg * TG:(tg + 1) * TG], func=AF.Square), [src], [sq])
                c.op("pe", lambda: nc.tensor.matmul(self.ssb[tg][:], lhsT=self.ones[:], rhs=sq[:], start=(k == 0), stop=(k == 7)),
                     [self.ones, sq], [self.ssb[tg]])
            r = self.rstd[tg]
            c.op("act", lambda: nc.scalar.activation(out=r[:], in_=self.ssb[tg][:], func=AF.Sqrt, scale=1.0 / D, bias=self.epsb[:, 0:1]), [self.ssb[tg], self.epsb], [r])
            c.op("dve", lambda: nc.vector.reciprocal(out=r[:], in_=r[:]), [r], [r])

    def acc_stats(self, src_ap, srcbuf, j, tg):
        c = self.c; nc = c.nc
        sq = self.sqrot.next()
        c.op("act", lambda: nc.scalar.activation(out=sq[:], in_=src_ap, func=AF.Square), [srcbuf], [sq])
        self.acc_pending.append((sq, j, tg))
        while len(self.acc_pending) > 3:
            self._acc_mm(*self.acc_pending.pop(0))

    def _acc_mm(self, sq, j, tg):
        c = self.c; nc = c.nc
        c.op("pe", lambda: nc.tensor.matmul(self.ssb[tg][:], lhsT=self.ones[:], rhs=sq[:], start=(j == 0), stop=(j == 7)),
             [self.ones, sq], [self.ssb[tg]])

    def finish_stats(self):
        c = self.c; nc = c.nc
        while self.acc_pending:
            self._acc_mm(*self.acc_pending.pop(0))
        for tg in range(SG // TG):
            r = self.rstd[tg]
            c.op("act", lambda: nc.scalar.activation(out=r[:], in_=self.ssb[tg][:], func=AF.Sqrt, scale=1.0 / D, bias=self.epsb[:, 0:1]), [self.ssb[tg], self.epsb], [r])
            c.op("dve", lambda: nc.vector.reciprocal(out=r[:], in_=r[:]), [r], [r])

    def norm_add(self, gi):
        c = self.c; nc = c.nc
        self.finish_stats()
        for tg in range(SG // TG):
            sl = slice(tg * TG, (tg + 1) * TG)
            for k in range(8):
                t = self.tmprot.next()
                c.op("dve", lambda: nc.vector.scalar_tensor_tensor(out=t[:], in0=self.yT[:, k, sl], scalar=self.gp[:, gi, k:k + 1],
                                                                   in1=self.rstd[tg][:], op0=ALU.mult, op1=ALU.mult),
                     [self.yT, self.gp, self.rstd[tg]], [t])
                c.op("dve", lambda: nc.vector.tensor_tensor(out=self.xT[:, k, sl], in0=self.xT[:, k, sl], in1=t[:], op=ALU.add),
                     [self.xT, t], [self.xT])
                self.acc_stats(self.xT[:, k, sl], self.xT, k, tg)

    def norm_to_a(self, gains, gi=None, have_stats=False):
        c = self.c; nc = c.nc
        if have_stats:
            self.finish_stats()
        else:
            self.stats(self.xT)
        for tg in range(SG // TG):
            sl = slice(tg * TG, (tg + 1) * TG)
            for k in range(8):
                g = gains[:, gi, k:k + 1] if gi is not None else gains[:, k:k + 1]
                c.op("dve", lambda: nc.vector.scalar_tensor_tensor(out=self.aT[:, k, sl], in0=self.xT[:, k, sl], scalar=g,
                                                                   in1=self.rstd[tg][:], op0=ALU.mult, op1=ALU.mult),
                     [self.xT, gains, self.rstd[tg]], [self.aT])

    def run_sg(self, sg):
        c = self.c; nc = c.nc
        t0 = sg * SG
        if sg == 0:
            self.epsb = c.sb((128, 1), F32, "epsb")
            c.op("pool", lambda: nc.gpsimd.memset(self.epsb[:], EPS), [], [self.epsb])
        c.dma("sp", self.xT[:], self.x_in[:, t0:t0 + SG].rearrange("(k p) t -> p k t", p=128), writes=[self.xT], track=self.xT)
        if self.post:
            if sg == 0:
                for k in range(8):
                    c.gather(self.aT[:, k, :], self.io["o_all2d"], self.oidx[:, k:k + 1], self.aT, [self.io["o_all_buf"], self.oidx])
            c.dma("pool", self.pT[:], self.p_in[:, t0:t0 + SG].rearrange("(k p) t -> p k t", p=128), writes=[self.pT], track=self.pT)

            def evac_y(j, tg, bank):
                c.op("act", lambda: nc.scalar.copy(out=self.yT[:, j, tg * TG:(tg + 1) * TG], in_=bank[:]), [bank], [self.yT])
                self.acc_stats(bank[:], bank, j, tg)

            self.proj(self.w_mo, 8, 8, (self.aT if sg == 0 else self.oB), evac_y)
            if sg == 0:
                for k in range(8):
                    c.gather(self.oB[:, k, :], self.io["o_all2d"], self.oidx[:, 8 + k:8 + k + 1], self.oB, [self.io["o_all_buf"], self.oidx])
            self.norm_add(0)
            self.norm_to_a(self.gp, 1, have_stats=True)
            pend = {}

            def evac_f1(j, tg, bank):
                cch, half = divmod(j, 2)
                if half == 0:
                    pend[tg] = bank
                    return
                b1 = pend.pop(tg)
                t = self.tmprot.next()
                c.op("act", lambda: nc.scalar.activation(out=t[:], in_=b1[:], func=AF.Silu), [b1], [t])
                c.op("dve", lambda: nc.vector.tensor_tensor(out=self.gT[:, cch, tg * TG:(tg + 1) * TG], in0=t[:], in1=bank[:], op=ALU.mult),
                     [t, bank], [self.gT])

            self.proj(self.w_f1, 44, 8, self.aT, evac_f1)
            self.proj(self.w_f2, 8, 22, self.gT, evac_y)
            self.norm_add(2)
            for k in range(8):
                c.op("dve", lambda: nc.vector.tensor_copy(out=self.aT[:, k, :], in_=self.xT[:, k, :]), [self.xT], [self.aT])

            def evac_gate(j, tg, bank):
                c.op("act", lambda: nc.scalar.activation(out=self.yT[:, j, tg * TG:(tg + 1) * TG], in_=bank[:], func=AF.Sigmoid,
                                                         bias=self.gp[:, 4, j:j + 1]), [bank, self.gp], [self.yT])

            self.proj(self.w_pg, 8, 8, self.aT, evac_gate)

            def evac_pp(j, tg, bank):
                sl = slice(tg * TG, (tg + 1) * TG)
                c.op("dve", lambda: nc.vector.tensor_tensor(out=self.yT[:, j, sl], in0=self.yT[:, j, sl], in1=bank[:], op=ALU.mult),
                     [self.yT, bank], [self.yT])
                self.acc_stats(self.yT[:, j, sl], self.yT, j, tg)

            self.proj(self.w_pp, 8, 2, self.pT, evac_pp)
            self.norm_add(3)
        t = c.dma("sp", self.x_out[:, t0:t0 + SG].rearrange("(k p) t -> p k t", p=128), self.xT[:], reads=[self.xT], track=self.xT)
        if self.io.get("final"):
            c.out_tickets.append(t)
        if not self.pre:
            return
        self.norm_to_a(self.gpre, have_stats=self.post)
        pr = self.pr_out
        if self.pre == "hnorm":
            for a in range(4):
                tk = c.dma("sp", pr[a, :, t0:t0 + SG].rearrange("(h p) t -> p h t", p=128), self.aT[:, 2 * a:2 * a + 2, :], reads=[self.aT], track=self.aT)
            if sg == TOK // SG - 1:
                c._wait("pool", [tk])
                for k in range(4):
                    self.io["ag_pr"](k)
            return

        def store(st, jo, tg):
            typ, hc = divmod(jo, 8)
            g, half = divmod(hc, 2)
            if self.pre == "rglru":
                dst = pr[g * 4 + typ * 2 + half, :, t0 + tg * TG:t0 + (tg + 1) * TG]
            else:
                dst = pr[g * 3 + typ, half * 128:(half + 1) * 128, t0 + tg * TG:t0 + (tg + 1) * TG]
            tk = c.dma("sp", dst, st[:], reads=[st], track=st)
            if sg == TOK // SG - 1:
                piece = (g * 4 + typ * 2 + half) if self.pre == "rglru" else (g * 3 + typ)
                lst = self.piece_tix.setdefault(piece, [])
                lst.append(tk)
                if len(lst) == (2 if self.pre == "rglru" else 4):
                    self.ag_queue.append((piece, lst))
                    while len(self.ag_queue) > 2:
                        pk, l = self.ag_queue.pop(0)
                        c._wait("pool", l); self.io["ag_pr"](pk)

        if self.pre == "rglru":
            def evac(j, tg, bank):
                st = self.strot.next()
                c.op("act", lambda: nc.scalar.copy(out=st[:], in_=bank[:]), [bank], [st])
                store(st, j, tg)
        elif self.pre == "plain":
            def evac(j, tg, bank):
                st = self.strot.next()
                c.op("act", lambda: nc.scalar.activation(out=st[:], in_=bank[:], func=AF.Copy, scale=(0.125 if j < 8 else 1.0)), [bank], [st])
                store(st, j, tg)
        else:
            pend = {}

            def evac(j, tg, bank):
                if j >= 32:
                    st = self.strot.next()
                    c.op("act", lambda: nc.scalar.copy(out=st[:], in_=bank[:]), [bank], [st])
                    store(st, j - 16, tg)
                    return
                jj, var = divmod(j, 2)
                if var == 0:
                    pend[tg] = bank
                    return
                b1 = pend.pop(tg)
                sc = 0.125 if jj < 8 else 1.0
                tsl = slice(t0 + tg * TG, t0 + (tg + 1) * TG)
                t1 = self.tmprot.next(); t2 = self.tmprot.next(); st = self.strot.next()
                c.op("dve", lambda: nc.vector.scalar_tensor_tensor(out=t1[:], in0=b1[:], scalar=sc, in1=self.cosT[:, tsl], op0=ALU.mult, op1=ALU.mult),
                     [b1, self.cosT], [t1])
                c.op("dve", lambda: nc.vector.scalar_tensor_tensor(out=t2[:], in0=bank[:], scalar=sc, in1=self.sinT[:, tsl], op0=ALU.mult, op1=ALU.mult),
                     [bank, self.sinT], [t2])
                c.op("dve", lambda: nc.vector.tensor_tensor(out=st[:], in0=t1[:], in1=t2[:], op=ALU.add), [t1, t2], [st])
                store(st, jj, tg)
        self.proj(self.w_pre, self.npre, 8, self.aT, evac)
        while self.ag_queue:
            pk, l = self.ag_queue.pop(0)
            c._wait("pool", l); self.io["ag_pr"](pk)


def emit_dense(c, post, pre, io):
    Dense(c, post, pre, io).build()
    c.end_phase()


def emit_rglru(c, io):
    nc = c.nc
    CH = 2048
    cw = io["cw"]; gw = io["gw"]; o_loc = io["o_loc"].rearrange("a p s -> (a p) s")
    V = nc.vector; G = nc.gpsimd; A = nc.scalar
    cws = c.sb((128, 2, 8), F32, "cws"); c.dma("sp", cws[:], cw, writes=[cws], track=cws)
    gws = c.sb((128, 4, 128), BF16, "gws")
    c.dma("pool", gws[:], gw.rearrange("a b p m -> p (a b) m"), writes=[gws], track=gws)
    ridx = c.sb((128, 16), mybir.dt.uint32, "ridx"); c.dma("sp", ridx[:], io["idx_r"], writes=[ridx], track=ridx)
    wg = c.sb((128, 4, 8 * 128), BF16, "wg")
    c.dma("pool", wg[:], io["w_rg"].rearrange("c p k m -> p c (k m)"), writes=[wg], track=wg)
    hall = io["h_all"]; hallb = io["h_all_buf"]
    hcr = Rot([c.sb((128, 8, CH), BF16, f"hc{i}") for i in range(2)])
    cs = c.sb((128, 2), F32, "cs")
    onec = c.sb((128, 1), F32, "onec")
    c.op("pool", lambda: G.memset(onec[:], 1.0), [], [onec])
    for ct in range(2):
        c.op("act", lambda: A.activation(out=cs[:, ct:ct + 1], in_=cws[:, ct, 7:8], func=AF.Exp, scale=-1.0), [cws], [cs])
        c.op("act", lambda: A.activation(out=cs[:, ct:ct + 1], in_=cs[:, ct:ct + 1], func=AF.Ln, bias=onec[:, 0:1]), [cs, onec], [cs])
    c.op("dve", lambda: V.tensor_scalar(out=cs[:], in0=cs[:], scalar1=-8.0, scalar2=None, op0=ALU.mult), [cs], [cs])
    xfull = c.sb((128, S + 3), F32, "xfull")
    yr = Rot([c.sb((128, CH), F32, f"y{i}") for i in range(2)])
    xc = c.sb((128, CH), F32, "xc"); xcb = c.sb((128, CH), BF16, "xcb")
    ra = c.sb((128, CH), F32, "ra"); mm = c.sb((128, CH), F32, "mm"); ib = c.sb((128, CH), F32, "ib")
    hh = c.sb((128, CH), F32, "hh"); tt = c.sb((128, CH), F32, "tt")
    orot = Rot([c.sb((128, CH), BF16, f"o{i}") for i in range(2)])
    carry = c.sb((128, 1), F32, "carry")
    prot = Rot([c.ps(name=f"pb{i}") for i in range(4)])
    for ct in range(2):
        rows = slice(ct * 128, (ct + 1) * 128)
        otix = []
        c.op("dve", lambda: V.memset(xfull[:, 0:3], 0.0), [], [xfull])
        for tc in range(S // CH):
            t0 = tc * CH
            y = yr.next()
            hc = hcr.next()
            for a in range(4):
                c.dma("sp", hc[:, 2 * a:2 * a + 2, :], hall[a, tc * 256:(tc + 1) * 256, :].rearrange("(h p) t -> p h t", p=128),
                      reads=[hallb], writes=[hc], track=hc)
            for typ in range(2):
                for sb_ in range(CH // 512):
                    bank = prot.next(); sl = slice(sb_ * 512, (sb_ + 1) * 512)
                    for k in range(8):
                        c.op("pe", lambda: nc.tensor.matmul(bank[:], lhsT=wg[:, typ * 2 + ct, k * 128:(k + 1) * 128], rhs=hc[:, k, sl],
                                                            start=(k == 0), stop=(k == 7)), [wg, hc], [bank])
                    if typ == 0:
                        c.op("act", lambda: A.copy(out=y[:, sl], in_=bank[:]), [bank], [y])
                    else:
                        c.op("act", lambda: A.copy(out=xfull[:, 3 + t0 + sb_ * 512:3 + t0 + (sb_ + 1) * 512], in_=bank[:]), [bank], [xfull])
            c.op("dve", lambda: V.tensor_scalar(out=xc[:], in0=xfull[:, t0:t0 + CH], scalar1=cws[:, ct, 0:1], scalar2=cws[:, ct, 4:5], op0=ALU.mult, op1=ALU.add), [xfull, cws], [xc])
            for tap in range(1, 4):
                c.op("dve", lambda: V.scalar_tensor_tensor(out=xc[:], in0=xfull[:, t0 + tap:t0 + tap + CH], scalar=cws[:, ct, tap:tap + 1], in1=xc[:], op0=ALU.mult, op1=ALU.add), [xfull, cws, xc], [xc])
            c.op("act", lambda: A.copy(out=xcb[:], in_=xc[:]), [xc], [xcb])
            for gi, dst in ((0, ra), (1, ib)):
                for sb_ in range(CH // 512):
                    bank = prot.next(); sl = slice(sb_ * 512, (sb_ + 1) * 512)
                    c.op("pe", lambda: nc.tensor.matmul(bank[:], lhsT=gws[:, ct * 2 + gi, :], rhs=xcb[:, sl], start=True, stop=True), [gws, xcb], [bank])
                    c.op("act", lambda: A.activation(out=dst[:, sl], in_=bank[:], func=AF.Sigmoid, bias=cws[:, ct, 5 + gi:6 + gi]), [bank, cws], [dst])
            c.op("act", lambda: A.activation(out=ra[:], in_=ra[:], func=AF.Exp, scale=cs[:, ct:ct + 1]), [ra, cs], [ra])
            c.op("pool", lambda: G.tensor_tensor(out=mm[:], in0=ra[:], in1=ra[:], op=ALU.mult), [ra], [mm])
            c.op("dve", lambda: V.tensor_scalar(out=mm[:], in0=mm[:], scalar1=-1.0, scalar2=1.0, op0=ALU.mult, op1=ALU.add), [mm], [mm])
            c.op("act", lambda: A.activation(out=mm[:], in_=mm[:], func=AF.Sqrt), [mm], [mm])
            c.op("dve", lambda: V.tensor_tensor(out=ib[:], in0=ib[:], in1=xc[:], op=ALU.mult), [ib, xc], [ib])
            c.op("dve", lambda: V.tensor_tensor(out=ib[:], in0=ib[:], in1=mm[:], op=ALU.mult), [ib, mm], [ib])
            if tc > 0:
                c.op("pool", lambda: G.tensor_tensor(out=carry[:], in0=ra[:, 0:1], in1=hh[:, CH - 1:CH], op=ALU.mult), [ra, hh], [carry])
                c.op("pool", lambda: G.tensor_tensor(out=ib[:, 0:1], in0=ib[:, 0:1], in1=carry[:], op=ALU.add), [ib, carry], [ib])
            c.op("dve", lambda: V.tensor_tensor_scan(out=hh[:], data0=ra[:], data1=ib[:], initial=0.0, op0=ALU.mult, op1=ALU.add),
                 [ra, ib], [hh])
            c.op("pool", lambda: G.tensor_tensor(out=tt[:], in0=y[:], in1=y[:], op=ALU.mult), [y], [tt])
            c.op("dve", lambda: V.tensor_scalar(out=tt[:], in0=tt[:], scalar1=0.044715, scalar2=1.0, op0=ALU.mult, op1=ALU.add), [tt], [tt])
            c.op("dve", lambda: V.tensor_tensor(out=tt[:], in0=tt[:], in1=y[:], op=ALU.mult), [tt, y], [tt])
            c.op("act", lambda: A.activation(out=tt[:], in_=tt[:], func=AF.Sigmoid, scale=2.0 * math.sqrt(2.0 / math.pi)), [tt], [tt])
            c.op("pool", lambda: G.tensor_tensor(out=tt[:], in0=tt[:], in1=y[:], op=ALU.mult), [tt, y], [tt])
            ob = orot.next()
            c.op("dve", lambda: V.tensor_tensor(out=ob[:], in0=hh[:], in1=tt[:], op=ALU.mult), [hh, tt], [ob])
            otix.append(c.dma("sp", o_loc[rows, t0:t0 + CH], ob[:], reads=[ob], track=ob))
        c._wait("pool", otix)
        io["ag_o"](2 * ct); io["ag_o"](2 * ct + 1)
    c.end_phase()


def attn_common(c, io, vcols, dil=False):
    nc = c.nc
    src = io["pr_all2d"]; srcb = io["pr_all_buf"]
    aidx = c.sb((128, 24), mybir.dt.uint32, "aidx"); c.dma("sp", aidx[:], io["idx_a"], writes=[aidx], track=aidx)
    ident = c.sb((128, 128), BF16, "ident"); c.dma("sp", ident[:], io["ident_bf"], writes=[ident], track=ident)
    qs = c.sb((128, 2, S), BF16, "qs"); ks = c.sb((128, 4, S), BF16, "kz"); vs = c.sb((128, S // 128, vcols + (0 if dil else 64)), BF16, "vs")
    vtr = Rot([c.sb((128, 2048), BF16, f"vT{i}") for i in range(2)])
    for h in range(4):
        c.op("pool" if h % 2 else "dve", lambda: (nc.gpsimd if h % 2 else nc.vector).memset(ks[:, h, :], 0.0), [], [ks])
    for s_ in range(2):
        for j in range(4):
            c.gather(qs[:, s_, j * 2048:(j + 1) * 2048], src, aidx[:, s_ * 4 + j:s_ * 4 + j + 1], qs, [srcb, aidx])
            kst = vtr.next()
            c.gather(kst[:], src, aidx[:, 8 + s_ * 4 + j:8 + s_ * 4 + j + 1], kst, [srcb, aidx])
            c.op("dve", lambda: nc.vector.tensor_copy(out=ks[0:64, s_, j * 2048:(j + 1) * 2048], in_=kst[0:64, :]), [kst], [ks])
            c.op("pool", lambda: nc.gpsimd.tensor_copy(out=ks[64:128, 2 + s_, j * 2048:(j + 1) * 2048], in_=kst[64:128, :]), [kst], [ks])
    if dil:
        c.op("dve", lambda: nc.vector.memset(vs[:], 1.0), [], [vs])
    tpr = Rot([c.ps((128, 1024), BF16, name="tp")])
    for vc in range(2):
        for j in range(4):
            vT = vtr.next()
            c.gather(vT[:], src, aidx[:, 16 + vc * 4 + j:16 + vc * 4 + j + 1], vT, [srcb, aidx])
            for q4 in range(4):
                tp = tpr.next()
                for b4 in range(4):
                    blk = q4 * 4 + b4
                    c.op("pe", lambda: nc.tensor.transpose(out=tp[:, b4 * 128:(b4 + 1) * 128], in_=vT[:, blk * 128:(blk + 1) * 128], identity=ident[:]),
                         [vT, ident], [tp])
                b0 = j * 16 + q4 * 4
                if dil:
                    dst = vs[:, b0:b0 + 4, vc * 130:(vc + 1) * 130].rearrange("p b (h c) -> p b h c", c=65)[:, :, :, 0:64]
                    srcp = tp[:, 0:512].rearrange("p (b h c) -> p b h c", b=4, h=2)
                else:
                    dst = vs[:, b0:b0 + 4, vc * 128:(vc + 1) * 128]
                    srcp = tp[:, 0:512].rearrange("p (b c) -> p b c", b=4)
                c.op("act", lambda: nc.scalar.copy(out=dst, in_=srcp), [tp], [vs])
    return qs, ks, vs


def sb_project(c, io):
    nc = c.nc; PE = nc.tensor; A = nc.scalar
    hall = io["h_all"]; hallb = io["h_all_buf"]
    qs = c.sb((128, 2, S), BF16, "qs"); ks = c.sb((128, 4, S), BF16, "kz"); vs = c.sb((128, S // 128, 320), BF16, "vs")
    wq = c.sb((128, 2, 1024), BF16, "wq"); wk = c.sb((128, 2, 1024), BF16, "wk"); wv = c.sb((128, 8, 256), BF16, "wv")
    c.dma("pool", wq[:], io["w_sbq"].rearrange("s p k m -> p s (k m)"), writes=[wq], track=wq)
    c.dma("pool", wk[:], io["w_sbk"].rearrange("s p k m -> p s (k m)"), writes=[wk], track=wk)
    c.dma("pool", wv[:], io["w_sbv"], writes=[wv], track=wv)
    for h in range(4):
        c.op("pool" if h % 2 else "dve", lambda: (nc.gpsimd if h % 2 else nc.vector).memset(ks[:, h, :], 0.0), [], [ks])
    hc = c.sb((128, 8, 2048), BF16, "hc")
    pj = c.ps(name="pj")
    for j in range(4):
        for a_ in range(4):
            c.dma("sp", hc[:, 2 * a_:2 * a_ + 2, :], hall[a_, j * 256:(j + 1) * 256, :].rearrange("(h p) t -> p h t", p=128),
                  reads=[hallb], writes=[hc], track=hc)
        for sb_ in range(4):
            sl = slice(sb_ * 512, (sb_ + 1) * 512); tsl = slice(j * 2048 + sb_ * 512, j * 2048 + (sb_ + 1) * 512)
            for s_ in range(2):
                for k in range(8):
                    c.op("pe", lambda: PE.matmul(pj[:], lhsT=wq[:, s_, k * 128:(k + 1) * 128], rhs=hc[:, k, sl], start=(k == 0), stop=(k == 7)), [wq, hc], [pj])
                c.op("act", lambda: A.activation(out=qs[:, s_, tsl], in_=pj[:], func=AF.Copy, scale=0.125), [pj], [qs])
                for k in range(8):
                    c.op("pe", lambda: PE.matmul(pj[:], lhsT=wk[:, s_, k * 128:(k + 1) * 128], rhs=hc[:, k, sl], start=(k == 0), stop=(k == 7)), [wk, hc], [pj])
                c.op("act", lambda: A.copy(out=ks[0:64, s_, tsl], in_=pj[0:64, :]), [pj], [ks])
                c.op("act", lambda: A.copy(out=ks[64:128, 2 + s_, tsl], in_=pj[64:128, :]), [pj], [ks])
            for b4 in range(4):
                blk = j * 16 + sb_ * 4 + b4; tk = slice(sb_ * 512 + b4 * 128, sb_ * 512 + (b4 + 1) * 128)
                for k in range(8):
                    c.op("pe", lambda: PE.matmul(pj[:, 0:256], lhsT=hc[:, k, tk], rhs=wv[:, k, :], start=(k == 0), stop=(k == 7)), [hc, wv], [pj])
                c.op("act", lambda: A.copy(out=vs[:, blk, 0:256], in_=pj[:, 0:256]), [pj], [vs])
    return qs, ks, vs


def head_ap(t, h):
    if t.ap.shape[1] == 4:
        return lambda sl: t[:, h, sl]
    return lambda sl: t[:, h % 2, sl]


def emit_sb(c, io):
    nc = c.nc
    V = nc.vector; G = nc.gpsimd; A = nc.scalar; PE = nc.tensor
    qs, ks, vs = sb_project(c, io)
    oT = io["o_loc"]
    cs_ = c.sb((128, 18, 128), BF16, "cst_sb"); c.dma("sp", cs_[:], io["sb_cst"], writes=[cs_], track=cs_)
    tri = cs_[:, 0, :]; ones = cs_[:, 1, :]
    onec = c.sb((128, 1), F32, "onec")
    c.op("pool", lambda: G.memset(onec[:], 1.0), [], [onec])
    zr = Rot([c.ps(name=f"z{i}") for i in range(2)])
    br = Rot([c.ps(name=f"b{i}") for i in range(2)])
    cr = Rot([c.ps(name=f"c{i}") for i in range(2)])
    ob = c.ps(name="o0")
    er = Rot([c.sb((128, 512), F32, f"e{i}") for i in range(2)])
    spr = Rot([c.sb((128, 512), BF16, f"sp{i}") for i in range(3)])
    t1r = Rot([c.sb((128, 512), F32, f"t1{i}") for i in range(2)])
    atr = Rot([c.sb((128, 512), BF16, f"at{i}") for i in range(3)])
    R = c.sb((128, 512), F32, "R")
    ost = Rot([c.sb((64, 512), BF16, f"ost{i}") for i in range(2)])
    items = []
    otix = {}
    for h in range(4):
        qh = head_ap(qs, h); kh = head_ap(ks, h)
        for qg in range(S // 512):
            qsl = slice(qg * 512, (qg + 1) * 512)
            nkb = 4 * qg + 4
            for i, kb in enumerate(range(nkb - 1, -1, -1)):
                def mk_item(h=h, qh=qh, kh=kh, qg=qg, qsl=qsl, nkb=nkb, i=i, kb=kb):
                    diag = kb >= 4 * qg
                    ksl = slice(kb * 128, (kb + 1) * 128)
                    zb = zr.next(); e = er.next(); sp = spr.next(); bb = br.next(); cb = cr.next(); t1 = t1r.next(); at = atr.next()
                    mk = cs_[:, 2 + 4 * (kb - 4 * qg):6 + 4 * (kb - 4 * qg), :].rearrange("p a b -> p (a b)") if diag else None
                    last = i == nkb - 1

                    def s1():
                        c.op("pe", lambda: PE.matmul(zb[:], lhsT=kh(ksl), rhs=qh(qsl), start=True, stop=True), [ks, qs], [zb])
                        c.op("act", lambda: A.activation(out=e[:], in_=zb[:], func=AF.Exp), [zb], [e])
                        c.op("act", lambda: A.activation(out=sp[:], in_=e[:], func=AF.Ln, bias=onec[:, 0:1]), [e, onec], [sp])
                        if diag:
                            c.op("pool", lambda: G.tensor_tensor(out=sp[:], in0=sp[:], in1=mk, op=ALU.mult), [sp, cs_], [sp])

                    def s2():
                        c.op("pe", lambda: PE.matmul(bb[:], lhsT=kh(ksl), rhs=qh(qsl), start=True, stop=False), [ks, qs], [bb])
                        c.op("pe", lambda: PE.matmul(bb[:], lhsT=tri, rhs=sp[:], start=False, stop=True), [cs_, sp], [bb])
                        if not last:
                            c.op("pe", lambda: PE.matmul(cb[:], lhsT=ones, rhs=sp[:], start=True, stop=True), [cs_, sp], [cb])

                    def s3a():
                        if i == 0:
                            c.op("pool", lambda: G.memset(R[:], 0.0), [], [R])
                        c.op("dve", lambda: V.tensor_tensor(out=t1[:], in0=bb[:], in1=R[:], op=ALU.subtract), [bb, R], [t1])
                        if not last:
                            c.op("dve", lambda: V.tensor_tensor(out=R[:], in0=R[:], in1=cb[:], op=ALU.add), [R, cb], [R])
                        c.op("act", lambda: A.activation(out=at[:], in_=t1[:], func=AF.Exp), [t1], [at])
                        if diag:
                            c.op("pool", lambda: G.tensor_tensor(out=at[:], in0=at[:], in1=mk, op=ALU.mult), [at, cs_], [at])

                    def s3b():
                        c.op("pe", lambda: PE.matmul(ob[:], lhsT=vs[:, kb, h * 64:h * 64 + 128], rhs=at[:], start=(i == 0), stop=last),
                             [vs, at], [ob])
                        if last:
                            st = ost.next()
                            c.op("act", lambda: A.copy(out=st[:], in_=ob[0:64, :]), [ob], [st])
                            otix.setdefault(h, []).append(c.dma("sp", oT[h, :, qsl], st[:], reads=[st], track=st))
                            if qg == S // 512 - 1:
                                c._wait("pool", otix[h]); io["ag_o"](h)
                    return (s3a, s1, s2, s3b)
                items.append(mk_item())
    run_pipeline(items, (2, 0, 1, 2))
    c.end_phase()


def emit_softmax_attn(c, io, kind, lam_init=0.0):
    nc = c.nc
    V = nc.vector; G = nc.gpsimd; A = nc.scalar; PE = nc.tensor
    dil = kind == "dil"
    vcols = 260 if dil else 256
    qs, ks, vs = attn_common(c, io, vcols, dil)
    NM = 20 if dil else 4
    mk = c.sb((128, NM, 512), BF16, "mk"); c.dma("sp", mk[:], io["dil_masks" if dil else "diff_masks"], writes=[mk], track=mk)
    onesf = c.sb((128, 128), F32, "onesf"); c.dma("sp", onesf[:], io["ones_f"], writes=[onesf], track=onesf)
    zr = Rot([c.ps(name=f"z{i}") for i in range(2)])
    atr = Rot([c.sb((128, 512), BF16, f"at{i}") for i in range(4)])
    if dil:
        oT = io["o_loc"]
        orr = Rot([c.ps(name=f"o{i}") for i in range(2)])
        bcr = Rot([c.ps(name=f"bc{i}") for i in range(2)])
        rl = c.sb((128, 512), F32, "rl"); bcs = c.sb((64, 512), F32, "bcs")
        ost = Rot([c.sb((64, 512), BF16, f"ost{i}") for i in range(2)])
        items = []
        otix = {}
        for h in range(4):
            qh = head_ap(qs, h); kh = head_ap(ks, h)
            for qg in range(S // 512):
                qsl = slice(qg * 512, (qg + 1) * 512)
                ob = orr.next()
                kbs = list(range(max(0, 4 * qg - 16), 4 * qg + 4))
                for i, kb in enumerate(kbs):
                    def mk_item(h=h, qh=qh, kh=kh, qg=qg, qsl=qsl, ob=ob, kbs=kbs, i=i, kb=kb):
                        zb = zr.next(); at = atr.next()
                        mi = (512 * qg - 128 * kb + 384) // 128
                        last = i == len(kbs) - 1

                        def s1():
                            c.op("pe", lambda: PE.matmul(zb[:], lhsT=kh(slice(kb * 128, (kb + 1) * 128)), rhs=qh(qsl), start=True, stop=True), [ks, qs], [zb])
                            c.op("act", lambda: A.activation(out=at[:], in_=zb[:], func=AF.Exp), [zb], [at])
                            c.op("dve", lambda: V.tensor_tensor(out=at[:], in0=at[:], in1=mk[:, mi, :], op=ALU.mult), [at, mk], [at])

                        def s2():
                            c.op("pe", lambda: PE.matmul(ob[0:65, :], lhsT=vs[:, kb, h * 65:(h + 1) * 65], rhs=at[:], start=(i == 0), stop=last),
                                 [vs, at], [ob])
                            if last:
                                bc = bcr.next(); st = ost.next()
                                c.op("dve", lambda: V.reciprocal(out=rl[64:65, :], in_=ob[64:65, :]), [ob], [rl])
                                c.op("pe", lambda: PE.matmul(bc[0:64, :], lhsT=onesf[64:65, 0:64], rhs=rl[64:65, :], start=True, stop=True), [onesf, rl], [bc])
                                c.op("act", lambda: A.copy(out=bcs[:], in_=bc[0:64, :]), [bc], [bcs])
                                c.op("dve", lambda: V.tensor_tensor(out=st[:], in0=ob[0:64, :], in1=bcs[:], op=ALU.mult), [ob, bcs], [st])
                                otix.setdefault(h, []).append(c.dma("sp", oT[h, :, qsl], st[:], reads=[st], track=st))
                                if qg == S // 512 - 1:
                                    c._wait("pool", otix[h]); io["ag_o"](h)
                        return (s1, s2)
                    items.append(mk_item())
        run_pipeline(items, (0, 2))
        c.end_phase()
        return
    oT = io["o_loc"].rearrange("(d a) p s -> d (a p) s", a=2)
    onesb = c.sb((128, 128), BF16, "onesb"); c.dma("sp", onesb[:], io["ones_bf"], writes=[onesb], track=onesb)
    lam = c.sb((1, 4, 64), F32, "lam_sb"); c.dma("sp", lam[:], io["lam"], writes=[lam], track=lam)
    sub = c.sb((128, 1), F32, "sub_sb"); c.dma("sp", sub[:], io["subln"], writes=[sub], track=sub)
    prod = c.sb((1, 2, 64), F32, "prod"); dots = c.sb((1, 2), F32, "dots"); ee = c.sb((1, 2), F32, "ee")
    dl = c.sb((1, 2), F32, "dl"); nl = c.sb((1, 2), F32, "nl"); nlam = c.sb((128, 2), F32, "nlam")
    epsb = c.sb((128, 1), F32, "epsb")
    c.op("pool", lambda: G.memset(epsb[:], 1e-5), [], [epsb])
    for m in range(2):
        c.op("dve", lambda: V.tensor_tensor(out=prod[:, m, :], in0=lam[:, 2 * m, :], in1=lam[:, 2 * m + 1, :], op=ALU.mult), [lam], [prod])
    c.op("pool", lambda: G.memset(dots[:], 0.0), [], [dots])
    c.op("dve", lambda: V.tensor_reduce(out=dots[:], in_=prod[:], op=ALU.add, axis=mybir.AxisListType.X), [prod, dots], [dots])
    c.op("act", lambda: A.activation(out=ee[:], in_=dots[:], func=AF.Exp), [dots], [ee])
    for j in range(2):
        c.op("pool", lambda: G.tensor_tensor(out=dl[:, j:j + 1], in0=ee[:, 1:2], in1=ee[:, 0:1], op=ALU.subtract), [ee], [dl])
    c.op("dve", lambda: V.tensor_scalar(out=nl[:], in0=dl[:], scalar1=-lam_init, scalar2=None, op0=ALU.add), [dl], [nl])
    zb = zr.next()
    c.op("pe", lambda: PE.matmul(zb[:, 0:2], lhsT=onesf[0:1, :], rhs=nl[0:1, :], start=True, stop=True), [onesf, nl], [zb])
    c.op("act", lambda: A.copy(out=nlam[:], in_=zb[:, 0:2]), [zb], [nlam])
    c.op("pool", lambda: G.tensor_scalar(out=sub[:], in0=sub[:], scalar1=(1.0 - lam_init), scalar2=None, op0=ALU.mult), [sub], [sub])
    ob = [c.ps(name=f"o{i}") for i in range(2)]
    lb = [c.ps(name=f"l{i}") for i in range(2)]
    bc = lb[0]; ssb = c.ps(name="ss")
    rl = c.sb((1, 2, 512), F32, "rl"); rbs = [c.sb((128, 512), F32, f"rbs{i}") for i in range(2)]
    t1 = c.sb((128, 512), F32, "t1"); t2 = c.sb((128, 512), F32, "t2"); sq = c.sb((128, 512), BF16, "sq")
    rstd = c.sb((128, 512), F32, "rstd")
    ost = Rot([c.sb((128, 512), BF16, f"ost{i}") for i in range(2)])
    items = []
    for dh in range(2):
        for qg in range(S // 512):
            qsl = slice(qg * 512, (qg + 1) * 512)
            nkb = 4 * qg + 4
            for kb in range(nkb):
                for m in range(2):
                    def mk_item(dh=dh, qg=qg, qsl=qsl, nkb=nkb, kb=kb, m=m):
                        h = 2 * dh + m
                        qh = head_ap(qs, h); kh = head_ap(ks, h)
                        zb = zr.next(); at = atr.next()

                        def s1():
                            c.op("pe", lambda: PE.matmul(zb[:], lhsT=kh(slice(kb * 128, (kb + 1) * 128)), rhs=qh(qsl), start=True, stop=True), [ks, qs], [zb])
                            c.op("act", lambda: A.activation(out=at[:], in_=zb[:], func=AF.Exp), [zb], [at])
                            if kb >= 4 * qg:
                                c.op("dve", lambda: V.tensor_tensor(out=at[:], in0=at[:], in1=mk[:, kb - 4 * qg, :], op=ALU.mult), [at, mk], [at])

                        def s2():
                            c.op("pe", lambda: PE.matmul(ob[m][:], lhsT=vs[:, kb, dh * 128:(dh + 1) * 128], rhs=at[:], start=(kb == 0), stop=(kb == nkb - 1)),
                                 [vs, at], [ob[m]])
                            c.op("pe", lambda: PE.matmul(lb[m][:], lhsT=onesb[:], rhs=at[:], start=(kb == 0), stop=(kb == nkb - 1)),
                                 [onesb, at], [lb[m]])
                            if kb == nkb - 1 and m == 1:
                                epilogue(dh, qsl)
                        return (s1, s2)
                    items.append(mk_item())

    def epilogue(dh, qsl):
        for m in range(2):
            c.op("dve", lambda: V.reciprocal(out=rbs[m][:], in_=lb[m][:]), [lb[m]], [rbs[m]])
        c.op("dve", lambda: V.tensor_tensor(out=t1[:], in0=ob[0][:], in1=rbs[0][:], op=ALU.mult), [ob[0], rbs[0]], [t1])
        c.op("dve", lambda: V.tensor_tensor(out=t2[:], in0=ob[1][:], in1=rbs[1][:], op=ALU.mult), [ob[1], rbs[1]], [t2])
        c.op("pool", lambda: G.tensor_scalar(out=t2[:], in0=t2[:], scalar1=nlam[:, 0:1], scalar2=None, op0=ALU.mult), [t2, nlam], [t2])
        c.op("dve", lambda: V.tensor_tensor(out=t1[:], in0=t1[:], in1=t2[:], op=ALU.add), [t1, t2], [t1])
        c.op("act", lambda: A.activation(out=sq[:], in_=t1[:], func=AF.Square), [t1], [sq])
        c.op("pe", lambda: PE.matmul(ssb[:], lhsT=onesb[:], rhs=sq[:], start=True, stop=True), [onesb, sq], [ssb])
        c.op("act", lambda: A.activation(out=rstd[:], in_=ssb[:], func=AF.Sqrt, scale=1.0 / 128, bias=epsb[:, 0:1]), [ssb, epsb], [rstd])
        c.op("pool", lambda: G.tensor_copy(out=rstd2[:], in_=rstd[:]), [rstd], [rstd2])
        c.op("dve", lambda: V.reciprocal(out=rstd2[:], in_=rstd2[:]), [rstd2], [rstd2])
        st = ost.next()
        c.op("dve", lambda: V.scalar_tensor_tensor(out=st[:], in0=t1[:], scalar=sub[:, 0:1], in1=rstd2[:], op0=ALU.mult, op1=ALU.mult), [t1, sub, rstd2], [st])
        otix.setdefault(dh, []).append(c.dma("sp", oT[dh, :, qsl], st[:], reads=[st], track=st))
        if len(otix[dh]) == S // 512:
            c._wait("pool", otix[dh]); io["ag_o"](2 * dh); io["ag_o"](2 * dh + 1)

    otix = {}
    rstd2 = c.sb((128, 512), F32, "rstd2")
    bcb = [lb[0], lb[1]]
    run_pipeline(items, (0, 2))
    c.end_phase()


PRE_KINDS = ["rglru", "rope", "plain", "rope"]
LAM_INIT3 = 0.8 - 0.6 * math.exp(-0.3 * 3)
U32 = mybir.dt.uint32


def build_fused(stop=999):
    c = Ctx()
    step = [0]

    def done():
        step[0] += 1
        return step[0] >= stop

    EI = lambda n, sh, dt: c.dram(n, sh, dt, "ExternalInput")
    g = {}
    g["xT"] = EI("xT", (D, TOK), F32)
    g["xT_out"] = c.dram("xT_out", (D, TOK), F32, "ExternalOutput")
    g["ones_bf"] = EI("ones_bf", (128, 128), BF16); g["ones_f"] = EI("ones_f", (128, 128), F32)
    g["ident_bf"] = EI("ident_bf", (128, 128), BF16)
    g["pos"] = EI("pos", (1, TOK), I32); g["invf"] = EI("invf", (128, 2), F32)
    g["idx_o"] = EI("idx_o", (128, 16), U32); g["idx_a"] = EI("idx_a", (128, 24), U32); g["idx_r"] = EI("idx_r", (128, 16), U32)
    g["cw"] = EI("cw", (128, 2, 8), F32); g["gw"] = EI("gw", (2, 2, 128, 128), F32)
    g["dil_masks"] = EI("dil_masks", (128, 20, 512), BF16); g["diff_masks"] = EI("diff_masks", (128, 4, 512), BF16)
    g["sb_cst"] = EI("sb_cst", (128, 18, 128), BF16)
    g["lam"] = EI("lam", (1, 4, 64), F32); g["subln"] = EI("subln", (128, 1), F32)
    L = []
    for l in range(DEPTH):
        npre = {"rglru": 16, "rope": 40, "plain": 24}[PRE_KINDS[l]]
        L.append(dict(w_mo=EI(f"w_mo{l}", (8, 128, 8, 128), F32), w_f1=EI(f"w_f1{l}", (44, 128, 8, 128), F32),
                      w_f2=EI(f"w_f2{l}", (8, 128, 22, 128), F32), w_pg=EI(f"w_pg{l}", (8, 128, 8, 128), F32),
                      w_pp=EI(f"w_pp{l}", (8, 128, 2, 128), F32), p_in=EI(f"pT{l}", (PLE, TOK), F32),
                      gains_post=EI(f"gains_post{l}", (128, 5, 8), F32), gain_pre=EI(f"gain_pre{l}", (128, 8), F32),
                      w_pre=EI(f"w_pre{l}", (npre, 128, 8, 128), F32)))
    x_scr = c.scratch("x_scr", (D, TOK), F32)
    pr_loc_r = c.scratch("pr_loc_r", (16, 128, TOK), F32); pr_all_r = c.scratch("pr_all_r", (16, 4 * 128, TOK), F32)
    pr_loc_a = c.scratch("pr_loc_a", (12, 256, TOK), BF16); pr_all_a = c.scratch("pr_all_a", (12, 4 * 256, TOK), BF16)
    o_loc = c.scratch("o_loc", (4, 64, S), BF16); o_all = c.scratch("o_all", (4, 4 * 64, S), BF16)
    rope_scr = c.scratch("rope_scr", (2, 128, TOK), F32)
    pr_loc_h = c.scratch("pr_loc_h", (4, 256, TOK), BF16); h_all = c.scratch("h_all", (4, 4 * 256, TOK), BF16)
    h_all_buf = Buf(h_all)
    g["w_rg"] = EI("w_rg", (4, 128, 8, 128), F32)
    g["w_sbq"] = EI("w_sbq", (2, 128, 8, 128), F32); g["w_sbk"] = EI("w_sbk", (2, 128, 8, 128), F32)
    g["w_sbv"] = EI("w_sbv", (128, 8, 256), F32)
    pr_all_r_buf = Buf(pr_all_r); pr_all_a_buf = Buf(pr_all_a); o_all_buf = Buf(o_all)
    o_all2d = o_all.rearrange("a r (c t) -> (a r c) t", t=SG)
    for i in range(DEPTH + 1):
        post = i > 0; pre = PRE_KINDS[i] if i < DEPTH else None
        if pre in ("rglru", "plain"):
            pre = "hnorm"
        io = dict(x_in=(g["xT"] if i == 0 else x_scr), x_out=(g["xT_out"] if i == DEPTH else x_scr), ones_bf=g["ones_bf"],
                  final=(i == DEPTH), idx_o=g["idx_o"], o_all2d=o_all2d, o_all_buf=o_all_buf,
                  rope_scr=rope_scr, rope_build=(i == 0))
        if post:
            io.update({k: L[i - 1][k] for k in ("w_mo", "w_f1", "w_f2", "w_pg", "w_pp", "p_in", "gains_post")})
        if pre:
            io.update(gain_pre=L[i]["gain_pre"], w_pre=L[i]["w_pre"], pos=g["pos"], invf=g["invf"],
                      pr_loc=(pr_loc_r if pre == "rglru" else pr_loc_a))
            if pre == "hnorm":
                io["pr_loc"] = pr_loc_h
                io["ag_pr"] = lambda k: c.all_gather(pr_loc_h[k], h_all[k], h_all_buf)
            else:
                io["ag_pr"] = lambda k: c.all_gather(pr_loc_a[k], pr_all_a[k], pr_all_a_buf)
        emit_dense(c, post, pre, io)
        if not pre or done():
            break
        if pre == "hnorm":
            mio = dict(h_all=h_all, h_all_buf=h_all_buf)
        else:
            mio = dict(pr_all2d=pr_all_a.rearrange("a r t -> (a r) t"), pr_all_buf=pr_all_a_buf)
        mio["ag_o"] = lambda k: c.all_gather(o_loc[k], o_all[k], o_all_buf)
        if done():
            break
        mio.update(g); mio["o_loc"] = o_loc
        if i == 0:
            emit_rglru(c, mio)
        elif i == 1:
            emit_softmax_attn(c, mio, "dil")
        elif i == 2:
            emit_sb(c, mio)
        else:
            emit_softmax_attn(c, mio, "diff", LAM_INIT3)
        if done():
            break
        if done():
            break
    c.barrier()
    c.finish(c.out_tickets)
    return c.nc


def _tok(cc):
    b, j = divmod(cc, 4)
    return b, slice(j * TOK, (j + 1) * TOK)


def _rope_cols():
    cols = []
    for jj in range(16):
        base = jj * 128
        e = np.arange(128)
        cols.append(np.arange(base, base + 128))
        cols.append(base + (e // 64) * 64 + ((e % 64) + 32) % 64)
    cols.append(np.arange(2048, 3072))
    return np.concatenate(cols)


def _mult(dist):
    m = np.zeros(dist.shape, np.float32)
    for dd in (1, 4, 16):
        m += ((dist >= 0) & (dist % dd == 0) & (dist <= 128 * dd)).astype(np.float32)
    return m


def kernel(**inp):
    inp = {k: np.asarray(v) for k, v in inp.items()}
    x = inp["x"]; p = inp["p"]; pos = inp["positions"].astype(np.int32)
    pp = np.arange(128)[:, None]; ff = np.arange(512)[None, :]
    sh = {"ones_bf": np.ones((128, 128), NPBF), "ones_f": np.ones((128, 128), np.float32),
          "ident_bf": np.eye(128, dtype=np.float32).astype(NPBF)}
    invf = np.zeros((128, 2), np.float32)
    e = np.arange(128) % 64
    invf[:, 0] = (np.float32(10000.0) ** (-(np.arange(0, 64, 2, dtype=np.float32)) / np.float32(64)))[e % 32]
    invf[:, 1] = np.where(e < 32, -1.0, 1.0)
    sh["invf"] = invf
    f1cols = np.concatenate([np.concatenate([np.arange(cc * 128, (cc + 1) * 128), FF + np.arange(cc * 128, (cc + 1) * 128)]) for cc in range(22)])
    rcols = _rope_cols()
    mix_out_w = [inp["a_w_out"][0], inp["b_w_out"][0], inp["c_w_out"][0], inp["d_w_out"][0]]
    pre_w = [w_chunks(inp["a_w_in"][0]), w_chunks(inp["b_w_qkv"][0], rcols), w_chunks(inp["c_w_qkv"][0]), w_chunks(inp["d_w_qkv"][0], rcols)]
    for l in range(DEPTH):
        sh[f"w_mo{l}"] = w_chunks(mix_out_w[l]); sh[f"w_f1{l}"] = w_chunks(inp["w_ffn_in"][l], f1cols)
        sh[f"w_f2{l}"] = w_chunks(inp["w_ffn_out"][l]); sh[f"w_pg{l}"] = w_chunks(inp["w_ple_gate"][l])
        sh[f"w_pp{l}"] = w_chunks(inp["w_ple_proj"][l])
        sh[f"gains_post{l}"] = np.ascontiguousarray(np.stack([col_vec(inp[n][l]) for n in
                                                             ("ln_mix_post", "ln_ffn_pre", "ln_ffn_post", "ln_ple", "b_ple_gate")], axis=1))
        sh[f"gain_pre{l}"] = col_vec(inp["ln_mix_pre"][l]); sh[f"w_pre{l}"] = pre_w[l]
    sh["dil_masks"] = np.ascontiguousarray(np.stack([_mult((128 * mi - 384) + ff - pp) for mi in range(20)], axis=1)).astype(NPBF)
    sh["diff_masks"] = np.ascontiguousarray(np.stack([((ff - pp - 128 * j) >= 0).astype(np.float32) for j in range(4)], axis=1)).astype(NPBF)
    cst = np.zeros((128, 18, 128), np.float32)
    cst[:, 0, :] = -1.0 * (pp >= np.arange(128)[None, :]); cst[:, 1, :] = 1.0
    for o in range(4):
        cst[:, 2 + 4 * o:6 + 4 * o, :] = ((128 * o + pp) < ff).astype(np.float32).reshape(128, 4, 128)
    sh["sb_cst"] = cst.astype(NPBF)
    sh["lam"] = np.ascontiguousarray(np.stack([inp[n][0] for n in ("d_lambda_q1", "d_lambda_k1", "d_lambda_q2", "d_lambda_k2")])[None])
    sh["subln"] = np.ascontiguousarray(inp["d_subln"][0].reshape(128, 1))
    pa = np.arange(128)
    maps = []
    for cc in range(NCORE):
        b, ts = _tok(cc)
        r = cc % 4
        m = dict(sh)
        m["xT"] = fm(x[b, ts]); m["pos"] = np.ascontiguousarray(pos[b:b + 1, ts])
        for l in range(DEPTH):
            m[f"pT{l}"] = fm(p[l, b, ts])
        cw = np.zeros((128, 2, 8), np.float32); gw = np.zeros((2, 2, 128, 128), np.float32)
        for ct in range(2):
            cs = slice(r * 256 + ct * 128, r * 256 + (ct + 1) * 128)
            cw[:, ct, 0:4] = inp["a_conv_w"][0][:, cs].T
            cw[:, ct, 4] = inp["a_conv_b"][0][cs]; cw[:, ct, 5] = inp["a_gate_r_b"][0][cs]
            cw[:, ct, 6] = inp["a_gate_i_b"][0][cs]; cw[:, ct, 7] = inp["a_lambda"][0][cs]
            for gi, nm in enumerate(("a_gate_r_w", "a_gate_i_w")):
                for hh in range(2):
                    n = r * 4 + ct * 2 + hh
                    gw[ct, gi, hh * 64:(hh + 1) * 64, hh * 64:(hh + 1) * 64] = inp[nm][0][n]
        m["cw"] = cw; m["gw"] = gw
        m["w_rg"] = np.ascontiguousarray(pre_w[0][[2 * r, 2 * r + 1, 8 + 2 * r, 8 + 2 * r + 1]])
        Wc = inp["c_w_qkv"][0]
        e64 = np.arange(64)
        slot_cols = lambda base, s_: np.concatenate([base + (4 * r + s_) * 64 + e64, base + (4 * r + 2 + s_) * 64 + e64])
        m["w_sbq"] = np.ascontiguousarray(np.stack([w_chunks(Wc, slot_cols(0, s_))[0] for s_ in range(2)]))
        m["w_sbk"] = np.ascontiguousarray(np.stack([w_chunks(Wc, slot_cols(1024, s_))[0] for s_ in range(2)]))
        m["w_sbv"] = np.ascontiguousarray(Wc[:, 2048 + r * 256:2048 + (r + 1) * 256].reshape(8, 128, 256).transpose(1, 0, 2))
        io_ = np.zeros((128, 16), np.uint32)
        for sg in range(2):
            for k in range(8):
                rl = (k % 2) * 128 + pa
                io_[:, sg * 8 + k] = ((((rl // 64) * 4 + k // 2) * 64 + rl % 64) * 4 + r) * 2 + sg
        m["idx_o"] = io_
        ia = np.zeros((128, 24), np.uint32)
        for s_ in range(2):
            for j in range(4):
                hl = 2 * (pa // 64) + s_
                ia[:, s_ * 4 + j] = ((r * 3 + 0) * 4 + j) * 256 + hl * 64 + pa % 64
                ia[:, 8 + s_ * 4 + j] = ((r * 3 + 1) * 4 + j) * 256 + hl * 64 + pa % 64
        for vc in range(2):
            for j in range(4):
                ia[:, 16 + vc * 4 + j] = ((r * 3 + 2) * 4 + j) * 256 + vc * 128 + pa
        m["idx_a"] = ia
        ir = np.zeros((128, 16), np.uint32)
        for tc_ in range(4):
            for j in range(4):
                ir[:, tc_ * 4 + j] = ((r * 4 + tc_) * 4 + j) * 128 + pa
        m["idx_r"] = ir
        maps.append(m)
    nc = build_fused(STOP)
    res = run_bass_kernel_spmd(nc, maps, core_ids=list(range(NCORE))).results
    out = np.zeros((B, S, D), np.float32)
    for cc in range(NCORE):
        b, ts = _tok(cc)
        out[b, ts] = res[cc]["xT_out"].T
    return out


STOP = 999
```

```python
import math
import numpy as np
import ml_dtypes
import concourse.bass as bass
import concourse.mybir as mybir
from concourse.bass_utils import run_bass_kernel_spmd

F32, BF16, I32 = mybir.dt.float32, mybir.dt.bfloat16, mybir.dt.int32
AF = mybir.ActivationFunctionType
ALU = mybir.AluOpType
NPBF = ml_dtypes.bfloat16

D = 1024; B = 2; S = 8192; DEPTH = 4; FF = 2816; PLE = 256; HD = 64; NH = 16
NCORE = 8; TOK = 2048
SG = 1024; TG = 512
EPS = 1e-6


class Buf:
    __slots__ = ("ap", "w", "r", "dsem", "dcnt", "excl")

    def __init__(self, ap, excl=False):
        self.ap = ap; self.w = None; self.r = {}; self.dsem = None; self.dcnt = 0; self.excl = excl

    def __getitem__(self, k):
        return self.ap[k]


class Ctx:
    NDSEM = 40

    def __init__(self):
        self.nc = bass.Bass("TRN2", target_bir_lowering=False)
        nc = self.nc
        self.E = {"pe": nc.tensor, "act": nc.scalar, "dve": nc.vector, "pool": nc.gpsimd, "sp": nc.sync}
        self.sem = {k: nc.semaphore("s_" + k).__enter__() for k in self.E}
        self.cnt = {k: 0 for k in self.E}
        self.waited = {}
        self.nbuf = 0
        self.out_tickets = []
        self.dpool = [[nc.semaphore(f"dq{i}").__enter__(), 0] for i in range(self.NDSEM)]
        self.dpool_i = 0
        self.cc_sem = nc.semaphore("cc").__enter__(); self.cc_cnt = 0
        self.live = []

    def sb(self, shape, dt, name=None):
        self.nbuf += 1
        cm = self.nc.sbuf_tensor(f"{name or 'sb'}_{self.nbuf}", list(shape), dt)
        self.live.append(cm)
        return Buf(cm.__enter__())

    def ps(self, shape=(128, 512), dt=F32, name=None):
        self.nbuf += 1
        cm = self.nc.psum_tensor(f"{name or 'ps'}_{self.nbuf}", list(shape), dt)
        self.live.append(cm)
        return Buf(cm.__enter__(), excl=True)

    def dram(self, name, shape, dt, kind):
        return self.nc.dram_tensor(name, list(shape), dt, kind=kind).ap()

    def scratch(self, name, shape, dt):
        return self.nc.dram_tensor(name, list(shape), dt).ap()

    def _wait(self, eng, tickets):
        for (sem, val, src) in tickets:
            if src == eng:
                continue
            key = (eng, id(sem))
            if self.waited.get(key, 0) < val:
                self.E[eng].wait_ge(sem, val)
                self.waited[key] = val

    def _deps(self, reads, writes):
        tk = []
        for b in reads:
            if b.w is not None:
                tk.append(b.w)
            if b.excl:
                tk.extend(b.r.values())
        for b in writes:
            if b.w is not None:
                tk.append(b.w)
            tk.extend(b.r.values())
        return tk

    def _commit(self, t, reads, writes):
        for b in reads:
            b.r[(id(t[0]), t[2])] = t
        for b in writes:
            b.w = t; b.r = {}

    def op(self, eng, fn, reads=(), writes=()):
        self._wait(eng, self._deps(reads, writes))
        ins = fn()
        self.cnt[eng] += 1
        ins.then_inc(self.sem[eng], 1)
        t = (self.sem[eng], self.cnt[eng], eng)
        self._commit(t, reads, writes)
        return t

    def dma(self, q, out, in_, reads=(), writes=(), track=None):
        self._wait(q, self._deps(reads, writes))
        b = track
        if b.dsem is None:
            assert self.dpool_i < self.NDSEM, "out of dma semaphores in this phase"
            b.dsem = self.dpool[self.dpool_i]; self.dpool_i += 1
        ins = self.E[q].dma_start(out=out, in_=in_)
        b.dsem[1] += 16
        ins.then_inc(b.dsem[0], 16)
        t = (b.dsem[0], b.dsem[1], "dma")
        self._commit(t, reads, writes)
        return t

    def gather(self, out, src2d, idx_col, out_buf, read_bufs):
        self._wait("pool", self._deps(read_bufs, (out_buf,)))
        b = out_buf
        if b.dsem is None:
            assert self.dpool_i < self.NDSEM, "out of dma semaphores in this phase"
            b.dsem = self.dpool[self.dpool_i]; self.dpool_i += 1
        ins = self.nc.gpsimd.indirect_dma_start(out=out, out_offset=None, in_=src2d,
                                                in_offset=bass.IndirectOffsetOnAxis(ap=idx_col, axis=0))
        b.dsem[1] += 16
        ins.then_inc(b.dsem[0], 16)
        t = (b.dsem[0], b.dsem[1], "dma")
        self._commit(t, read_bufs, (out_buf,))
        return t

    def all_gather(self, in_ap, out_ap, out_buf):
        self._wait("pool", self._deps((), (out_buf,)))
        ins = self.nc.gpsimd.collective_compute("AllGather", ALU.bypass, replica_groups=[[0, 1, 2, 3], [4, 5, 6, 7]],
                                                ins=[in_ap.opt()], outs=[out_ap.opt()])
        self.cc_cnt += 1
        ins.then_inc(self.cc_sem)
        t = (self.cc_sem, self.cc_cnt, "cc")
        self._commit(t, (), (out_buf,))
        return t

    def barrier(self):
        tk = [(self.sem[e], self.cnt[e], e) for e in self.E if self.cnt[e] > 0]
        tk += [(d[0], d[1], "dma") for d in self.dpool if d[1] > 0]
        if self.cc_cnt:
            tk.append((self.cc_sem, self.cc_cnt, "cc"))
        for e in self.E:
            self._wait(e, tk)

    def end_phase(self):
        self.barrier()
        for cm in reversed(self.live):
            cm.__exit__(None, None, None)
        self.live = []
        self.dpool_i = 0

    def finish(self, tickets):
        self._wait("sp", tickets)


def run_pipeline(items, skews):
    n = len(items)
    for step in range(n + max(skews)):
        for k, sk in enumerate(skews):
            idx = step - sk
            if 0 <= idx < n:
                items[idx][k]()


class Rot:
    def __init__(self, bufs):
        self.bufs = bufs; self.i = 0

    def next(self):
        b = self.bufs[self.i % len(self.bufs)]; self.i += 1
        return b


def w_chunks(W, cols=None):
    if cols is not None:
        W = W[:, cols]
    Din, Dout = W.shape
    K, J = Din // 128, Dout // 128
    return np.ascontiguousarray(W.reshape(K, 128, J, 128).transpose(2, 1, 0, 3))


def col_vec(v):
    return np.ascontiguousarray(v.reshape(-1, 128).T)


def fm(x2d):
    return np.ascontiguousarray(x2d.T)


class Dense:
    def __init__(self, c, post, pre, io):
        self.c = c
        self.post = post; self.pre = pre
        self.io = io
        self.piece_tix = {}
        self.acc_pending = []
        self.ag_queue = []
        self.x_in = io["x_in"]; self.x_out = io["x_out"]; self.ones_d = io["ones_bf"]
        if post:
            for k in ("w_mo", "w_f1", "w_f2", "w_pg", "w_pp", "p_in", "gains_post"):
                setattr(self, k, io[k])
        if pre:
            self.gain_pre = io["gain_pre"]; self.w_pre = io["w_pre"]
            self.npre = {"rglru": 16, "rope": 40, "plain": 24, "hnorm": 0}[pre]
            self.pre_dt = F32 if pre == "rglru" else BF16
            self.pos = io["pos"]; self.invf = io["invf"]
            self.pr_out = io["pr_loc"]

    def build(self):
        c = self.c; nc = c.nc
        self.xT = c.sb((128, 8, SG), F32, "xT_sb")
        self.aT = c.sb((128, 8, SG), BF16, "aT_sb")
        self.yT = c.sb((128, 8, SG), F32, "yT_sb")
        self.ones = c.sb((128, 128), BF16, "ones_sb")
        self.wrot = Rot([c.sb((128, 22 * 128), BF16, f"w{i}") for i in range(4)])
        self.prot = Rot([c.ps(name=f"pb{i}") for i in range(6)])
        self.ssb = [c.ps(name=f"ss{i}") for i in range(2)]
        self.sqrot = Rot([c.sb((128, TG), BF16, f"sq{i}") for i in range(6)])
        self.rstd = [c.sb((128, TG), F32, f"rstd{i}") for i in range(2)]
        self.tmprot = Rot([c.sb((128, TG), F32, f"tmp{i}") for i in range(4)])
        self.strot = Rot([c.sb((128, TG), self.pre_dt if self.pre else F32, f"st{i}") for i in range(4)])
        c.dma("sp", self.ones[:], self.ones_d[:], writes=[self.ones], track=self.ones)
        if self.post:
            self.oidx = c.sb((128, 16), mybir.dt.uint32, "oidx")
            c.dma("sp", self.oidx[:], self.io["idx_o"], writes=[self.oidx], track=self.oidx)
            self.oB = c.sb((128, 8, SG), BF16, "oB_sb")
            self.gT = c.sb((128, 22, SG), BF16, "gT_sb")
            self.pT = c.sb((128, 2, SG), BF16, "pT_sb")
            self.gp = c.sb((128, 5, 8), F32, "gp_sb")
            c.dma("sp", self.gp[:], self.gains_post[:], writes=[self.gp], track=self.gp)
        if self.pre:
            self.gpre = c.sb((128, 8), F32, "gpre_sb")
            c.dma("sp", self.gpre[:], self.gain_pre[:], writes=[self.gpre], track=self.gpre)
        if self.pre == "rope":
            rs = self.io["rope_scr"]
            self.cosT = c.sb((128, TOK), F32, "cosT"); self.sinT = c.sb((128, TOK), F32, "sinT")
            c.dma("sp", self.cosT[:], rs[0], writes=[self.cosT], track=self.cosT)
            c.dma("sp", self.sinT[:], rs[1], writes=[self.sinT], track=self.sinT)
        for sg in range(TOK // SG):
            self.run_sg(sg)
        if self.io.get("rope_build"):
            self.build_rope_tables()
            rs = self.io["rope_scr"]
            c.dma("sp", rs[0], self.cosT[:], reads=[self.cosT], track=self.cosT)
            c.dma("sp", rs[1], self.sinT[:], reads=[self.sinT], track=self.sinT)

    def build_rope_tables(self):
        c = self.c; nc = c.nc
        self.cosT = c.sb((128, TOK), F32, "cosT"); self.sinT = c.sb((128, TOK), F32, "sinT")
        CW = 512
        posi = c.sb((128, CW), I32, "posi"); ang = c.sb((128, CW), F32, "ang")
        kf = c.sb((128, CW), F32, "kf"); ki = c.sb((128, CW), I32, "ki"); m = c.sb((128, CW), F32, "rm")
        inv = c.sb((128, 2), F32, "invf_sb")
        c.dma("sp", inv[:], self.invf[:], writes=[inv], track=inv)
        V = nc.vector; G = nc.gpsimd
        TWO_PI = 2.0 * math.pi
        C1 = 6.28125; C2 = TWO_PI - C1

        def wrap(t):
            c.op("pool", lambda: G.tensor_scalar(out=m[:], in0=t[:], scalar1=math.pi, scalar2=-TWO_PI, op0=ALU.is_gt, op1=ALU.mult), [t], [m])
            c.op("dve", lambda: V.tensor_tensor(out=t[:], in0=t[:], in1=m[:], op=ALU.add), [t, m], [t])
            c.op("pool", lambda: G.tensor_scalar(out=m[:], in0=t[:], scalar1=-math.pi, scalar2=TWO_PI, op0=ALU.is_lt, op1=ALU.mult), [t], [m])
            c.op("dve", lambda: V.tensor_tensor(out=t[:], in0=t[:], in1=m[:], op=ALU.add), [t, m], [t])

        for ch in range(TOK // CW):
            sl = slice(ch * CW, (ch + 1) * CW)
            c.dma("sp", posi[:], self.pos[:, sl].partition_broadcast(128), writes=[posi], track=posi)
            c.op("dve", lambda: V.tensor_copy(out=ang[:], in_=posi[:]), [posi], [ang])
            c.op("pool", lambda: G.tensor_scalar(out=ang[:], in0=ang[:], scalar1=inv[:, 0:1], scalar2=None, op0=ALU.mult), [ang, inv], [ang])
            c.op("dve", lambda: V.tensor_scalar(out=kf[:], in0=ang[:], scalar1=1.0 / TWO_PI, scalar2=0.5, op0=ALU.mult, op1=ALU.add), [ang], [kf])
            c.op("pool", lambda: G.tensor_copy(out=ki[:], in_=kf[:]), [kf], [ki])
            c.op("dve", lambda: V.tensor_copy(out=kf[:], in_=ki[:]), [ki], [kf])
            c.op("dve", lambda: V.scalar_tensor_tensor(out=ang[:], in0=kf[:], scalar=-C1, in1=ang[:], op0=ALU.mult, op1=ALU.add), [kf, ang], [ang])
            c.op("dve", lambda: V.scalar_tensor_tensor(out=ang[:], in0=kf[:], scalar=-C2, in1=ang[:], op0=ALU.mult, op1=ALU.add), [kf, ang], [ang])
            wrap(ang); wrap(ang)
            c.op("act", lambda: nc.scalar.activation(out=self.sinT[:, sl], in_=ang[:], func=AF.Sin), [ang], [self.sinT])
            c.op("pool", lambda: G.tensor_scalar(out=self.sinT[:, sl], in0=self.sinT[:, sl], scalar1=inv[:, 1:2], scalar2=None, op0=ALU.mult), [self.sinT, inv], [self.sinT])
            c.op("dve", lambda: V.tensor_scalar(out=ang[:], in0=ang[:], scalar1=math.pi / 2, scalar2=None, op0=ALU.add), [ang], [ang])
            wrap(ang)
            c.op("act", lambda: nc.scalar.activation(out=self.cosT[:, sl], in_=ang[:], func=AF.Sin), [ang], [self.cosT])

    def proj(self, wd, J, K, act, evac):
        c = self.c; nc = c.nc
        for j in range(J):
            wt = self.wrot.next()
            wv = wt[:, 0:K * 128]
            c.dma("pool", wv, wd[j].rearrange("p k m -> p (k m)"), writes=[wt], track=wt)
            for tg in range(SG // TG):
                bank = self.prot.next()
                for k in range(K):
                    c.op("pe", lambda k=k: nc.tensor.matmul(bank[:], lhsT=wt[:, k * 128:(k + 1) * 128],
                                                             rhs=act[:, k, tg * TG:(tg + 1) * TG],
                                                             start=(k == 0), stop=(k == K - 1)),
                         [wt, act], [bank])
                evac(j, tg, bank)

    def stats(self, src):
        c = self.c; nc = c.nc
        for tg in range(SG // TG):
            for k in range(8):
                sq = self.sqrot.next()
                c.op("act", lambda: nc.scalar.activation(out=sq[:], in_=src[:, k, tg * TG:(tg + 1) * TG], func=AF.Square), [src], [sq])
                c.op("pe", lambda: nc.tensor.matmul(self.ssb[tg][:], lhsT=self.ones[:], rhs=sq[:], start=(k == 0), stop=(k == 7)),
                     [self.ones, sq], [self.ssb[tg]])
            r = self.rstd[tg]
            c.op("act", lambda: nc.scalar.activation(out=r[:], in_=self.ssb[tg][:], func=AF.Sqrt, scale=1.0 / D, bias=self.epsb[:, 0:1]), [self.ssb[tg], self.epsb], [r])
            c.op("dve", lambda: nc.vector.reciprocal(out=r[:], in_=r[:]), [r], [r])

    def acc_stats(self, src_ap, srcbuf, j, tg):
        c = self.c; nc = c.nc
        sq = self.sqrot.next()
        c.op("act", lambda: nc.scalar.activation(out=sq[:], in_=src_ap, func=AF.Square), [srcbuf], [sq])
        self.acc_pending.append((sq, j, tg))
        while len(self.acc_pending) > 3:
            self._acc_mm(*self.acc_pending.pop(0))

    def _acc_mm(self, sq, j, tg):
        c = self.c; nc = c.nc
        c.op("pe", lambda: nc.tensor.matmul(self.ssb[tg][:], lhsT=self.ones[:], rhs=sq[:], start=(j == 0), stop=(j == 7)),
             [self.ones, sq], [self.ssb[tg]])

    def finish_stats(self):
        c = self.c; nc = c.nc
        while self.acc_pending:
            self._acc_mm(*self.acc_pending.pop(0))
        for tg in range(SG // TG):
            r = self.rstd[tg]
            c.op("act", lambda: nc.scalar.activation(out=r[:], in_=self.ssb[tg][:], func=AF.Sqrt, scale=1.0 / D, bias=self.epsb[:, 0:1]), [self.ssb[tg], self.epsb], [r])
            c.op("dve", lambda: nc.vector.reciprocal(out=r[:], in_=r[:]), [r], [r])

    def norm_add(self, gi):
        c = self.c; nc = c.nc
        self.finish_stats()
        for tg in range(SG // TG):
            sl = slice(tg * TG, (tg + 1) * TG)
            for k in range(8):
                t = self.tmprot.next()
                c.op("dve", lambda: nc.vector.scalar_tensor_tensor(out=t[:], in0=self.yT[:, k, sl], scalar=self.gp[:, gi, k:k + 1],
                                                                   in1=self.rstd[tg][:], op0=ALU.mult, op1=ALU.mult),
                     [self.yT, self.gp, self.rstd[tg]], [t])
                c.op("dve", lambda: nc.vector.tensor_tensor(out=self.xT[:, k, sl], in0=self.xT[:, k, sl], in1=t[:], op=ALU.add),
                     [self.xT, t], [self.xT])
                self.acc_stats(self.xT[:, k, sl], self.xT, k, tg)

    def norm_to_a(self, gains, gi=None, have_stats=False):
        c = self.c; nc = c.nc
        if have_stats:
            self.finish_stats()
        else:
            self.stats(self.xT)
        for tg in range(SG // TG):
            sl = slice(tg * TG, (tg + 1) * TG)
            for k in range(8):
                g = gains[:, gi, k:k + 1] if gi is not None else gains[:, k:k + 1]
                c.op("dve", lambda: nc.vector.scalar_tensor_tensor(out=self.aT[:, k, sl], in0=self.xT[:, k, sl], scalar=g,
                                                                   in1=self.rstd[tg][:], op0=ALU.mult, op1=ALU.mult),
                     [self.xT, gains, self.rstd[tg]], [self.aT])

    def run_sg(self, sg):
        c = self.c; nc = c.nc
        t0 = sg * SG
        if sg == 0:
            self.epsb = c.sb((128, 1), F32, "epsb")
            c.op("pool", lambda: nc.gpsimd.memset(self.epsb[:], EPS), [], [self.epsb])
        c.dma("sp", self.xT[:], self.x_in[:, t0:t0 + SG].rearrange("(k p) t -> p k t", p=128), writes=[self.xT], track=self.xT)
        if self.post:
            if sg == 0:
                for k in range(8):
                    c.gather(self.aT[:, k, :], self.io["o_all2d"], self.oidx[:, k:k + 1], self.aT, [self.io["o_all_buf"], self.oidx])
            c.dma("pool", self.pT[:], self.p_in[:, t0:t0 + SG].rearrange("(k p) t -> p k t", p=128), writes=[self.pT], track=self.pT)

            def evac_y(j, tg, bank):
                c.op("act", lambda: nc.scalar.copy(out=self.yT[:, j, tg * TG:(tg + 1) * TG], in_=bank[:]), [bank], [self.yT])
                self.acc_stats(bank[:], bank, j, tg)

            self.proj(self.w_mo, 8, 8, (self.aT if sg == 0 else self.oB), evac_y)
            if sg == 0:
                for k in range(8):
                    c.gather(self.oB[:, k, :], self.io["o_all2d"], self.oidx[:, 8 + k:8 + k + 1], self.oB, [self.io["o_all_buf"], self.oidx])
            self.norm_add(0)
            self.norm_to_a(self.gp, 1, have_stats=True)
            pend = {}

            def evac_f1(j, tg, bank):
                cch, half = divmod(j, 2)
                if half == 0:
                    pend[tg] = bank
                    return
                b1 = pend.pop(tg)
                t = self.tmprot.next()
                c.op("act", lambda: nc.scalar.activation(out=t[:], in_=b1[:], func=AF.Silu), [b1], [t])
                c.op("dve", lambda: nc.vector.tensor_tensor(out=self.gT[:, cch, tg * TG:(tg + 1) * TG], in0=t[:], in1=bank[:], op=ALU.mult),
                     [t, bank], [self.gT])

            self.proj(self.w_f1, 44, 8, self.aT, evac_f1)
            self.proj(self.w_f2, 8, 22, self.gT, evac_y)
            self.norm_add(2)
            for k in range(8):
                c.op("dve", lambda: nc.vector.tensor_copy(out=self.aT[:, k, :], in_=self.xT[:, k, :]), [self.xT], [self.aT])

            def evac_gate(j, tg, bank):
                c.op("act", lambda: nc.scalar.activation(out=self.yT[:, j, tg * TG:(tg + 1) * TG], in_=bank[:], func=AF.Sigmoid,
                                                         bias=self.gp[:, 4, j:j + 1]), [bank, self.gp], [self.yT])

            self.proj(self.w_pg, 8, 8, self.aT, evac_gate)

            def evac_pp(j, tg, bank):
                sl = slice(tg * TG, (tg + 1) * TG)
                c.op("dve", lambda: nc.vector.tensor_tensor(out=self.yT[:, j, sl], in0=self.yT[:, j, sl], in1=bank[:], op=ALU.mult),
                     [self.yT, bank], [self.yT])
                self.acc_stats(self.yT[:, j, sl], self.yT, j, tg)

            self.proj(self.w_pp, 8, 2, self.pT, evac_pp)
            self.norm_add(3)
        t = c.dma("sp", self.x_out[:, t0:t0 + SG].rearrange("(k p) t -> p k t", p=128), self.xT[:], reads=[self.xT], track=self.xT)
        if self.io.get("final"):
            c.out_tickets.append(t)
        if not self.pre:
            return
        self.norm_to_a(self.gpre, have_stats=self.post)
        pr = self.pr_out
        if self.pre == "hnorm":
            for a in range(4):
                tk = c.dma("sp", pr[a, :, t0:t0 + SG].rearrange("(h p) t -> p h t", p=128), self.aT[:, 2 * a:2 * a + 2, :], reads=[self.aT], track=self.aT)
            if sg == TOK // SG - 1:
                c._wait("pool", [tk])
                for k in range(4):
                    self.io["ag_pr"](k)
            return

        def store(st, jo, tg):
            typ, hc = divmod(jo, 8)
            g, half = divmod(hc, 2)
            if self.pre == "rglru":
                dst = pr[g * 4 + typ * 2 + half, :, t0 + tg * TG:t0 + (tg + 1) * TG]
            else:
                dst = pr[g * 3 + typ, half * 128:(half + 1) * 128, t0 + tg * TG:t0 + (tg + 1) * TG]
            tk = c.dma("sp", dst, st[:], reads=[st], track=st)
            if sg == TOK // SG - 1:
                piece = (g * 4 + typ * 2 + half) if self.pre == "rglru" else (g * 3 + typ)
                lst = self.piece_tix.setdefault(piece, [])
                lst.append(tk)
                if len(lst) == (2 if self.pre == "rglru" else 4):
                    self.ag_queue.append((piece, lst))
                    while len(self.ag_queue) > 2:
                        pk, l = self.ag_queue.pop(0)
                        c._wait("pool", l); self.io["ag_pr"](pk)

        if self.pre == "rglru":
            def evac(j, tg, bank):
                st = self.strot.next()
                c.op("act", lambda: nc.scalar.copy(out=st[:], in_=bank[:]), [bank], [st])
                store(st, j, tg)
        elif self.pre == "plain":
            def evac(j, tg, bank):
                st = self.strot.next()
                c.op("act", lambda: nc.scalar.activation(out=st[:], in_=bank[:], func=AF.Copy, scale=(0.125 if j < 8 else 1.0)), [bank], [st])
                store(st, j, tg)
        else:
            pend = {}

            def evac(j, tg, bank):
                if j >= 32:
                    st = self.strot.next()
                    c.op("act", lambda: nc.scalar.copy(out=st[:], in_=bank[:]), [bank], [st])
                    store(st, j - 16, tg)
                    return
                jj, var = divmod(j, 2)
                if var == 0:
                    pend[tg] = bank
                    return
                b1 = pend.pop(tg)
                sc = 0.125 if jj < 8 else 1.0
                tsl = slice(t0 + tg * TG, t0 + (tg + 1) * TG)
                t1 = self.tmprot.next(); t2 = self.tmprot.next(); st = self.strot.next()
                c.op("dve", lambda: nc.vector.scalar_tensor_tensor(out=t1[:], in0=b1[:], scalar=sc, in1=self.cosT[:, tsl], op0=ALU.mult, op1=ALU.mult),
                     [b1, self.cosT], [t1])
                c.op("dve", lambda: nc.vector.scalar_tensor_tensor(out=t2[:], in0=bank[:], scalar=sc, in1=self.sinT[:, tsl], op0=ALU.mult, op1=ALU.mult),
                     [bank, self.sinT], [t2])
                c.op("dve", lambda: nc.vector.tensor_tensor(out=st[:], in0=t1[:], in1=t2[:], op=ALU.add), [t1, t2], [st])
                store(st, jj, tg)
        self.proj(self.w_pre, self.npre, 8, self.aT, evac)
        while self.ag_queue:
            pk, l = self.ag_queue.pop(0)
            c._wait("pool", l); self.io["ag_pr"](pk)


def emit_dense(c, post, pre, io):
    Dense(c, post, pre, io).build()
    c.end_phase()


def emit_rglru(c, io):
    nc = c.nc
    CH = 2048
    cw = io["cw"]; gw = io["gw"]; o_loc = io["o_loc"].rearrange("a p s -> (a p) s")
    V = nc.vector; G = nc.gpsimd; A = nc.scalar
    cws = c.sb((128, 2, 8), F32, "cws"); c.dma("sp", cws[:], cw, writes=[cws], track=cws)
    gws = c.sb((128, 4, 128), BF16, "gws")
    c.dma("pool", gws[:], gw.rearrange("a b p m -> p (a b) m"), writes=[gws], track=gws)
    ridx = c.sb((128, 16), mybir.dt.uint32, "ridx"); c.dma("sp", ridx[:], io["idx_r"], writes=[ridx], track=ridx)
    wg = c.sb((128, 4, 8 * 128), BF16, "wg")
    c.dma("pool", wg[:], io["w_rg"].rearrange("c p k m -> p c (k m)"), writes=[wg], track=wg)
    hall = io["h_all"]; hallb = io["h_all_buf"]
    hcr = Rot([c.sb((128, 8, CH), BF16, f"hc{i}") for i in range(2)])
    cs = c.sb((128, 2), F32, "cs")
    onec = c.sb((128, 1), F32, "onec")
    c.op("pool", lambda: G.memset(onec[:], 1.0), [], [onec])
    for ct in range(2):
        c.op("act", lambda: A.activation(out=cs[:, ct:ct + 1], in_=cws[:, ct, 7:8], func=AF.Exp, scale=-1.0), [cws], [cs])
        c.op("act", lambda: A.activation(out=cs[:, ct:ct + 1], in_=cs[:, ct:ct + 1], func=AF.Ln, bias=onec[:, 0:1]), [cs, onec], [cs])
    c.op("dve", lambda: V.tensor_scalar(out=cs[:], in0=cs[:], scalar1=-8.0, scalar2=None, op0=ALU.mult), [cs], [cs])
    xfull = c.sb((128, S + 3), F32, "xfull")
    yr = Rot([c.sb((128, CH), F32, f"y{i}") for i in range(2)])
    xc = c.sb((128, CH), F32, "xc"); xcb = c.sb((128, CH), BF16, "xcb")
    ra = c.sb((128, CH), F32, "ra"); mm = c.sb((128, CH), F32, "mm"); ib = c.sb((128, CH), F32, "ib")
    hh = c.sb((128, CH), F32, "hh"); tt = c.sb((128, CH), F32, "tt")
    orot = Rot([c.sb((128, CH), BF16, f"o{i}") for i in range(2)])
    carry = c.sb((128, 1), F32, "carry")
    prot = Rot([c.ps(name=f"pb{i}") for i in range(4)])
    for ct in range(2):
        rows = slice(ct * 128, (ct + 1) * 128)
        otix = []
        c.op("dve", lambda: V.memset(xfull[:, 0:3], 0.0), [], [xfull])
        for tc in range(S // CH):
            t0 = tc * CH
            y = yr.next()
            hc = hcr.next()
            for a in range(4):
                c.dma("sp", hc[:, 2 * a:2 * a + 2, :], hall[a, tc * 256:(tc + 1) * 256, :].rearrange("(h p) t -> p h t", p=128),
                      reads=[hallb], writes=[hc], track=hc)
            for typ in range(2):
                for sb_ in range(CH // 512):
                    bank = prot.next(); sl = slice(sb_ * 512, (sb_ + 1) * 512)
                    for k in range(8):
                        c.op("pe", lambda: nc.tensor.matmul(bank[:], lhsT=wg[:, typ * 2 + ct, k * 128:(k + 1) * 128], rhs=hc[:, k, sl],
                                                            start=(k == 0), stop=(k == 7)), [wg, hc], [bank])
                    if typ == 0:
                        c.op("act", lambda: A.copy(out=y[:, sl], in_=bank[:]), [bank], [y])
                    else:
                        c.op("act", lambda: A.copy(out=xfull[:, 3 + t0 + sb_ * 512:3 + t0 + (sb_ + 1) * 512], in_=bank[:]), [bank], [xfull])
            c.op("dve", lambda: V.tensor_scalar(out=xc[:], in0=xfull[:, t0:t0 + CH], scalar1=cws[:, ct, 0:1], scalar2=cws[:, ct, 4:5], op0=ALU.mult, op1=ALU.add), [xfull, cws], [xc])
            for tap in range(1, 4):
                c.op("dve", lambda: V.scalar_tensor_tensor(out=xc[:], in0=xfull[:, t0 + tap:t0 + tap + CH], scalar=cws[:, ct, tap:tap + 1], in1=xc[:], op0=ALU.mult, op1=ALU.add), [xfull, cws, xc], [xc])
            c.op("act", lambda: A.copy(out=xcb[:], in_=xc[:]), [xc], [xcb])
            for gi, dst in ((0, ra), (1, ib)):
                for sb_ in range(CH // 512):
                    bank = prot.next(); sl = slice(sb_ * 512, (sb_ + 1) * 512)
                    c.op("pe", lambda: nc.tensor.matmul(bank[:], lhsT=gws[:, ct * 2 + gi, :], rhs=xcb[:, sl], start=True, stop=True), [gws, xcb], [bank])
                    c.op("act", lambda: A.activation(out=dst[:, sl], in_=bank[:], func=AF.Sigmoid, bias=cws[:, ct, 5 + gi:6 + gi]), [bank, cws], [dst])
            c.op("act", lambda: A.activation(out=ra[:], in_=ra[:], func=AF.Exp, scale=cs[:, ct:ct + 1]), [ra, cs], [ra])
            c.op("pool", lambda: G.tensor_tensor(out=mm[:], in0=ra[:], in1=ra[:], op=ALU.mult), [ra], [mm])
            c.op("dve", lambda: V.tensor_scalar(out=mm[:], in0=mm[:], scalar1=-1.0, scalar2=1.0, op0=ALU.mult, op1=ALU.add), [mm], [mm])
            c.op("act", lambda: A.activation(out=mm[:], in_=mm[:], func=AF.Sqrt), [mm], [mm])
            c.op("dve", lambda: V.tensor_tensor(out=ib[:], in0=ib[:], in1=xc[:], op=ALU.mult), [ib, xc], [ib])
            c.op("dve", lambda: V.tensor_tensor(out=ib[:], in0=ib[:], in1=mm[:], op=ALU.mult), [ib, mm], [ib])
            if tc > 0:
                c.op("pool", lambda: G.tensor_tensor(out=carry[:], in0=ra[:, 0:1], in1=hh[:, CH - 1:CH], op=ALU.mult), [ra, hh], [carry])
                c.op("pool", lambda: G.tensor_tensor(out=ib[:, 0:1], in0=ib[:, 0:1], in1=carry[:], op=ALU.add), [ib, carry], [ib])
            c.op("dve", lambda: V.tensor_tensor_scan(out=hh[:], data0=ra[:], data1=ib[:], initial=0.0, op0=ALU.mult, op1=ALU.add),
                 [ra, ib], [hh])
            c.op("pool", lambda: G.tensor_tensor(out=tt[:], in0=y[:], in1=y[:], op=ALU.mult), [y], [tt])
            c.op("dve", lambda: V.tensor_scalar(out=tt[:], in0=tt[:], scalar1=0.044715, scalar2=1.0, op0=ALU.mult, op1=ALU.add), [tt], [tt])
            c.op("dve", lambda: V.tensor_tensor(out=tt[:], in0=tt[:], in1=y[:], op=ALU.mult), [tt, y], [tt])
            c.op("act", lambda: A.activation(out=tt[:], in_=tt[:], func=AF.Sigmoid, scale=2.0 * math.sqrt(2.0 / math.pi)), [tt], [tt])
            c.op("pool", lambda: G.tensor_tensor(out=tt[:], in0=tt[:], in1=y[:], op=ALU.mult), [tt, y], [tt])
            ob = orot.next()
            c.op("dve", lambda: V.tensor_tensor(out=ob[:], in0=hh[:], in1=tt[:], op=ALU.mult), [hh, tt], [ob])
            otix.append(c.dma("sp", o_loc[rows, t0:t0 + CH], ob[:], reads=[ob], track=ob))
        c._wait("pool", otix)
        io["ag_o"](2 * ct); io["ag_o"](2 * ct + 1)
    c.end_phase()


def attn_common(c, io, vcols, dil=False):
    nc = c.nc
    src = io["pr_all2d"]; srcb = io["pr_all_buf"]
    aidx = c.sb((128, 24), mybir.dt.uint32, "aidx"); c.dma("sp", aidx[:], io["idx_a"], writes=[aidx], track=aidx)
    ident = c.sb((128, 128), BF16, "ident"); c.dma("sp", ident[:], io["ident_bf"], writes=[ident], track=ident)
    qs = c.sb((128, 2, S), BF16, "qs"); ks = c.sb((128, 4, S), BF16, "kz"); vs = c.sb((128, S // 128, vcols + (0 if dil else 64)), BF16, "vs")
    vtr = Rot([c.sb((128, 2048), BF16, f"vT{i}") for i in range(2)])
    for h in range(4):
        c.op("pool" if h % 2 else "dve", lambda: (nc.gpsimd if h % 2 else nc.vector).memset(ks[:, h, :], 0.0), [], [ks])
    for s_ in range(2):
        for j in range(4):
            c.gather(qs[:, s_, j * 2048:(j + 1) * 2048], src, aidx[:, s_ * 4 + j:s_ * 4 + j + 1], qs, [srcb, aidx])
            kst = vtr.next()
            c.gather(kst[:], src, aidx[:, 8 + s_ * 4 + j:8 + s_ * 4 + j + 1], kst, [srcb, aidx])
            c.op("dve", lambda: nc.vector.tensor_copy(out=ks[0:64, s_, j * 2048:(j + 1) * 2048], in_=kst[0:64, :]), [kst], [ks])
            c.op("pool", lambda: nc.gpsimd.tensor_copy(out=ks[64:128, 2 + s_, j * 2048:(j + 1) * 2048], in_=kst[64:128, :]), [kst], [ks])
    if dil:
        c.op("dve", lambda: nc.vector.memset(vs[:], 1.0), [], [vs])
    tpr = Rot([c.ps((128, 1024), BF16, name="tp")])
    for vc in range(2):
        for j in range(4):
            vT = vtr.next()
            c.gather(vT[:], src, aidx[:, 16 + vc * 4 + j:16 + vc * 4 + j + 1], vT, [srcb, aidx])
            for q4 in range(4):
                tp = tpr.next()
                for b4 in range(4):
                    blk = q4 * 4 + b4
                    c.op("pe", lambda: nc.tensor.transpose(out=tp[:, b4 * 128:(b4 + 1) * 128], in_=vT[:, blk * 128:(blk + 1) * 128], identity=ident[:]),
                         [vT, ident], [tp])
                b0 = j * 16 + q4 * 4
                if dil:
                    dst = vs[:, b0:b0 + 4, vc * 130:(vc + 1) * 130].rearrange("p b (h c) -> p b h c", c=65)[:, :, :, 0:64]
                    srcp = tp[:, 0:512].rearrange("p (b h c) -> p b h c", b=4, h=2)
                else:
                    dst = vs[:, b0:b0 + 4, vc * 128:(vc + 1) * 128]
                    srcp = tp[:, 0:512].rearrange("p (b c) -> p b c", b=4)
                c.op("act", lambda: nc.scalar.copy(out=dst, in_=srcp), [tp], [vs])
    return qs, ks, vs


def sb_project(c, io):
    nc = c.nc; PE = nc.tensor; A = nc.scalar
    hall = io["h_all"]; hallb = io["h_all_buf"]
    qs = c.sb((128, 2, S), BF16, "qs"); ks = c.sb((128, 4, S), BF16, "kz"); vs = c.sb((128, S // 128, 320), BF16, "vs")
    wq = c.sb((128, 2, 1024), BF16, "wq"); wk = c.sb((128, 2, 1024), BF16, "wk"); wv = c.sb((128, 8, 256), BF16, "wv")
    c.dma("pool", wq[:], io["w_sbq"].rearrange("s p k m -> p s (k m)"), writes=[wq], track=wq)
    c.dma("pool", wk[:], io["w_sbk"].rearrange("s p k m -> p s (k m)"), writes=[wk], track=wk)
    c.dma("pool", wv[:], io["w_sbv"], writes=[wv], track=wv)
    for h in range(4):
        c.op("pool" if h % 2 else "dve", lambda: (nc.gpsimd if h % 2 else nc.vector).memset(ks[:, h, :], 0.0), [], [ks])
    hcr = Rot([c.sb((128, 8, 1024), BF16, f"hc{i}") for i in range(2)])
    pj = c.ps(name="pj")
    for jj in range(8):
        j, hf = divmod(jj, 2)
        hc = hcr.next()
        for a_ in range(4):
            c.dma("sp", hc[:, 2 * a_:2 * a_ + 2, :], hall[a_, j * 256:(j + 1) * 256, hf * 1024:(hf + 1) * 1024].rearrange("(h p) t -> p h t", p=128),
                  reads=[hallb], writes=[hc], track=hc)
        for sb_ in range(2):
            sl = slice(sb_ * 512, (sb_ + 1) * 512); tsl = slice(jj * 1024 + sb_ * 512, jj * 1024 + (sb_ + 1) * 512)
            for s_ in range(2):
                for k in range(8):
                    c.op("pe", lambda: PE.matmul(pj[:], lhsT=wq[:, s_, k * 128:(k + 1) * 128], rhs=hc[:, k, sl], start=(k == 0), stop=(k == 7)), [wq, hc], [pj])
                c.op("act", lambda: A.activation(out=qs[:, s_, tsl], in_=pj[:], func=AF.Copy, scale=0.125), [pj], [qs])
                for k in range(8):
                    c.op("pe", lambda: PE.matmul(pj[:], lhsT=wk[:, s_, k * 128:(k + 1) * 128], rhs=hc[:, k, sl], start=(k == 0), stop=(k == 7)), [wk, hc], [pj])
                c.op("act", lambda: A.copy(out=ks[0:64, s_, tsl], in_=pj[0:64, :]), [pj], [ks])
                c.op("act", lambda: A.copy(out=ks[64:128, 2 + s_, tsl], in_=pj[64:128, :]), [pj], [ks])
            for b4 in range(4):
                blk = jj * 8 + sb_ * 4 + b4; tk = slice(sb_ * 512 + b4 * 128, sb_ * 512 + (b4 + 1) * 128)
                for k in range(8):
                    c.op("pe", lambda: PE.matmul(pj[:, 0:256], lhsT=hc[:, k, tk], rhs=wv[:, k, :], start=(k == 0), stop=(k == 7)), [hc, wv], [pj])
                c.op("act", lambda: A.copy(out=vs[:, blk, 0:256], in_=pj[:, 0:256]), [pj], [vs])
    return qs, ks, vs


def head_ap(t, h):
    if t.ap.shape[1] == 4:
        return lambda sl: t[:, h, sl]
    return lambda sl: t[:, h % 2, sl]


def emit_sb(c, io):
    nc = c.nc
    V = nc.vector; G = nc.gpsimd; A = nc.scalar; PE = nc.tensor
    qs, ks, vs = sb_project(c, io)
    oT = io["o_loc"]
    cs_ = c.sb((128, 18, 128), BF16, "cst_sb"); c.dma("sp", cs_[:], io["sb_cst"], writes=[cs_], track=cs_)
    tri = cs_[:, 0, :]; ones = cs_[:, 1, :]
    onec = c.sb((128, 1), F32, "onec")
    c.op("pool", lambda: G.memset(onec[:], 1.0), [], [onec])
    zr = Rot([c.ps(name=f"z{i}") for i in range(2)])
    br = Rot([c.ps(name=f"b{i}") for i in range(2)])
    cr = Rot([c.ps(name=f"c{i}") for i in range(2)])
    ob = c.ps(name="o0")
    er = Rot([c.sb((128, 512), F32, f"e{i}") for i in range(2)])
    spr = Rot([c.sb((128, 512), BF16, f"sp{i}") for i in range(3)])
    t1r = Rot([c.sb((128, 512), F32, f"t1{i}") for i in range(2)])
    atr = Rot([c.sb((128, 512), BF16, f"at{i}") for i in range(3)])
    R = c.sb((128, 512), F32, "R")
    ost = Rot([c.sb((64, 512), BF16, f"ost{i}") for i in range(2)])
    items = []
    otix = {}
    for h in range(4):
        qh = head_ap(qs, h); kh = head_ap(ks, h)
        for qg in range(S // 512):
            qsl = slice(qg * 512, (qg + 1) * 512)
            nkb = 4 * qg + 4
            for i, kb in enumerate(range(nkb - 1, -1, -1)):
                def mk_item(h=h, qh=qh, kh=kh, qg=qg, qsl=qsl, nkb=nkb, i=i, kb=kb):
                    diag = kb >= 4 * qg
                    ksl = slice(kb * 128, (kb + 1) * 128)
                    zb = zr.next(); e = er.next(); sp = spr.next(); bb = br.next(); cb = cr.next(); t1 = t1r.next(); at = atr.next()
                    mk = cs_[:, 2 + 4 * (kb - 4 * qg):6 + 4 * (kb - 4 * qg), :].rearrange("p a b -> p (a b)") if diag else None
                    last = i == nkb - 1

                    def s1():
                        c.op("pe", lambda: PE.matmul(zb[:], lhsT=kh(ksl), rhs=qh(qsl), start=True, stop=True), [ks, qs], [zb])
                        c.op("act", lambda: A.activation(out=e[:], in_=zb[:], func=AF.Exp), [zb], [e])
                        c.op("act", lambda: A.activation(out=sp[:], in_=e[:], func=AF.Ln, bias=onec[:, 0:1]), [e, onec], [sp])
                        if diag:
                            c.op("pool", lambda: G.tensor_tensor(out=sp[:], in0=sp[:], in1=mk, op=ALU.mult), [sp, cs_], [sp])

                    def s2():
                        c.op("pe", lambda: PE.matmul(bb[:], lhsT=kh(ksl), rhs=qh(qsl), start=True, stop=False), [ks, qs], [bb])
                        c.op("pe", lambda: PE.matmul(bb[:], lhsT=tri, rhs=sp[:], start=False, stop=True), [cs_, sp], [bb])
                        if not last:
                            c.op("pe", lambda: PE.matmul(cb[:], lhsT=ones, rhs=sp[:], start=True, stop=True), [cs_, sp], [cb])

                    def s3a():
                        if i == 0:
                            c.op("pool", lambda: G.memset(R[:], 0.0), [], [R])
                        c.op("dve", lambda: V.tensor_tensor(out=t1[:], in0=bb[:], in1=R[:], op=ALU.subtract), [bb, R], [t1])
                        if not last:
                            c.op("dve", lambda: V.tensor_tensor(out=R[:], in0=R[:], in1=cb[:], op=ALU.add), [R, cb], [R])
                        c.op("act", lambda: A.activation(out=at[:], in_=t1[:], func=AF.Exp), [t1], [at])
                        if diag:
                            c.op("pool", lambda: G.tensor_tensor(out=at[:], in0=at[:], in1=mk, op=ALU.mult), [at, cs_], [at])

                    def s3b():
                        c.op("pe", lambda: PE.matmul(ob[:], lhsT=vs[:, kb, h * 64:h * 64 + 128], rhs=at[:], start=(i == 0), stop=last),
                             [vs, at], [ob])
                        if last:
                            st = ost.next()
                            c.op("act", lambda: A.copy(out=st[:], in_=ob[0:64, :]), [ob], [st])
                            otix.setdefault(h, []).append(c.dma("sp", oT[h, :, qsl], st[:], reads=[st], track=st))
                            if qg == S // 512 - 1:
                                c._wait("pool", otix[h]); io["ag_o"](h)
                    return (s3a, s1, s2, s3b)
                items.append(mk_item())
    run_pipeline(items, (2, 0, 1, 2))
    c.end_phase()


def emit_softmax_attn(c, io, kind, lam_init=0.0):
    nc = c.nc
    V = nc.vector; G = nc.gpsimd; A = nc.scalar; PE = nc.tensor
    dil = kind == "dil"
    vcols = 260 if dil else 256
    qs, ks, vs = attn_common(c, io, vcols, dil)
    NM = 20 if dil else 4
    mk = c.sb((128, NM, 512), BF16, "mk"); c.dma("sp", mk[:], io["dil_masks" if dil else "diff_masks"], writes=[mk], track=mk)
    onesf = c.sb((128, 128), F32, "onesf"); c.dma("sp", onesf[:], io["ones_f"], writes=[onesf], track=onesf)
    zr = Rot([c.ps(name=f"z{i}") for i in range(2)])
    atr = Rot([c.sb((128, 512), BF16, f"at{i}") for i in range(4)])
    if dil:
        oT = io["o_loc"]
        orr = Rot([c.ps(name=f"o{i}") for i in range(2)])
        bcr = Rot([c.ps(name=f"bc{i}") for i in range(2)])
        rl = c.sb((128, 512), F32, "rl"); bcs = c.sb((64, 512), F32, "bcs")
        ost = Rot([c.sb((64, 512), BF16, f"ost{i}") for i in range(2)])
        items = []
        otix = {}
        for h in range(4):
            qh = head_ap(qs, h); kh = head_ap(ks, h)
            for qg in range(S // 512):
                qsl = slice(qg * 512, (qg + 1) * 512)
                ob = orr.next()
                kbs = list(range(max(0, 4 * qg - 16), 4 * qg + 4))
                for i, kb in enumerate(kbs):
                    def mk_item(h=h, qh=qh, kh=kh, qg=qg, qsl=qsl, ob=ob, kbs=kbs, i=i, kb=kb):
                        zb = zr.next(); at = atr.next()
                        mi = (512 * qg - 128 * kb + 384) // 128
                        last = i == len(kbs) - 1

                        def s1():
                            c.op("pe", lambda: PE.matmul(zb[:], lhsT=kh(slice(kb * 128, (kb + 1) * 128)), rhs=qh(qsl), start=True, stop=True), [ks, qs], [zb])
                            c.op("act", lambda: A.activation(out=at[:], in_=zb[:], func=AF.Exp), [zb], [at])
                            c.op("dve", lambda: V.tensor_tensor(out=at[:], in0=at[:], in1=mk[:, mi, :], op=ALU.mult), [at, mk], [at])

                        def s2():
                            c.op("pe", lambda: PE.matmul(ob[0:65, :], lhsT=vs[:, kb, h * 65:(h + 1) * 65], rhs=at[:], start=(i == 0), stop=last),
                                 [vs, at], [ob])
                            if last:
                                bc = bcr.next(); st = ost.next()
                                c.op("dve", lambda: V.reciprocal(out=rl[64:65, :], in_=ob[64:65, :]), [ob], [rl])
                                c.op("pe", lambda: PE.matmul(bc[0:64, :], lhsT=onesf[64:65, 0:64], rhs=rl[64:65, :], start=True, stop=True), [onesf, rl], [bc])
                                c.op("act", lambda: A.copy(out=bcs[:], in_=bc[0:64, :]), [bc], [bcs])
                                c.op("dve", lambda: V.tensor_tensor(out=st[:], in0=ob[0:64, :], in1=bcs[:], op=ALU.mult), [ob, bcs], [st])
                                otix.setdefault(h, []).append(c.dma("sp", oT[h, :, qsl], st[:], reads=[st], track=st))
                                if qg == S // 512 - 1:
                                    c._wait("pool", otix[h]); io["ag_o"](h)
                        return (s1, s2)
                    items.append(mk_item())
        run_pipeline(items, (0, 2))
        c.end_phase()
        return
    oT = io["o_loc"].rearrange("(d a) p s -> d (a p) s", a=2)
    onesb = c.sb((128, 128), BF16, "onesb"); c.dma("sp", onesb[:], io["ones_bf"], writes=[onesb], track=onesb)
    lam = c.sb((1, 4, 64), F32, "lam_sb"); c.dma("sp", lam[:], io["lam"], writes=[lam], track=lam)
    sub = c.sb((128, 1), F32, "sub_sb"); c.dma("sp", sub[:], io["subln"], writes=[sub], track=sub)
    prod = c.sb((1, 2, 64), F32, "prod"); dots = c.sb((1, 2), F32, "dots"); ee = c.sb((1, 2), F32, "ee")
    dl = c.sb((1, 2), F32, "dl"); nl = c.sb((1, 2), F32, "nl"); nlam = c.sb((128, 2), F32, "nlam")
    epsb = c.sb((128, 1), F32, "epsb")
    c.op("pool", lambda: G.memset(epsb[:], 1e-5), [], [epsb])
    for m in range(2):
        c.op("dve", lambda: V.tensor_tensor(out=prod[:, m, :], in0=lam[:, 2 * m, :], in1=lam[:, 2 * m + 1, :], op=ALU.mult), [lam], [prod])
    c.op("pool", lambda: G.memset(dots[:], 0.0), [], [dots])
    c.op("dve", lambda: V.tensor_reduce(out=dots[:], in_=prod[:], op=ALU.add, axis=mybir.AxisListType.X), [prod, dots], [dots])
    c.op("act", lambda: A.activation(out=ee[:], in_=dots[:], func=AF.Exp), [dots], [ee])
    for j in range(2):
        c.op("pool", lambda: G.tensor_tensor(out=dl[:, j:j + 1], in0=ee[:, 1:2], in1=ee[:, 0:1], op=ALU.subtract), [ee], [dl])
    c.op("dve", lambda: V.tensor_scalar(out=nl[:], in0=dl[:], scalar1=-lam_init, scalar2=None, op0=ALU.add), [dl], [nl])
    zb = zr.next()
    c.op("pe", lambda: PE.matmul(zb[:, 0:2], lhsT=onesf[0:1, :], rhs=nl[0:1, :], start=True, stop=True), [onesf, nl], [zb])
    c.op("act", lambda: A.copy(out=nlam[:], in_=zb[:, 0:2]), [zb], [nlam])
    c.op("pool", lambda: G.tensor_scalar(out=sub[:], in0=sub[:], scalar1=(1.0 - lam_init), scalar2=None, op0=ALU.mult), [sub], [sub])
    ob = [c.ps(name=f"o{i}") for i in range(2)]
    lb = [c.ps(name=f"l{i}") for i in range(2)]
    bc = lb[0]; ssb = c.ps(name="ss")
    rl = c.sb((1, 2, 512), F32, "rl"); rbs = [c.sb((128, 512), F32, f"rbs{i}") for i in range(2)]
    t1 = c.sb((128, 512), F32, "t1"); t2 = c.sb((128, 512), F32, "t2"); sq = c.sb((128, 512), BF16, "sq")
    rstd = c.sb((128, 512), F32, "rstd")
    ost = Rot([c.sb((128, 512), BF16, f"ost{i}") for i in range(2)])
    items = []
    for dh in range(2):
        for qg in range(S // 512):
            qsl = slice(qg * 512, (qg + 1) * 512)
            nkb = 4 * qg + 4
            for kb in range(nkb):
                for m in range(2):
                    def mk_item(dh=dh, qg=qg, qsl=qsl, nkb=nkb, kb=kb, m=m):
                        h = 2 * dh + m
                        qh = head_ap(qs, h); kh = head_ap(ks, h)
                        zb = zr.next(); at = atr.next()

                        def s1():
                            c.op("pe", lambda: PE.matmul(zb[:], lhsT=kh(slice(kb * 128, (kb + 1) * 128)), rhs=qh(qsl), start=True, stop=True), [ks, qs], [zb])
                            c.op("act", lambda: A.activation(out=at[:], in_=zb[:], func=AF.Exp), [zb], [at])
                            if kb >= 4 * qg:
                                c.op("dve", lambda: V.tensor_tensor(out=at[:], in0=at[:], in1=mk[:, kb - 4 * qg, :], op=ALU.mult), [at, mk], [at])

                        def s2():
                            c.op("pe", lambda: PE.matmul(ob[m][:], lhsT=vs[:, kb, dh * 128:(dh + 1) * 128], rhs=at[:], start=(kb == 0), stop=(kb == nkb - 1)),
                                 [vs, at], [ob[m]])
                            c.op("pe", lambda: PE.matmul(lb[m][:], lhsT=onesb[:], rhs=at[:], start=(kb == 0), stop=(kb == nkb - 1)),
                                 [onesb, at], [lb[m]])
                            if kb == nkb - 1 and m == 1:
                                epilogue(dh, qsl)
                        return (s1, s2)
                    items.append(mk_item())

    def epilogue(dh, qsl):
        for m in range(2):
            c.op("dve", lambda: V.reciprocal(out=rbs[m][:], in_=lb[m][:]), [lb[m]], [rbs[m]])
        c.op("dve", lambda: V.tensor_tensor(out=t1[:], in0=ob[0][:], in1=rbs[0][:], op=ALU.mult), [ob[0], rbs[0]], [t1])
        c.op("dve", lambda: V.tensor_tensor(out=t2[:], in0=ob[1][:], in1=rbs[1][:], op=ALU.mult), [ob[1], rbs[1]], [t2])
        c.op("pool", lambda: G.tensor_scalar(out=t2[:], in0=t2[:], scalar1=nlam[:, 0:1], scalar2=None, op0=ALU.mult), [t2, nlam], [t2])
        c.op("dve", lambda: V.tensor_tensor(out=t1[:], in0=t1[:], in1=t2[:], op=ALU.add), [t1, t2], [t1])
        c.op("act", lambda: A.activation(out=sq[:], in_=t1[:], func=AF.Square), [t1], [sq])
        c.op("pe", lambda: PE.matmul(ssb[:], lhsT=onesb[:], rhs=sq[:], start=True, stop=True), [onesb, sq], [ssb])
        c.op("act", lambda: A.activation(out=rstd[:], in_=ssb[:], func=AF.Sqrt, scale=1.0 / 128, bias=epsb[:, 0:1]), [ssb, epsb], [rstd])
        c.op("pool", lambda: G.tensor_copy(out=rstd2[:], in_=rstd[:]), [rstd], [rstd2])
        c.op("dve", lambda: V.reciprocal(out=rstd2[:], in_=rstd2[:]), [rstd2], [rstd2])
        st = ost.next()
        c.op("dve", lambda: V.scalar_tensor_tensor(out=st[:], in0=t1[:], scalar=sub[:, 0:1], in1=rstd2[:], op0=ALU.mult, op1=ALU.mult), [t1, sub, rstd2], [st])
        otix.setdefault(dh, []).append(c.dma("sp", oT[dh, :, qsl], st[:], reads=[st], track=st))
        if len(otix[dh]) == S // 512:
            c._wait("pool", otix[dh]); io["ag_o"](2 * dh); io["ag_o"](2 * dh + 1)

    otix = {}
    rstd2 = c.sb((128, 512), F32, "rstd2")
    bcb = [lb[0], lb[1]]
    run_pipeline(items, (0, 2))
    c.end_phase()


PRE_KINDS = ["rglru", "rope", "plain", "rope"]
LAM_INIT3 = 0.8 - 0.6 * math.exp(-0.3 * 3)
U32 = mybir.dt.uint32


def build_fused(stop=999):
    c = Ctx()
    step = [0]

    def done():
        step[0] += 1
        return step[0] >= stop

    EI = lambda n, sh, dt: c.dram(n, sh, dt, "ExternalInput")
    g = {}
    g["xT"] = EI("xT", (D, TOK), F32)
    g["xT_out"] = c.dram("xT_out", (D, TOK), F32, "ExternalOutput")
    g["ones_bf"] = EI("ones_bf", (128, 128), BF16); g["ones_f"] = EI("ones_f", (128, 128), F32)
    g["ident_bf"] = EI("ident_bf", (128, 128), BF16)
    g["pos"] = EI("pos", (1, TOK), I32); g["invf"] = EI("invf", (128, 2), F32)
    g["idx_o"] = EI("idx_o", (128, 16), U32); g["idx_a"] = EI("idx_a", (128, 24), U32); g["idx_r"] = EI("idx_r", (128, 16), U32)
    g["cw"] = EI("cw", (128, 2, 8), F32); g["gw"] = EI("gw", (2, 2, 128, 128), F32)
    g["dil_masks"] = EI("dil_masks", (128, 20, 512), BF16); g["diff_masks"] = EI("diff_masks", (128, 4, 512), BF16)
    g["sb_cst"] = EI("sb_cst", (128, 18, 128), BF16)
    g["lam"] = EI("lam", (1, 4, 64), F32); g["subln"] = EI("subln", (128, 1), F32)
    L = []
    for l in range(DEPTH):
        npre = {"rglru": 16, "rope": 40, "plain": 24}[PRE_KINDS[l]]
        L.append(dict(w_mo=EI(f"w_mo{l}", (8, 128, 8, 128), F32), w_f1=EI(f"w_f1{l}", (44, 128, 8, 128), F32),
                      w_f2=EI(f"w_f2{l}", (8, 128, 22, 128), F32), w_pg=EI(f"w_pg{l}", (8, 128, 8, 128), F32),
                      w_pp=EI(f"w_pp{l}", (8, 128, 2, 128), F32), p_in=EI(f"pT{l}", (PLE, TOK), F32),
                      gains_post=EI(f"gains_post{l}", (128, 5, 8), F32), gain_pre=EI(f"gain_pre{l}", (128, 8), F32),
                      w_pre=EI(f"w_pre{l}", (npre, 128, 8, 128), F32)))
    x_scr = c.scratch("x_scr", (D, TOK), F32)
    pr_loc_r = c.scratch("pr_loc_r", (16, 128, TOK), F32); pr_all_r = c.scratch("pr_all_r", (16, 4 * 128, TOK), F32)
    pr_loc_a = c.scratch("pr_loc_a", (12, 256, TOK), BF16); pr_all_a = c.scratch("pr_all_a", (12, 4 * 256, TOK), BF16)
    o_loc = c.scratch("o_loc", (4, 64, S), BF16); o_all = c.scratch("o_all", (4, 4 * 64, S), BF16)
    rope_scr = c.scratch("rope_scr", (2, 128, TOK), F32)
    pr_loc_h = c.scratch("pr_loc_h", (4, 256, TOK), BF16); h_all = c.scratch("h_all", (4, 4 * 256, TOK), BF16)
    h_all_buf = Buf(h_all)
    g["w_rg"] = EI("w_rg", (4, 128, 8, 128), F32)
    g["w_sbq"] = EI("w_sbq", (2, 128, 8, 128), F32); g["w_sbk"] = EI("w_sbk", (2, 128, 8, 128), F32)
    g["w_sbv"] = EI("w_sbv", (128, 8, 256), F32)
    pr_all_r_buf = Buf(pr_all_r); pr_all_a_buf = Buf(pr_all_a); o_all_buf = Buf(o_all)
    o_all2d = o_all.rearrange("a r (c t) -> (a r c) t", t=SG)
    for i in range(DEPTH + 1):
        post = i > 0; pre = PRE_KINDS[i] if i < DEPTH else None
        if pre in ("rglru", "plain"):
            pre = "hnorm"
        io = dict(x_in=(g["xT"] if i == 0 else x_scr), x_out=(g["xT_out"] if i == DEPTH else x_scr), ones_bf=g["ones_bf"],
                  final=(i == DEPTH), idx_o=g["idx_o"], o_all2d=o_all2d, o_all_buf=o_all_buf,
                  rope_scr=rope_scr, rope_build=(i == 0))
        if post:
            io.update({k: L[i - 1][k] for k in ("w_mo", "w_f1", "w_f2", "w_pg", "w_pp", "p_in", "gains_post")})
        if pre:
            io.update(gain_pre=L[i]["gain_pre"], w_pre=L[i]["w_pre"], pos=g["pos"], invf=g["invf"],
                      pr_loc=(pr_loc_r if pre == "rglru" else pr_loc_a))
            if pre == "hnorm":
                io["pr_loc"] = pr_loc_h
                io["ag_pr"] = lambda k: c.all_gather(pr_loc_h[k], h_all[k], h_all_buf)
            else:
                io["ag_pr"] = lambda k: c.all_gather(pr_loc_a[k], pr_all_a[k], pr_all_a_buf)
        emit_dense(c, post, pre, io)
        if not pre or done():
            break
        if pre == "hnorm":
            mio = dict(h_all=h_all, h_all_buf=h_all_buf)
        else:
            mio = dict(pr_all2d=pr_all_a.rearrange("a r t -> (a r) t"), pr_all_buf=pr_all_a_buf)
        mio["ag_o"] = lambda k: c.all_gather(o_loc[k], o_all[k], o_all_buf)
        if done():
            break
        mio.update(g); mio["o_loc"] = o_loc
        if i == 0:
            emit_rglru(c, mio)
        elif i == 1:
            emit_softmax_attn(c, mio, "dil")
        elif i == 2:
            emit_sb(c, mio)
        else:
            emit_softmax_attn(c, mio, "diff", LAM_INIT3)
        if done():
            break
        if done():
            break
    c.barrier()
    c.finish(c.out_tickets)
    return c.nc


def _tok(cc):
    b, j = divmod(cc, 4)
    return b, slice(j * TOK, (j + 1) * TOK)


def _rope_cols():
    cols = []
    for jj in range(16):
        base = jj * 128
        e = np.arange(128)
        cols.append(np.arange(base, base + 128))
        cols.append(base + (e // 64) * 64 + ((e % 64) + 32) % 64)
    cols.append(np.arange(2048, 3072))
    return np.concatenate(cols)


def _mult(dist):
    m = np.zeros(dist.shape, np.float32)
    for dd in (1, 4, 16):
        m += ((dist >= 0) & (dist % dd == 0) & (dist <= 128 * dd)).astype(np.float32)
    return m


def kernel(**inp):
    inp = {k: np.asarray(v) for k, v in inp.items()}
    x = inp["x"]; p = inp["p"]; pos = inp["positions"].astype(np.int32)
    pp = np.arange(128)[:, None]; ff = np.arange(512)[None, :]
    sh = {"ones_bf": np.ones((128, 128), NPBF), "ones_f": np.ones((128, 128), np.float32),
          "ident_bf": np.eye(128, dtype=np.float32).astype(NPBF)}
    invf = np.zeros((128, 2), np.float32)
    e = np.arange(128) % 64
    invf[:, 0] = (np.float32(10000.0) ** (-(np.arange(0, 64, 2, dtype=np.float32)) / np.float32(64)))[e % 32]
    invf[:, 1] = np.where(e < 32, -1.0, 1.0)
    sh["invf"] = invf
    f1cols = np.concatenate([np.concatenate([np.arange(cc * 128, (cc + 1) * 128), FF + np.arange(cc * 128, (cc + 1) * 128)]) for cc in range(22)])
    rcols = _rope_cols()
    mix_out_w = [inp["a_w_out"][0], inp["b_w_out"][0], inp["c_w_out"][0], inp["d_w_out"][0]]
    pre_w = [w_chunks(inp["a_w_in"][0]), w_chunks(inp["b_w_qkv"][0], rcols), w_chunks(inp["c_w_qkv"][0]), w_chunks(inp["d_w_qkv"][0], rcols)]
    for l in range(DEPTH):
        sh[f"w_mo{l}"] = w_chunks(mix_out_w[l]); sh[f"w_f1{l}"] = w_chunks(inp["w_ffn_in"][l], f1cols)
        sh[f"w_f2{l}"] = w_chunks(inp["w_ffn_out"][l]); sh[f"w_pg{l}"] = w_chunks(inp["w_ple_gate"][l])
        sh[f"w_pp{l}"] = w_chunks(inp["w_ple_proj"][l])
        sh[f"gains_post{l}"] = np.ascontiguousarray(np.stack([col_vec(inp[n][l]) for n in
                                                             ("ln_mix_post", "ln_ffn_pre", "ln_ffn_post", "ln_ple", "b_ple_gate")], axis=1))
        sh[f"gain_pre{l}"] = col_vec(inp["ln_mix_pre"][l]); sh[f"w_pre{l}"] = pre_w[l]
    sh["dil_masks"] = np.ascontiguousarray(np.stack([_mult((128 * mi - 384) + ff - pp) for mi in range(20)], axis=1)).astype(NPBF)
    sh["diff_masks"] = np.ascontiguousarray(np.stack([((ff - pp - 128 * j) >= 0).astype(np.float32) for j in range(4)], axis=1)).astype(NPBF)
    cst = np.zeros((128, 18, 128), np.float32)
    cst[:, 0, :] = -1.0 * (pp >= np.arange(128)[None, :]); cst[:, 1, :] = 1.0
    for o in range(4):
        cst[:, 2 + 4 * o:6 + 4 * o, :] = ((128 * o + pp) < ff).astype(np.float32).reshape(128, 4, 128)
    sh["sb_cst"] = cst.astype(NPBF)
    sh["lam"] = np.ascontiguousarray(np.stack([inp[n][0] for n in ("d_lambda_q1", "d_lambda_k1", "d_lambda_q2", "d_lambda_k2")])[None])
    sh["subln"] = np.ascontiguousarray(inp["d_subln"][0].reshape(128, 1))
    pa = np.arange(128)
    maps = []
    for cc in range(NCORE):
        b, ts = _tok(cc)
        r = cc % 4
        m = dict(sh)
        m["xT"] = fm(x[b, ts]); m["pos"] = np.ascontiguousarray(pos[b:b + 1, ts])
        for l in range(DEPTH):
            m[f"pT{l}"] = fm(p[l, b, ts])
        cw = np.zeros((128, 2, 8), np.float32); gw = np.zeros((2, 2, 128, 128), np.float32)
        for ct in range(2):
            cs = slice(r * 256 + ct * 128, r * 256 + (ct + 1) * 128)
            cw[:, ct, 0:4] = inp["a_conv_w"][0][:, cs].T
            cw[:, ct, 4] = inp["a_conv_b"][0][cs]; cw[:, ct, 5] = inp["a_gate_r_b"][0][cs]
            cw[:, ct, 6] = inp["a_gate_i_b"][0][cs]; cw[:, ct, 7] = inp["a_lambda"][0][cs]
            for gi, nm in enumerate(("a_gate_r_w", "a_gate_i_w")):
                for hh in range(2):
                    n = r * 4 + ct * 2 + hh
                    gw[ct, gi, hh * 64:(hh + 1) * 64, hh * 64:(hh + 1) * 64] = inp[nm][0][n]
        m["cw"] = cw; m["gw"] = gw
        m["w_rg"] = np.ascontiguousarray(pre_w[0][[2 * r, 2 * r + 1, 8 + 2 * r, 8 + 2 * r + 1]])
        Wc = inp["c_w_qkv"][0]
        e64 = np.arange(64)
        slot_cols = lambda base, s_: np.concatenate([base + (4 * r + s_) * 64 + e64, base + (4 * r + 2 + s_) * 64 + e64])
        m["w_sbq"] = np.ascontiguousarray(np.stack([w_chunks(Wc, slot_cols(0, s_))[0] for s_ in range(2)]))
        m["w_sbk"] = np.ascontiguousarray(np.stack([w_chunks(Wc, slot_cols(1024, s_))[0] for s_ in range(2)]))
        m["w_sbv"] = np.ascontiguousarray(Wc[:, 2048 + r * 256:2048 + (r + 1) * 256].reshape(8, 128, 256).transpose(1, 0, 2))
        io_ = np.zeros((128, 16), np.uint32)
        for sg in range(2):
            for k in range(8):
                rl = (k % 2) * 128 + pa
                io_[:, sg * 8 + k] = ((((rl // 64) * 4 + k // 2) * 64 + rl % 64) * 4 + r) * 2 + sg
        m["idx_o"] = io_
        ia = np.zeros((128, 24), np.uint32)
        for s_ in range(2):
            for j in range(4):
                hl = 2 * (pa // 64) + s_
                ia[:, s_ * 4 + j] = ((r * 3 + 0) * 4 + j) * 256 + hl * 64 + pa % 64
                ia[:, 8 + s_ * 4 + j] = ((r * 3 + 1) * 4 + j) * 256 + hl * 64 + pa % 64
        for vc in range(2):
            for j in range(4):
                ia[:, 16 + vc * 4 + j] = ((r * 3 + 2) * 4 + j) * 256 + vc * 128 + pa
        m["idx_a"] = ia
        ir = np.zeros((128, 16), np.uint32)
        for tc_ in range(4):
            for j in range(4):
                ir[:, tc_ * 4 + j] = ((r * 4 + tc_) * 4 + j) * 128 + pa
        m["idx_r"] = ir
        maps.append(m)
    nc = build_fused(STOP)
    res = run_bass_kernel_spmd(nc, maps, core_ids=list(range(NCORE))).results
    out = np.zeros((B, S, D), np.float32)
    for cc in range(NCORE):
        b, ts = _tok(cc)
        out[b, ts] = res[cc]["xT_out"].T
    return out


STOP = 999
```
